# Optimizing a Trainium2 kernel written in Bass

```python
import jax, jax.numpy as jnp
from jax import lax
import numpy as np

D_MODEL = 2048
BATCH = 4
SEQ = 2048
DEPTH = 1
DEC_BATCH = 128
DEC_SEQ = 8
PAST_LEN = 16384
PAGE_SIZE = 128

POOL_WIDTH = D_MODEL // 2
POOL_WINDOWS = (2, 4, 8, 16)
POOL_GROUPS = len(POOL_WINDOWS)
POOL_GROUP_DIM = POOL_WIDTH // POOL_GROUPS
POOL_BUF = max(POOL_WINDOWS) - 1
LRU_WIDTH = D_MODEL
LRU_BLOCKS = 8
LRU_BLOCK_DIM = LRU_WIDTH // LRU_BLOCKS
LRU_CONV = 4
LRU_C = 8.0
D_FF = 3 * D_MODEL
FFN_CONV = 3
IN_WIDTH = POOL_WIDTH + LRU_WIDTH + 2 * D_MODEL
N_ADA = 6
EPS = 1e-6

kernel_name = 'hybrid_pool_rglru_convffn_step'


def rmsnorm(x, g):
    xf = x.astype(jnp.float32)
    y = xf * lax.rsqrt(jnp.mean(xf * xf, axis=-1, keepdims=True) + EPS)
    return (y * g.astype(jnp.float32)).astype(x.dtype)


def causal_dwconv(buf, u, w, b):
    K = w.shape[0]
    T = u.shape[1]
    ext = jnp.concatenate([buf.astype(u.dtype), u], axis=1)
    out = ext[:, 0:T] * w[0]
    for k in range(1, K):
        out = out + ext[:, k:k + T] * w[k]
    return out + b, ext[:, -(K - 1):]


def pool_mix(buf, u, start, w_grp, scale):
    B, T, P = u.shape
    ext = jnp.concatenate([buf.astype(u.dtype), u], axis=1)
    cs = jnp.cumsum(ext.astype(jnp.float32), axis=1)
    cs = jnp.pad(cs, ((0, 0), (1, 0), (0, 0)))
    hi = cs[:, POOL_BUF + 1:]
    pos = start + jnp.arange(T, dtype=jnp.int32)
    means = []
    for k, w in enumerate(POOL_WINDOWS):
        sl = slice(k * POOL_GROUP_DIM, (k + 1) * POOL_GROUP_DIM)
        lo = cs[:, POOL_BUF + 1 - w:POOL_BUF + 1 - w + T, sl]
        cnt = jnp.minimum(w, pos + 1).astype(jnp.float32)[None, :, None]
        means.append((hi[..., sl] - lo) / cnt)
    mean = jnp.concatenate(means, axis=-1).astype(u.dtype)
    d = (mean - u).reshape(B, T, POOL_GROUPS, POOL_GROUP_DIM)
    y = jnp.einsum('btgc,gcd->btgd', d, w_grp).reshape(B, T, P) * scale
    return y, ext[:, -POOL_BUF:]


def rglru(h0, xc, w_rg, b_rg, w_ig, b_ig, lam):
    B, T, R = xc.shape
    xb = xc.reshape(B, T, LRU_BLOCKS, LRU_BLOCK_DIM)
    r = jax.nn.sigmoid((jnp.einsum('btnc,ncd->btnd', xb, w_rg).reshape(B, T, R) + b_rg).astype(jnp.float32))
    i = jax.nn.sigmoid((jnp.einsum('btnc,ncd->btnd', xb, w_ig).reshape(B, T, R) + b_ig).astype(jnp.float32))
    log_a = -LRU_C * r * jax.nn.softplus(-lam.astype(jnp.float32))
    a = jnp.exp(log_a)
    u = jnp.sqrt(-jnp.expm1(2.0 * log_a)) * (i * xc.astype(jnp.float32))

    def step(h, inp):
        a_t, u_t = inp
        h = a_t * h + u_t
        return h, h

    hT, hs = lax.scan(step, h0.astype(jnp.float32), (jnp.swapaxes(a, 0, 1), jnp.swapaxes(u, 0, 1)))
    return jnp.swapaxes(hs, 0, 1).astype(xc.dtype), hT


def _layer(x, c, pool_buf, lru_buf, lru_h, ffn_buf, start,
           w_ada, b_ada, g_pre1, g_post1, g_pre2, g_post2, w_in, w_pool_grp, pool_scale,
           w_lru_conv, b_lru_conv, w_rg, b_rg, w_ig, b_ig, lru_lambda,
           w_pool_up, w_lru_up, w_out, w_ffn_up, w_ffn_conv, b_ffn_conv, w_ffn_down):
    ada = (jax.nn.silu(c) @ w_ada + b_ada)[:, None, :]
    shift1, scale1, gate1, shift2, scale2, gate2 = jnp.split(ada, N_ADA, axis=-1)

    h = rmsnorm(x, g_pre1) * (1.0 + scale1) + shift1
    z = h @ w_in
    u_pool, u_lru, g_pool, g_lru = jnp.split(
        z, [POOL_WIDTH, POOL_WIDTH + LRU_WIDTH, POOL_WIDTH + LRU_WIDTH + D_MODEL], axis=-1)
    y_pool, new_pool = pool_mix(pool_buf, u_pool, start, w_pool_grp, pool_scale)
    xc, new_lru_buf = causal_dwconv(lru_buf, u_lru, w_lru_conv, b_lru_conv)
    y_lru, new_h = rglru(lru_h, xc, w_rg, b_rg, w_ig, b_ig, lru_lambda)
    merged = jax.nn.sigmoid(g_pool) * (y_pool @ w_pool_up) + jax.nn.sigmoid(g_lru) * (y_lru @ w_lru_up)
    x = x + gate1 * rmsnorm(merged @ w_out, g_post1)

    h = rmsnorm(x, g_pre2) * (1.0 + scale2) + shift2
    up = h @ w_ffn_up
    upc, new_ffn_buf = causal_dwconv(ffn_buf, up, w_ffn_conv, b_ffn_conv)
    gt, val = jnp.split(upc, 2, axis=-1)
    f = jax.nn.gelu(gt, approximate=True) * val
    x = x + gate2 * rmsnorm(f @ w_ffn_down, g_post2)
    return x, new_pool, new_lru_buf, new_h, new_ffn_buf


def setup_inputs(seed: int = 0) -> dict:
    key = jax.random.key(seed)
    ks = iter(jax.random.split(key, 40))
    f32 = jnp.float32

    def nrm(shape, scale):
        return jax.random.normal(next(ks), shape, f32) * scale

    def gain(shape):
        return 1.0 + 0.05 * jax.random.normal(next(ks), shape, f32)

    a0 = jax.random.uniform(next(ks), (DEPTH, LRU_WIDTH), f32, 0.9, 0.999)
    p = a0 ** (1.0 / LRU_C)
    lru_lambda = jnp.log(p) - jnp.log1p(-p)

    return {
        'x_prompt': nrm((BATCH, SEQ, D_MODEL), 1.0),
        'x_sample': nrm((DEC_BATCH, DEC_SEQ, D_MODEL), 1.0),
        'c_prompt': nrm((BATCH, D_MODEL), 1.0),
        'c_sample': nrm((DEC_BATCH, D_MODEL), 1.0),
        'state_pool': nrm((DEPTH, DEC_BATCH, POOL_BUF, POOL_WIDTH), 1.0),
        'state_lru_conv': nrm((DEPTH, DEC_BATCH, LRU_CONV - 1, LRU_WIDTH), 1.0),
        'state_lru_h': nrm((DEPTH, DEC_BATCH, LRU_WIDTH), 0.5),
        'state_ffn_conv': nrm((DEPTH, DEC_BATCH, FFN_CONV - 1, 2 * D_FF), 1.0),
        'w_ada': nrm((DEPTH, D_MODEL, N_ADA * D_MODEL), 0.5 * D_MODEL ** -0.5),
        'b_ada': nrm((DEPTH, N_ADA * D_MODEL), 0.02),
        'g_pre1': gain((DEPTH, D_MODEL)),
        'g_post1': gain((DEPTH, D_MODEL)),
        'g_pre2': gain((DEPTH, D_MODEL)),
        'g_post2': gain((DEPTH, D_MODEL)),
        'w_in': nrm((DEPTH, D_MODEL, IN_WIDTH), D_MODEL ** -0.5),
        'w_pool_grp': nrm((DEPTH, POOL_GROUPS, POOL_GROUP_DIM, POOL_GROUP_DIM), POOL_GROUP_DIM ** -0.5),
        'pool_scale': gain((DEPTH, POOL_WIDTH)),
        'w_lru_conv': nrm((DEPTH, LRU_CONV, LRU_WIDTH), LRU_CONV ** -0.5),
        'b_lru_conv': nrm((DEPTH, LRU_WIDTH), 0.02),
        'w_rg': nrm((DEPTH, LRU_BLOCKS, LRU_BLOCK_DIM, LRU_BLOCK_DIM), LRU_BLOCK_DIM ** -0.5),
        'b_rg': nrm((DEPTH, LRU_WIDTH), 0.02),
        'w_ig': nrm((DEPTH, LRU_BLOCKS, LRU_BLOCK_DIM, LRU_BLOCK_DIM), LRU_BLOCK_DIM ** -0.5),
        'b_ig': nrm((DEPTH, LRU_WIDTH), 0.02),
        'lru_lambda': lru_lambda,
        'w_pool_up': nrm((DEPTH, POOL_WIDTH, D_MODEL), POOL_WIDTH ** -0.5),
        'w_lru_up': nrm((DEPTH, LRU_WIDTH, D_MODEL), LRU_WIDTH ** -0.5),
        'w_out': nrm((DEPTH, D_MODEL, D_MODEL), D_MODEL ** -0.5),
        'w_ffn_up': nrm((DEPTH, D_MODEL, 2 * D_FF), D_MODEL ** -0.5),
        'w_ffn_conv': nrm((DEPTH, FFN_CONV, 2 * D_FF), FFN_CONV ** -0.5),
        'b_ffn_conv': nrm((DEPTH, 2 * D_FF), 0.02),
        'w_ffn_down': nrm((DEPTH, D_FF, D_MODEL), D_FF ** -0.5),
    }


def reference(x_prompt, x_sample, c_prompt, c_sample, state_pool, state_lru_conv, state_lru_h, state_ffn_conv,
              w_ada, b_ada, g_pre1, g_post1, g_pre2, g_post2, w_in, w_pool_grp, pool_scale,
              w_lru_conv, b_lru_conv, w_rg, b_rg, w_ig, b_ig, lru_lambda,
              w_pool_up, w_lru_up, w_out, w_ffn_up, w_ffn_conv, b_ffn_conv, w_ffn_down):
    weights = (w_ada, b_ada, g_pre1, g_post1, g_pre2, g_post2, w_in, w_pool_grp, pool_scale,
               w_lru_conv, b_lru_conv, w_rg, b_rg, w_ig, b_ig, lru_lambda,
               w_pool_up, w_lru_up, w_out, w_ffn_up, w_ffn_conv, b_ffn_conv, w_ffn_down)
    dt = x_prompt.dtype
    yp, ys = x_prompt, x_sample
    pp, plc, plh, pfc = [], [], [], []
    sp, slc, slh, sfc = [], [], [], []
    for l in range(DEPTH):
        params = [w[l] for w in weights]
        yp, a1, a2, a3, a4 = _layer(
            yp, c_prompt,
            jnp.zeros((BATCH, POOL_BUF, POOL_WIDTH), dt),
            jnp.zeros((BATCH, LRU_CONV - 1, LRU_WIDTH), dt),
            jnp.zeros((BATCH, LRU_WIDTH), jnp.float32),
            jnp.zeros((BATCH, FFN_CONV - 1, 2 * D_FF), dt),
            0, *params)
        pp.append(a1); plc.append(a2); plh.append(a3); pfc.append(a4)
        ys, b1, b2, b3, b4 = _layer(
            ys, c_sample, state_pool[l], state_lru_conv[l], state_lru_h[l], state_ffn_conv[l],
            PAST_LEN, *params)
        sp.append(b1); slc.append(b2); slh.append(b3); sfc.append(b4)
    return (yp, ys,
            jnp.stack(pp), jnp.stack(plc), jnp.stack(plh), jnp.stack(pfc),
            jnp.stack(sp), jnp.stack(slc), jnp.stack(slh), jnp.stack(sfc))
```

```python
import contextlib
import numpy as np
import concourse.bass as bass
import concourse.mybir as mybir
from concourse.bass_utils import run_bass_kernel_spmd

F32 = mybir.dt.float32
BF16 = mybir.dt.bfloat16
AF = mybir.ActivationFunctionType
ALU = mybir.AluOpType

D = 2048
DK = 16
PW = 1024
DFF = 6144
EPS = 1e-6
NCORES = 8
HALO = 32
NPRE = 992
ENGS = ("pe", "act", "dve", "pool", "sp")

CV_GPRE1, CV_GPOST1, CV_GPRE2, CV_GPOST2 = 0, 16, 32, 48
CV_PSCALE = 64
CV_WLC = 72
CV_BLC = 136
CV_BRG = 152
CV_BIG = 168
CV_LAM = 184
CV_WFC = 200
CV_BFC = 488
CV_BADA = 584
NV = 680


class Prog:
    def __init__(self, nc, stack):
        self.nc = nc
        self.stack = stack
        self.q = {e: [] for e in ENGS}
        self.sem = {e: stack.enter_context(nc.semaphore("prog_" + e)) for e in ENGS}
        self.cnt = {e: 0 for e in ENGS}
        self.waited = {e: {} for e in ENGS}

    def new_sem(self, name):
        return self.stack.enter_context(self.nc.semaphore(name))

    def _waits(self, eng, deps):
        out = []
        for d in deps:
            if d is None:
                continue
            if isinstance(d, list):
                out.extend(self._waits(eng, d))
                continue
            s, v = d
            key = id(s)
            prev = self.waited[eng].get(key, 0)
            if v > prev:
                self.waited[eng][key] = v
                out.append((s, v))
        return out

    def op(self, eng, fn, deps=(), signal=True):
        w = self._waits(eng, deps)
        tok = None
        if signal:
            self.cnt[eng] += 1
            tok = (self.sem[eng], self.cnt[eng])
        self.q[eng].append((fn, w, tok))
        return tok

    def dma(self, eng, fn, sem, val, deps=()):
        w = self._waits(eng, deps)
        self.q[eng].append((fn, w, ("dma", sem)))
        return (sem, val)

    def cur(self, eng):
        if self.cnt[eng] == 0:
            return None
        return (self.sem[eng], self.cnt[eng])

    def wait_only(self, eng, deps):
        w = self._waits(eng, deps)
        if w:
            self.q[eng].append((None, w, None))

    def replay(self, block):
        def run(name):
            def body(e):
                for fn, waits, tok in self.q[name]:
                    for (s, v) in waits:
                        e.wait_ge(s, v)
                    if fn is None:
                        continue
                    ins = fn(e)
                    if tok is not None:
                        if tok[0] == "dma":
                            ins.then_inc(tok[1], 16)
                        else:
                            ins.then_inc(tok[0], 1)
            return body
        block.tensor(run("pe"))
        block.scalar(run("act"))
        block.vector(run("dve"))
        block.gpsimd(run("pool"))
        block.sync(run("sp"))


class PassCfg:
    def __init__(self, name, ntok, groups, samp, prm, halo, xtiles, lru_only, out_tiles):
        self.name = name
        self.ntok = ntok
        self.groups = groups
        self.samp = samp
        self.prm = prm
        self.halo = halo
        self.xtiles = xtiles
        self.lru_only = lru_only
        self.out_tiles = out_tiles


PASSES = [
    PassCfg("p0", 992, [(0, 496), (496, 992)], None, (0, 992), 0,
            [(0, 128, 0), (128, 128, 128), (256, 128, 256), (384, 128, 384), (512, 128, 512), (640, 128, 640),
             (768, 128, 768), (896, 96, 896)], True, []),
    PassCfg("p1", 608, [(0, 160), (160, 608)], (0, 128), (128, 608), HALO,
            [(2048, 128, 0), (992, 32, 128), (1024, 128, 160), (1152, 128, 288), (1280, 128, 416), (1408, 64, 544)],
            False, [(0, 1024, 128), (160, 0, 128), (288, 128, 128), (416, 256, 128), (544, 384, 64)]),
    PassCfg("p2", 576, [(0, 512), (512, 576)], None, (0, 576), 0,
            [(1472, 128, 0), (1600, 128, 128), (1728, 128, 256), (1856, 128, 384), (1984, 64, 512)],
            False, [(0, 448, 128), (128, 576, 128), (256, 704, 128), (384, 832, 128), (512, 960, 64)]),
]
NTMAX = 672


class Builder:
    def __init__(self, debug=False):
        self.debug = debug
        nc = bass.Bass("TRN2", target_bir_lowering=False)
        self.nc = nc
        di = lambda n, s: nc.dram_tensor(n, s, F32, kind="ExternalInput").ap()
        do = lambda n, s: nc.dram_tensor(n, s, F32, kind="ExternalOutput").ap()
        self.xq = di("xq", [2176, D])
        self.cT_d = di("cT", [128, DK, 17])
        self.cvec_d = di("cvec", [128, NV])
        self.sel_d = di("sel", [128, 1])
        self.invc_d = di("invc", [128, 4, 16])
        self.ident_d = di("ident", [128, 128])
        self.st_pool = di("st_pool", [16, 15, PW])
        self.st_lconv = di("st_lconv", [16, 3, D])
        self.st_lh = di("st_lh", [16, D])
        self.st_fconv = di("st_fconv", [16, 2, 2 * DFF])
        self.w_ada = di("w_ada", [D, 6 * D])
        self.w_in = di("w_in", [D, 7168])
        self.w_grp = di("w_pool_grp", [4, 256, 256])
        self.w_rg = di("w_rg", [8, 256, 256])
        self.w_ig = di("w_ig", [8, 256, 256])
        self.w_pup = di("w_pool_up", [PW, D])
        self.w_lup = di("w_lru_up", [D, D])
        self.w_out = di("w_out", [D, D])
        self.w_fup = di("w_ffn_up", [D, 2 * DFF])
        self.w_fdn = di("w_ffn_down", [DFF, D])
        self.y = do("y", [1152, D])
        self.o_pool_p = do("o_pool_p", [15, PW])
        self.o_lconv_p = do("o_lconv_p", [3, D])
        self.o_lh_p = do("o_lh_p", [1, D])
        self.o_fconv_p = do("o_fconv_p", [2, 2 * DFF])
        self.o_pool_s = do("o_pool_s", [16, 15, PW])
        self.o_lconv_s = do("o_lconv_s", [16, 3, D])
        self.o_lh_s = do("o_lh_s", [16, D])
        self.o_fconv_s = do("o_fconv_s", [16, 2, 2 * DFF])

    def sb(self, name, shape, dt):
        return self.st.enter_context(self.nc.sbuf_tensor("sb_" + name, shape, dt))

    def A(self, out, in_, func, scale=None, bias=None, deps=()):
        kw = {}
        if scale is not None:
            kw["scale"] = scale
        if bias is not None:
            kw["bias"] = bias
        return self.P.op("act", lambda e: e.activation(out=out, in_=in_, func=func, **kw), deps=deps)

    def Vtt(self, out, in0, in1, op, deps=(), eng="dve"):
        return self.P.op(eng, lambda e: e.tensor_tensor(out=out, in0=in0, in1=in1, op=op), deps=deps)

    def Vstt(self, out, in0, scalar, in1, op0, op1, deps=()):
        return self.P.op("dve", lambda e: e.scalar_tensor_tensor(out=out, in0=in0, scalar=scalar, in1=in1,
                                                                  op0=op0, op1=op1), deps=deps)

    def Vts(self, out, in0, s1, s2, op0, op1=None, deps=()):
        if op1 is None:
            return self.P.op("dve", lambda e: e.tensor_scalar(out=out, in0=in0, scalar1=s1, scalar2=None, op0=op0),
                             deps=deps)
        return self.P.op("dve", lambda e: e.tensor_scalar(out=out, in0=in0, scalar1=s1, scalar2=s2, op0=op0, op1=op1),
                         deps=deps)

    def Vcopy(self, out, in_, deps=()):
        return self.P.op("dve", lambda e: e.tensor_copy(out=out, in_=in_), deps=deps)

    def MM(self, out, lhsT, rhs, start, stop, deps=(), signal=False):
        return self.P.op("pe", lambda e: e.matmul(out, lhsT=lhsT, rhs=rhs, start=start, stop=stop),
                         deps=deps, signal=signal)

    def TR(self, out, in_, ident, deps=(), signal=False):
        return self.P.op("pe", lambda e: e.transpose(out, in_, ident), deps=deps, signal=signal)

    def barrier(self, extra=()):
        P = self.P
        toks = [P.cur(e) for e in ("pe", "act", "dve")] + list(extra) + self.so_all()
        for e in ("pe", "act", "dve"):
            P.wait_only(e, toks)
        self.last_barrier = toks
        return toks

    def alloc_bank(self):
        for _ in range(8):
            b = self.bank_next
            self.bank_next = (self.bank_next + 1) % 8
            if b not in self.bank_reserved:
                if b in self.bank_busy:
                    raise RuntimeError("PSUM bank %d re-allocated before release" % b)
                self.bank_busy.add(b)
                return b
        raise RuntimeError("no bank")

    def bank_ap(self, b, n, p=128):
        return self.ps[0:p, b, 0:n]

    def release_bank(self, b, toks):
        self.bank_free[b] = list(toks)
        self.bank_busy.discard(b)

    def wload(self, src, kch=16, ncol=256):
        i = self.w_next
        self.w_next = (i + 1) % len(self.wslots)
        slot = self.wslots[i]
        self.w_cnt[i] += 16
        dst = slot[:, 0:kch, 0:ncol]
        srcv = src.rearrange("(k p) n -> p k n", p=128)
        tok = self.P.dma("pool", lambda e: e.dma_start(out=dst, in_=srcv), self.w_sem[i], self.w_cnt[i],
                         deps=[self.w_rel[i]])
        return dst, tok, i

    def wrelease(self, i, tok):
        self.w_rel[i] = tok

    def job(self, groups, parts):
        banks = [self.alloc_bank() for _ in groups]
        n = len(parts)
        tok = None
        for idx, (lhsT, rhs_fn, deps) in enumerate(parts):
            for gi, (c0, c1) in enumerate(groups):
                b = banks[gi]
                d = list(deps)
                if idx == 0:
                    d += self.bank_free[b]
                last = (idx == n - 1 and gi == len(groups) - 1)
                t = self.MM(self.bank_ap(b, c1 - c0), lhsT, rhs_fn(c0, c1), idx == 0, idx == n - 1, deps=d,
                            signal=last)
                if last:
                    tok = t
        return banks, tok

    def build(self):
        nc = self.nc
        with contextlib.ExitStack() as st:
            self.st = st
            self.P = P = Prog(nc, st)
            self.ident = self.sb("ident", [128, 128], F32)
            self.ones = self.sb("ones", [128, 128], BF16)
            self.cvec = self.sb("cvec", [128, NV], F32)
            self.dv = self.sb("dv", [128, 4, 16], F32)
            self.mod = self.sb("mod", [128, 6, 16, 17], F32)
            self.sel = self.sb("sel", [128, 1], F32)
            self.invc = self.sb("invc", [128, 4, 16], F32)
            self.cT = self.sb("cT", [128, DK, 17], F32)
            self.sl = self.sb("sl", [128, DK, 17], BF16)
            self.wgrp = self.sb("wgrp", [128, 4, 2, 256], BF16)
            self.hist_pool = self.sb("hist_pool", [128, 8, 15], F32)
            self.hist_lru = self.sb("hist_lru", [128, 16, 3], F32)
            self.h_carry = self.sb("h_carry", [128, 16], F32)
            self.hist_up = self.sb("hist_up", [128, 96, 2], F32)
            self.rstd = self.sb("rstd", [128, NTMAX], F32)
            self.sq_scratch = self.sb("sqs", [128, NTMAX], F32)
            self.ada_tm_buf = self.sb("ada_tm", [128, 512], F32)
            self.R1 = self.sb("R1", [128, 16 * NTMAX], F32)
            self.R2 = self.sb("R2", [128, 16128], F32)
            self.R4 = self.sb("R4", [128, 16 * NTMAX], F32)
            NW = 4
            self.wslots = [self.sb("w%d" % i, [128, 16, 256], BF16) for i in range(NW)]
            self.w_sem = [P.new_sem("wsem%d" % i) for i in range(NW)]
            self.w_cnt = [0] * NW
            self.w_rel = [None] * NW
            self.w_next = 0
            self.wsm = [self.sb("wsm%d" % i, [128, 2, 2, 256], BF16) for i in range(2)]
            self.wsm_sem = [P.new_sem("wsmsem%d" % i) for i in range(2)]
            self.wsm_cnt = [0, 0]
            self.wsm_rel = [None, None]
            self.wsm_next = 0
            self.ps = st.enter_context(nc.psum_tensor("ps_all", [128, 8, 512], F32))
            self.bank_next = 0
            self.bank_reserved = set()
            self.bank_busy = set()
            self.bank_free = [[] for _ in range(8)]
            self.s_misc = P.new_sem("misc")
            self.misc_cnt = 0
            self.s_x = [P.new_sem("xs0"), P.new_sem("xs1"), P.new_sem("xs2")]
            self.x_cnt = [0, 0, 0]
            self.s_o = [P.new_sem("os0"), P.new_sem("os1")]
            self.o_cnt = [0, 0]
            self.s_so = P.new_sem("so")
            self.s_stf = [P.new_sem("stf0"), P.new_sem("stf1")]
            self.stf_cnt = [0, 0]
            self.so_cnt = 0
            self.so_streams = {}
            self.out_tokens = []
            self.last_barrier = []

            self.prologue()
            for cfg in PASSES:
                self.run_pass(cfg)
            self.epilogue()
            with nc.Block() as block:
                P.replay(block)
        return nc

    def misc_dma(self, eng, out, in_, deps=()):
        self.misc_cnt += 16
        return self.P.dma(eng, lambda e: e.dma_start(out=out, in_=in_), self.s_misc, self.misc_cnt, deps=deps)

    def so_tok(self, stream):
        st_ = self.so_streams.get(stream)
        return (st_[0], st_[1]) if st_ else None

    def so_all(self):
        return [(v[0], v[1]) for v in self.so_streams.values()]

    def so_dma(self, out, in_, deps=(), stream="main"):
        if stream not in self.so_streams:
            self.so_streams[stream] = [self.P.new_sem("so_" + stream), 0]
        st_ = self.so_streams[stream]
        st_[1] += 16
        self.so_cnt += 16
        t = self.P.dma("sp", lambda e: e.dma_start(out=out, in_=in_), st_[0], st_[1], deps=deps)
        return t

    def prologue(self):
        P = self.P
        cv = self.cvec
        lds = []
        lds.append(self.misc_dma("sp", self.ident[:], self.ident_d))
        lds.append(self.misc_dma("sp", self.cvec[:], self.cvec_d))
        lds.append(self.misc_dma("sp", self.sel[:], self.sel_d))
        lds.append(self.misc_dma("sp", self.invc[:], self.invc_d))
        lds.append(self.misc_dma("sp", self.cT[:], self.cT_d))
        ld = lds[-1]
        ld = (self.s_misc, self.misc_cnt)
        s_wg = P.new_sem("wgsem")
        wgv = self.w_grp.rearrange("g (k p) n -> p g k n", p=128)
        self.t_wgrp = P.dma("pool", lambda e: e.dma_start(out=self.wgrp[:], in_=wgv), s_wg, 16)
        t0 = P.op("dve", lambda e: e.memset(self.ones[:], 1.0))
        P.op("dve", lambda e: e.memset(self.hist_pool[:], 0.0))
        P.op("dve", lambda e: e.memset(self.hist_lru[:], 0.0))
        P.op("dve", lambda e: e.memset(self.h_carry[:], 0.0))
        self.t_init = P.op("dve", lambda e: e.memset(self.hist_up[:], 0.0))
        self.Vts(self.dv[:, 0, :], cv[:, CV_BRG:CV_BRG + 16], 0.5, None, ALU.mult, deps=[ld])
        self.Vts(self.dv[:, 1, :], cv[:, CV_BIG:CV_BIG + 16], 0.5, None, ALU.mult)
        ta = self.A(self.dv[:, 2, :], cv[:, CV_LAM:CV_LAM + 16], AF.Exp, scale=-1.0, deps=[ld])
        ta = self.A(self.dv[:, 2, :], self.dv[:, 2, :], AF.Ln, bias=1.0, deps=[ta])
        tv = self.Vts(self.dv[:, 2, :], self.dv[:, 2, :], -8.0, None, ALU.mult, deps=[ta])
        self.t_dv = self.Vts(self.dv[:, 3, :], self.dv[:, 2, :], 0.5, None, ALU.mult, deps=[tv])
        th = self.R4[:, 0:DK * 17].rearrange("p (k j) -> p k j", k=DK)
        ta = self.A(th, self.cT[:], AF.Tanh, scale=0.5, deps=[ld])
        t_sl = self.Vstt(self.sl[:], th, 1.0, self.cT[:], ALU.add, ALU.mult, deps=[ta])
        self.t_sl = t_sl
        self.t_ld = ld
        self.ada_tm_free = None
        self.ada_last = None
        self.ada_pending = list(range(8, 24))
        self.ada_mid_done = False
        for cb in range(8):
            self.ada_item(cb)
        self.ada_finalize([0, 1])
        self.barrier([ld, self.t_wgrp])


    def ada_item(self, cb):
        ada_tm = self.ada_tm_buf
        modf = self.mod[:].rearrange("p m k j -> p (m k) j")
        b = self.alloc_bank()
        tok = None
        for half in range(2):
            src = self.w_ada[:, cb * 512 + half * 256: cb * 512 + (half + 1) * 256]
            w, wt, wi = self.wload(src)
            for k in range(DK):
                d = [wt, self.t_sl] + (self.bank_free[b] if (k == 0 and half == 0) else [])
                tok = self.MM(self.ps[0:17, b, half * 256:(half + 1) * 256], self.sl[:, k, :], w[:, k, :],
                              k == 0, k == DK - 1, deps=d, signal=(k == DK - 1))
            self.wrelease(wi, tok)
        te = self.A(ada_tm[0:17, :], self.ps[0:17, b, :], AF.Copy, scale=0.5, deps=[tok, self.ada_tm_free])
        self.release_bank(b, [te])
        b2 = self.alloc_bank()
        tt = None
        for qq in range(4):
            d = [te, self.t_ld] + (self.bank_free[b2] if qq == 0 else [])
            tt = self.TR(self.ps[:, b2, qq * 17:(qq + 1) * 17], ada_tm[0:17, qq * 128:(qq + 1) * 128],
                         self.ident[0:17, 0:17], deps=d, signal=(qq == 3))
        self.ada_tm_free = tt
        te2 = self.Vcopy(modf[:, cb * 4:(cb + 1) * 4, :],
                         self.ps[:, b2, 0:68].rearrange("p (q j) -> p q j", q=4), deps=[tt])
        self.release_bank(b2, [te2])
        self.ada_last = te2

    def ada_finalize(self, ms):
        cv = self.cvec
        t = self.ada_last
        for m in ms:
            bada = cv[:, CV_BADA + 16 * m:CV_BADA + 16 * (m + 1)].unsqueeze(2).broadcast_to([128, 16, 17])
            t = self.Vtt(self.mod[:, m], self.mod[:, m], bada, ALU.add, deps=[t, self.t_ld])
            goff = {1: CV_GPRE1, 2: CV_GPOST1, 4: CV_GPRE2, 5: CV_GPOST2}.get(m)
            if goff is not None:
                gbc = cv[:, goff:goff + 16].unsqueeze(2).broadcast_to([128, 16, 17])
                if m in (1, 4):
                    t = self.Vstt(self.mod[:, m], self.mod[:, m], 1.0, gbc, ALU.add, ALU.mult, deps=[t])
                else:
                    t = self.Vtt(self.mod[:, m], self.mod[:, m], gbc, ALU.mult, deps=[t])
        self.t_mod = t

    def ada_drain(self):
        while self.ada_pending:
            self.ada_item(self.ada_pending.pop(0))
        self.ada_finalize([2, 3, 4, 5])

    def xT(self, cfg):
        if cfg.lru_only:
            return None
        return self.R1[:, 0:16 * cfg.ntok].rearrange("p (k n) -> p k n", k=16)

    def bfview(self, region, off_bytes, nch, ntok):
        o = off_bytes // 4
        n32 = nch * ntok // 2
        return region[:, o:o + n32].bitcast(BF16).rearrange("p (k n) -> p k n", k=nch)

    def f32view(self, region, off_bytes, nch, ntok):
        o = off_bytes // 4
        return region[:, o:o + nch * ntok].rearrange("p (k n) -> p k n", k=nch)

    def stage_load_x(self, cfg):
        P = self.P
        xT = self.xT(cfg)
        stg = [self.R2[:, 8064:8064 + 2048], self.R2[:, 8064 + 2048:8064 + 4096]]
        stg_free = [None, None]
        last = []
        for ti, (row0, nrows, col0) in enumerate(cfg.xtiles):
            s = ti % 2
            self.x_cnt[s] += 16
            dst = stg[s][0:nrows, :]
            src = self.xq[row0:row0 + nrows, :]
            tl = P.dma("sp", lambda e, dst=dst, src=src: e.dma_start(out=dst, in_=src), self.s_x[s], self.x_cnt[s],
                       deps=[stg_free[s]] + self.last_barrier)
            evs = []
            trs = None
            for g4 in range(4):
                b = self.alloc_bank()
                for qq in range(4):
                    k = g4 * 4 + qq
                    d = [tl] + (self.bank_free[b] if qq == 0 else [])
                    trs = self.TR(self.ps[:, b, qq * 128: qq * 128 + nrows], stg[s][0:nrows, k * 128:(k + 1) * 128],
                                  self.ident[0:nrows, 0:nrows], deps=d, signal=(qq == 3))
                src_ps = self.ps[:, b, :].rearrange("p (q n) -> p q n", q=4)[:, :, 0:nrows]
                dst_x = xT[:, g4 * 4:(g4 + 1) * 4, col0:col0 + nrows]
                if g4 % 2 == 0:
                    te = self.A(dst_x, src_ps, AF.Copy, deps=[trs])
                else:
                    te = self.Vcopy(dst_x, src_ps, deps=[trs])
                self.release_bank(b, [te])
                evs.append(te)
            stg_free[s] = trs
            last = evs
        return [P.cur("act"), P.cur("dve")]

    def stage_front(self, cfg, h):
        P = self.P
        xT = self.xT(cfg)
        NSLOT = 3
        stg = [self.R2[:, 8064 + i * 2048:8064 + (i + 1) * 2048] for i in range(NSLOT)]
        sqb = self.R4[:, 0:2048]
        ssb = [self.R4[:, 2048 + i:2049 + i] for i in range(NSLOT)]
        stg_free = [None] * NSLOT
        AX = mybir.AxisListType.X
        sqf = [None]
        tinfo = {}
        def phaseA(ti):
            row0, nrows, col0 = cfg.xtiles[ti]
            sq_free = sqf[0]
            s = ti % NSLOT
            self.x_cnt[s] += 16
            dst = stg[s][0:nrows, :]
            src = self.xq[row0:row0 + nrows, :]
            tl = P.dma("sp", lambda e, dst=dst, src=src: e.dma_start(out=dst, in_=src), self.s_x[s], self.x_cnt[s],
                       deps=[stg_free[s]] + self.last_barrier)
            tq = self.A(sqb[0:nrows, :], stg[s][0:nrows, :], AF.Square, deps=[tl, sq_free])
            ss = ssb[s][0:nrows, :]
            tr_ = P.op("dve", lambda e, ss=ss, nrows=nrows: e.reduce_sum(out=ss, in_=sqb[0:nrows, :], axis=AX),
                       deps=[tq, stg_free[s]])
            sq_free = tr_
            ta = self.A(ss, ss, AF.Sqrt, scale=1.0 / D, bias=EPS, deps=[tr_])
            trc = P.op("dve", lambda e, ss=ss: e.reciprocal(out=ss, in_=ss), deps=[ta])
            raw_done = []
            if not cfg.lru_only:
                for g4 in range(4):
                    b = self.alloc_bank()
                    trs = None
                    for qq in range(4):
                        k = g4 * 4 + qq
                        d = [tl, self.t_ld] + (self.bank_free[b] if qq == 0 else [])
                        trs = self.TR(self.ps[:, b, qq * 128: qq * 128 + nrows],
                                      stg[s][0:nrows, k * 128:(k + 1) * 128],
                                      self.ident[0:nrows, 0:nrows], deps=d, signal=(qq == 3))
                    src_ps = self.ps[:, b, :].rearrange("p (q n) -> p q n", q=4)[:, :, 0:nrows]
                    dst_x = xT[:, g4 * 4:(g4 + 1) * 4, col0:col0 + nrows]
                    if g4 % 2 == 0:
                        te = self.A(dst_x, src_ps, AF.Copy, deps=[trs])
                    else:
                        te = self.Vcopy(dst_x, src_ps, deps=[trs])
                    self.release_bank(b, [te])
                    raw_done = [trs]
            tsc = self.Vts(stg[s][0:nrows, :], stg[s][0:nrows, :], ss, None, ALU.mult, deps=[trc, tl] + raw_done)
            sqf[0] = sq_free
            tinfo[ti] = (s, nrows, col0, tsc)

        def phaseB(ti):
            s, nrows, col0, tsc = tinfo.pop(ti)
            is_samp = cfg.samp is not None and col0 == 0
            last_tr = None
            for g4 in range(4):
                b = self.alloc_bank()
                trs = None
                for qq in range(4):
                    k = g4 * 4 + qq
                    d = [tsc, self.t_ld] + (self.bank_free[b] if qq == 0 else [])
                    trs = self.TR(self.ps[:, b, qq * 128: qq * 128 + nrows], stg[s][0:nrows, k * 128:(k + 1) * 128],
                                  self.ident[0:nrows, 0:nrows], deps=d, signal=(qq == 3))
                last_tr = trs
                rel = []
                for qq in range(4):
                    k = g4 * 4 + qq
                    src_ps = self.ps[:, b, qq * 128: qq * 128 + nrows]
                    dst_h = h[:, k, col0:col0 + nrows]
                    if is_samp:
                        s3 = src_ps.rearrange("p (t s) -> p t s", t=8)
                        d3 = dst_h.rearrange("p (t s) -> p t s", t=8)
                        scb = self.mod[:, 1, k, 1:17].unsqueeze(1).broadcast_to([128, 8, 16])
                        shb = self.mod[:, 0, k, 1:17].unsqueeze(1).broadcast_to([128, 8, 16])
                        tmp3 = self.R4[:, 2056 + qq * 128:2056 + (qq + 1) * 128].rearrange("p (t s) -> p t s", t=8)
                        t1 = self.Vtt(tmp3, s3, scb, ALU.mult, deps=[trs, self.t_mod])
                        t2 = self.Vtt(d3, tmp3, shb, ALU.add, deps=[t1])
                        rel += [t1, t2]
                    elif g4 % 2 == 0:
                        rel.append(self.A(dst_h, src_ps, AF.Identity, scale=self.mod[:, 1, k, 0:1],
                                          bias=self.mod[:, 0, k, 0:1], deps=[trs, self.t_mod]))
                    else:
                        rel.append(self.Vts(dst_h, src_ps, self.mod[:, 1, k, 0:1], self.mod[:, 0, k, 0:1],
                                            ALU.mult, ALU.add, deps=[trs, self.t_mod]))
                self.release_bank(b, rel)
            stg_free[s] = last_tr
        nt_ = len(cfg.xtiles)
        phaseA(0)
        for ti in range(nt_):
            if ti + 1 < nt_:
                phaseA(ti + 1)
            phaseB(ti)
        toks = [P.cur("act"), P.cur("dve")]
        if cfg.halo:
            p0 = cfg.prm[0]
            hv = h[:, :, p0:p0 + cfg.halo]
            t = self.Vts(hv, hv, self.sel[:, 0:1], None, ALU.mult, deps=toks)
            toks = toks + [t]
        return toks

    def stage_stats(self, cfg, src, src_ready, pre_scale=1.0):
        P = self.P
        ntok = cfg.ntok
        sq = [self.R4[:, 0:ntok // 2].bitcast(BF16), self.R4[:, 512:512 + ntok // 2].bitcast(BF16)]
        sq_free = [None, None]
        banks = [self.alloc_bank() for _ in cfg.groups]
        for b in banks:
            self.bank_reserved.add(b)
        tok = None
        for k in range(DK):
            s = k % 2
            if k % 2 == 0:
                tq = self.A(sq[s][:, 0:ntok], src[:, k, :], AF.Square, deps=[src_ready, sq_free[s]])
            else:
                tq = self.Vtt(sq[s][:, 0:ntok], src[:, k, :], src[:, k, :], ALU.mult, deps=[src_ready, sq_free[s]])
            for gi, (c0, c1) in enumerate(cfg.groups):
                d = [tq] + (self.bank_free[banks[gi]] if k == 0 else [])
                tok = self.MM(self.bank_ap(banks[gi], c1 - c0), self.ones[:], sq[s][:, c0:c1], k == 0, k == DK - 1,
                              deps=d, signal=True)
            sq_free[s] = tok
        toks = []
        for gi, (c0, c1) in enumerate(cfg.groups):
            ta = self.A(self.rstd[:, c0:c1], self.bank_ap(banks[gi], c1 - c0), AF.Sqrt, scale=1.0 / D, bias=EPS,
                        deps=[tok])
            tv = P.op("dve", lambda e, c0=c0, c1=c1: e.reciprocal(out=self.rstd[:, c0:c1], in_=self.rstd[:, c0:c1]),
                      deps=[ta])
            self.release_bank(banks[gi], [ta])
            self.bank_reserved.discard(banks[gi])
            toks.append(tv)
        return toks

    def stage_normmod(self, cfg, src, dst, mi_shift, mi_scale, deps):
        P = self.P
        ntok = cfg.ntok
        p0, p1 = cfg.prm
        tmp = [self.R4[:, 1024:1024 + ntok], self.R4[:, 1024 + NTMAX:1024 + NTMAX + ntok]]
        tmp_free = [None, None]
        for k in range(DK):
            s = k % 2
            t1 = self.Vtt(tmp[s], src[:, k, :], self.rstd[:, 0:ntok], ALU.mult, deps=[deps, tmp_free[s]])
            ta = self.A(dst[:, k, p0:p1], tmp[s][:, p0:p1], AF.Identity, scale=self.mod[:, mi_scale, k, 0:1],
                        bias=self.mod[:, mi_shift, k, 0:1], deps=[t1, self.t_mod])
            rel = [ta]
            if cfg.samp is not None:
                v3 = tmp[s][:, 0:128].rearrange("p (t s) -> p t s", t=8)
                scb = self.mod[:, mi_scale, k, 1:17].unsqueeze(1).broadcast_to([128, 8, 16])
                shb = self.mod[:, mi_shift, k, 1:17].unsqueeze(1).broadcast_to([128, 8, 16])
                t2 = self.Vtt(v3, v3, scb, ALU.mult, deps=[t1, self.t_mod])
                t3 = self.Vtt(dst[:, k, 0:128].rearrange("p (t s) -> p t s", t=8), v3, shb, ALU.add, deps=[t2])
                rel.append(t3)
            tmp_free[s] = rel
        toks = [P.cur("act"), P.cur("dve")]
        if cfg.halo:
            hv = dst[:, :, p0:p0 + cfg.halo]
            t = self.Vts(hv, hv, self.sel[:, 0:1], None, ALU.mult, deps=toks)
            toks = [t]
        return toks

    def stage_resid(self, cfg, acc, mi_gate, deps, fuse_stats=False):
        P = self.P
        xT = self.xT(cfg)
        ntok = cfg.ntok
        p0, p1 = cfg.prm
        if fuse_stats:
            sq = [self.R4[:, 0:ntok // 2].bitcast(BF16), self.R4[:, 512:512 + ntok // 2].bitcast(BF16)]
            sq_free = [None, None]
            banks = [self.alloc_bank() for _ in cfg.groups]
            for b in banks:
                self.bank_reserved.add(b)
            tok = None
        for k in range(DK):
            t1 = self.Vtt(acc[:, k, :], acc[:, k, :], self.rstd[:, 0:ntok], ALU.mult, deps=[deps])
            done = [self.Vstt(xT[:, k, p0:p1], acc[:, k, p0:p1], self.mod[:, mi_gate, k, 0:1], xT[:, k, p0:p1],
                              ALU.mult, ALU.add, deps=[t1, self.t_mod])]
            if cfg.samp is not None:
                v3 = acc[:, k, 0:128].rearrange("p (t s) -> p t s", t=8)
                gtb = self.mod[:, mi_gate, k, 1:17].unsqueeze(1).broadcast_to([128, 8, 16])
                t2 = self.Vtt(v3, v3, gtb, ALU.mult, deps=[t1, self.t_mod])
                x3 = xT[:, k, 0:128].rearrange("p (t s) -> p t s", t=8)
                done.append(self.Vtt(x3, x3, v3, ALU.add, deps=[t2]))
            if fuse_stats:
                s_ = k % 2
                tq = self.A(sq[s_][:, 0:ntok], xT[:, k, :], AF.Square, deps=done + [sq_free[s_]])
                for gi, (c0, c1) in enumerate(cfg.groups):
                    d = [tq] + (self.bank_free[banks[gi]] if k == 0 else [])
                    tok = self.MM(self.bank_ap(banks[gi], c1 - c0), self.ones[:], sq[s_][:, c0:c1], k == 0,
                                  k == DK - 1, deps=d, signal=True)
                sq_free[s_] = tok
        if not fuse_stats:
            return [P.cur("dve")]
        toks = []
        last_dve = P.cur("dve")
        for gi, (c0, c1) in enumerate(cfg.groups):
            ta = self.A(self.rstd[:, c0:c1], self.bank_ap(banks[gi], c1 - c0), AF.Sqrt, scale=1.0 / D, bias=EPS,
                        deps=[tok, last_dve])
            tv = P.op("dve", lambda e, c0=c0, c1=c1: e.reciprocal(out=self.rstd[:, c0:c1], in_=self.rstd[:, c0:c1]),
                      deps=[ta])
            self.release_bank(banks[gi], [ta])
            self.bank_reserved.discard(banks[gi])
            toks.append(tv)
        return toks

    def ext_layout(self, cfg, H):
        if cfg.samp is not None:
            hs = H * 16
            return hs + cfg.ntok, hs, hs + 128, hs
        return H + cfg.ntok, H, H, None

    def run_pass(self, cfg):
        P = self.P
        nt = cfg.ntok
        p0, p1 = cfg.prm
        Lp = p1 - p0
        groups = cfg.groups
        cv = self.cvec
        xT = self.xT(cfg)
        h = self.bfview(self.R2, 0, 16, nt)
        if cfg.lru_only:
            th = self.stage_front(cfg, h)
            self.barrier()
            self.stage_lru(cfg, h, th, None)
            self.barrier()
            return
        y_pool = self.bfview(self.R2, 21504, 8, nt)
        y_lru = self.bfview(self.R2, 32256, 16, nt)
        o_sb = self.f32view(self.R2, 0, 16, nt)
        merged = self.bfview(self.R4, 0, 16, nt)
        h2 = self.bfview(self.R4, 21504, 16, nt)
        d_sb = self.f32view(self.R4, 0, 16, nt)
        f = self.bfview(self.R2, 0, 48, nt)

        th = self.stage_front(cfg, h)
        self.barrier()
        h_ready = th

        if not cfg.lru_only:
            self.stage_pool(cfg, h, h_ready, y_pool)
            self.barrier()
        self.stage_lru(cfg, h, h_ready, y_lru)
        self.barrier()
        if cfg.lru_only:
            return
        self.stage_merge(cfg, h, y_pool, y_lru, merged)
        self.barrier()
        self.stage_proj_norm(cfg, merged, self.w_out, 16, o_sb, 0.5)
        if self.ada_pending is not None and not self.ada_mid_done:
            while self.ada_pending and self.ada_pending[0] < 20:
                self.ada_item(self.ada_pending.pop(0))
            self.ada_finalize([2, 3, 4])
            self.ada_mid_done = True
        self.barrier()
        ts = self.stage_resid(cfg, o_sb, 2, [P.cur("dve"), P.cur("act")], fuse_stats=True)
        th2 = self.stage_normmod(cfg, xT, h2, 3, 4, ts)
        self.barrier()
        self.stage_ffn_up(cfg, h2, f)
        if self.ada_pending is not None:
            while self.ada_pending:
                self.ada_item(self.ada_pending.pop(0))
            self.ada_finalize([5])
            self.ada_pending = None
        self.barrier(self.so_all())
        self.stage_proj_norm(cfg, f, self.w_fdn, 48, d_sb, 1.0)
        self.barrier()
        self.stage_resid(cfg, d_sb, 5, [P.cur("dve"), P.cur("act")])
        self.barrier()
        self.stage_store_y(cfg)
        self.barrier(self.out_tokens)

    def stage_pool(self, cfg, h, h_ready, y_pool):
        P = self.P
        nt = cfg.ntok
        p0, p1 = cfg.prm
        Lp = p1 - p0
        cv = self.cvec
        W, cur, prm, scur = self.ext_layout(cfg, 15)
        def u_ap(c, i):
            o = (c * 3 + i) * 928
            return self.R4[:, o:o + W]
        dbuf = [self.R4[:, 6 * 928 + c * 336: 6 * 928 + c * 336 + nt // 2].bitcast(BF16) for c in range(2)]
        sstage = self.R4[:, 7256:7256 + 2048]
        t_hl = None
        if cfg.samp is not None:
            for r in range(15):
                tno, rr = (0, r) if r < 8 else (1, r - 8)
                self.misc_dma("sp", sstage[rr * 16:(rr + 1) * 16, tno * 1024:(tno + 1) * 1024], self.st_pool[:, r, :],
                              deps=self.last_barrier)
            t_hl = (self.s_misc, self.misc_cnt)
        fix = self.R4[:, 6 * 928 + 2 * 336 + 512: 6 * 928 + 2 * 336 + 512 + 16]
        so_stage = self.R4[:, 7000:7000 + 256]
        prev_done = None
        for g in range(4):
            w = 2 ** (g + 1)
            if self.ada_pending and cfg.samp is not None and self.ada_pending[0] < 16:
                self.ada_item(self.ada_pending.pop(0))
            wsl, wt, wi = self.wload(self.w_in[:, g * 256:(g + 1) * 256])
            dtoks = []
            hist_b = []
            if cfg.samp is not None:
                for c in range(2):
                    b = self.alloc_bank()
                    tt = None
                    for tno, nrow in enumerate((128, 112)):
                        d = [t_hl] + (self.bank_free[b] if tno == 0 else [])
                        tt = self.TR(self.ps[:, b, tno * 128: tno * 128 + nrow],
                                     sstage[0:nrow, tno * 1024 + (2 * g + c) * 128: tno * 1024 + (2 * g + c + 1) * 128],
                                     self.ident[0:nrow, 0:nrow], deps=d, signal=(tno == 1))
                    hist_b.append((b, tt))
            zjobs = []
            for c in range(2):
                banks, tok = self.job(cfg.groups, [(wsl[:, k, c * 128:(c + 1) * 128],
                                                    (lambda c0, c1, k=k: h[:, k, c0:c1]), [wt, h_ready])
                                                   for k in range(DK)])
                zjobs.append((banks, tok))
            self.wrelease(wi, zjobs[1][1])
            evs_c = []
            for c in range(2):
                ch = 2 * g + c
                U = u_ap(c, 0)
                banks, tok = zjobs[c]
                evs = []
                for gi, (c0, c1) in enumerate(cfg.groups):
                    evs.append(self.A(U[:, cur + c0:cur + c1], self.bank_ap(banks[gi], c1 - c0), AF.Copy,
                                      deps=[tok, prev_done]))
                    self.release_bank(banks[gi], [evs[-1]])
                if cfg.samp is not None:
                    b, tt = hist_b[c]
                    tevh = self.Vcopy(U[:, 0:240], self.ps[:, b, 0:240], deps=[tt, prev_done])
                    self.release_bank(b, [tevh])
                    evs.append(tevh)
                else:
                    evs.append(self.Vcopy(U[:, 0:15], self.hist_pool[:, ch, :], deps=[prev_done, self.t_init]))
                evs_c.append(evs)
                st_ = 16 if cfg.samp is not None else 1
                regions = []
                if cfg.samp is not None:
                    regions.append((0, 240 + 128, 16, 240))
                    regions.append((prm - 15, W, 1, prm))
                else:
                    regions.append((0, W, 1, cur))
                bufs = [U, u_ap(c, 1), u_ap(c, 2)]
                tlast = evs
                for (r0, r1, strd, fo) in regions:
                    srcb = U
                    di = 1
                    sh = 1
                    tl = tlast
                    lo = r0
                    while sh < w:
                        dstb = bufs[di]
                        lo2 = lo + sh * strd
                        tl = [self.Vtt(dstb[:, lo2:r1], srcb[:, lo2:r1], srcb[:, lo2 - sh * strd:r1 - sh * strd],
                                       ALU.add, deps=tl)]
                        srcb = dstb
                        di = 2 if di == 1 else 1
                        lo = lo2
                        sh *= 2
                    n = r1 - fo
                    dcol = 0 if (cfg.samp is not None and strd == 16) else p0
                    td = self.Vstt(dbuf[c][:, dcol:dcol + n], srcb[:, fo:r1], 1.0 / w, U[:, fo:r1], ALU.mult,
                                   ALU.subtract, deps=tl)
                    dtoks.append(td)
                    tlast = evs + [td]
                    if cfg.halo and strd == 1:
                        m0 = fo + cfg.halo
                        tf = self.Vtt(fix, srcb[:, m0:m0 + 16], self.invc[:, g, :], ALU.mult, deps=tl)
                        td2 = self.Vtt(dbuf[c][:, p0 + cfg.halo:p0 + cfg.halo + 16], fix, U[:, m0:m0 + 16],
                                       ALU.subtract, deps=[tf, td])
                        dtoks.append(td2)
                tsv = self.A(self.hist_pool[:, ch, :], U[:, W - 15:W], AF.Copy, deps=evs)
                dtoks.append(tsv)
            if cfg.samp is not None:
                for c in range(2):
                    U = u_ap(c, 0)
                    b = self.alloc_bank()
                    tt = self.TR(self.ps[:, b, 0:128], U[:, 240:368], self.ident[:],
                                 deps=evs_c[c] + self.bank_free[b], signal=True)
                    te = self.A(so_stage[:, c * 128:(c + 1) * 128], self.ps[:, b, 0:128], AF.Copy,
                                deps=[tt, prev_done])
                    self.release_bank(b, [te])
                    dtoks += [te, tt]
                so_toks = []
                for t in range(8):
                    so_toks.append(self.so_dma(self.o_pool_s[:, 7 + t, g * 256:(g + 1) * 256],
                                               so_stage[t * 16:(t + 1) * 16, :], deps=dtoks, stream="pool"))
                t_so = self.so_tok("pool")
            ytoks = []
            for j in range(2):
                banks, tok = self.job(cfg.groups, [(self.wgrp[:, g, kk, j * 128:(j + 1) * 128],
                                                    (lambda c0, c1, kk=kk: dbuf[kk][:, c0:c1]),
                                                    [self.t_wgrp] + dtoks) for kk in range(2)])
                for gi, (c0, c1) in enumerate(cfg.groups):
                    te = self.A(y_pool[:, 2 * g + j, c0:c1], self.bank_ap(banks[gi], c1 - c0), AF.Copy,
                                scale=cv[:, CV_PSCALE + 2 * g + j:CV_PSCALE + 2 * g + j + 1], deps=[tok])
                    self.release_bank(banks[gi], [te])
                    ytoks.append(te)
            prev_done = [P.cur("pe"), P.cur("dve"), P.cur("act")]
            if cfg.samp is not None:
                prev_done = prev_done + [t_so]
        if cfg.samp is not None:
            self.so_dma(self.o_pool_s[:, 0:7, :], self.st_pool[:, 8:15, :], stream="carry")
            self.out_tokens += self.so_all()

    def wsm_load(self, blk):
        i = self.wsm_next
        self.wsm_next = (i + 1) % 2
        self.wsm_cnt[i] += 32
        P = self.P
        dst0 = self.wsm[i][:, 0]
        dst1 = self.wsm[i][:, 1]
        s0 = self.w_rg[blk].rearrange("(k p) n -> p k n", p=128)
        s1 = self.w_ig[blk].rearrange("(k p) n -> p k n", p=128)
        P.dma("pool", lambda e: e.dma_start(out=dst0, in_=s0), self.wsm_sem[i], 0, deps=[self.wsm_rel[i]])
        tok = P.dma("pool", lambda e: e.dma_start(out=dst1, in_=s1), self.wsm_sem[i], self.wsm_cnt[i])
        return self.wsm[i], tok, i

    def stage_lru(self, cfg, h, h_ready, y_lru):
        P = self.P
        nt = cfg.ntok
        p0, p1 = cfg.prm
        Lp = p1 - p0
        cv = self.cvec
        W, cur, prm, scur = self.ext_layout(cfg, 3)
        samp = cfg.samp is not None
        big = nt > NTMAX
        NTP = 992 if big else NTMAX
        UW = 1000 if big else 768
        CS, GSZ = UW + NTP, 3 * NTP
        def U_ap(s_, c):
            o = (s_ * 2 + c) * CS
            return self.R4[:, o:o + W]
        def xc_ap(s_, c):
            o = (s_ * 2 + c) * CS + UW
            return self.R4[:, o:o + nt]
        def ap3(base, stride):
            return bass.AP(base.tensor, base.offset, [list(base.ap[0]), [stride, 2], [1, nt]])
        def xc3(s_):
            return ap3(xc_ap(s_, 0), CS)
        NGS = 2 if cfg.lru_only else 1
        gs_cur = [0]
        def g_ap(c, i):
            if big:
                reg, base = (self.R1, 0) if gs_cur[0] == 0 else (self.R2, 8064)
                o = base + c * GSZ + i * NTP
                return reg[:, o:o + nt]
            if gs_cur[0] == 1:
                o = c * GSZ + i * NTP
                return self.R1[:, o:o + nt]
            o = 5760 + c * GSZ + i * NTP
            return self.R4[:, o:o + nt]
        def g3(i):
            return ap3(g_ap(0, i), GSZ)
        XH = NTP // 2
        if big:
            xcb_t = [self.R1[:, 5952:5952 + 2 * XH], self.R1[:, 5952 + 2 * XH:5952 + 4 * XH]]
        else:
            xcb_t = [self.rstd, self.sq_scratch]
        def xcb_ap(s_, c):
            return xcb_t[s_][:, c * XH:c * XH + nt // 2].bitcast(BF16)
        def xcb3(s_):
            return xcb_t[s_][:, 0:2 * XH].bitcast(BF16).rearrange("p (c n) -> p c n", c=2)[:, :, 0:nt]
        stage_start = list(self.last_barrier)
        if samp:
            st_l = self.R2[:, 13440:13440 + 2048]
            for r in range(3):
                self.misc_dma("sp", st_l[r * 16:(r + 1) * 16, :], self.st_lconv[:, r, :], deps=stage_start)
            self.misc_dma("sp", st_l[48:64, :], self.st_lh, deps=stage_start)
            t_stl = (self.s_misc, self.misc_cnt)
            h0s = self.R2[:, 15488:15488 + 256].rearrange("p (k s) -> p k s", k=16)
            so_conv = self.R2[:, 15744:15744 + 256]
        st = {"conv_free": [None, None], "xcb_free": [None, None], "gate_free": [None, None], "so1": None, "so2": None,
              "s1": {}}

        def S1A(blk):
            s_ = blk % 2
            if self.ada_pending:
                if cfg.lru_only and blk % 2 == 0 and self.ada_pending[0] < 12:
                    self.ada_item(self.ada_pending.pop(0))
                elif samp and blk % 2 == 0 and self.ada_pending[0] < 20:
                    self.ada_item(self.ada_pending.pop(0))
            wsl, wt, wi = self.wload(self.w_in[:, 1024 + blk * 256:1024 + (blk + 1) * 256])
            wg, wgt, wgi = self.wsm_load(blk)
            cfree = st["conv_free"][s_]
            tap0 = []
            evs_c = []
            hist_toks = []
            hist_tr = []
            if samp:
                for c in range(2):
                    ch = 2 * blk + c
                    b = self.alloc_bank()
                    tt = self.TR(self.ps[:, b, 0:64], st_l[0:64, ch * 128:(ch + 1) * 128], self.ident[0:64, 0:64],
                                 deps=[t_stl] + self.bank_free[b], signal=True)
                    hist_tr.append((b, tt))
            jobs = []
            for c in range(2):
                banks, tok = self.job(cfg.groups, [(wsl[:, k, c * 128:(c + 1) * 128],
                                                    (lambda c0, c1, k=k: h[:, k, c0:c1]), [wt, h_ready])
                                                   for k in range(DK)])
                jobs.append((banks, tok))
            self.wrelease(wi, jobs[1][1])
            for c in range(2):
                ch = 2 * blk + c
                U = U_ap(s_, c)
                xc = xc_ap(s_, c)
                banks, tok = jobs[c]
                evs = []
                if samp:
                    b, tt = hist_tr[c]
                    te1 = self.Vcopy(U[:, 0:48], self.ps[:, b, 0:48], deps=[tt, cfree])
                    te2 = self.Vcopy(h0s[:, ch, :], self.ps[:, b, 48:64], deps=[tt])
                    self.release_bank(b, [te1, te2])
                    evs += [te1, te2]
                else:
                    evs.append(self.Vcopy(U[:, 0:3], self.hist_lru[:, ch, :], deps=[cfree, self.t_init]))
                for gi, (c0, c1) in enumerate(cfg.groups):
                    evs.append(self.A(U[:, cur + c0:cur + c1], self.bank_ap(banks[gi], c1 - c0), AF.Copy,
                                      deps=[tok, cfree]))
                    self.release_bank(banks[gi], [evs[-1]])
                wl3 = cv[:, CV_WLC + 48 + ch:CV_WLC + 48 + ch + 1]
                bl = cv[:, CV_BLC + ch:CV_BLC + ch + 1]
                regs = []
                if samp:
                    regs.append((48, 128, 16, 0))
                    regs.append((prm, Lp, 1, p0))
                else:
                    regs.append((cur, Lp, 1, p0))
                t0s = []
                for (co, n, strd, xo) in regs:
                    t0s.append(self.A(xc[:, xo:xo + n], U[:, co:co + n], AF.Identity, scale=wl3, bias=bl,
                                      deps=evs + [cfree]))
                tap0.append((regs, t0s))
                evs_c.append(evs)
                hist_toks.append(self.A(self.hist_lru[:, ch, :], U[:, W - 3:W], AF.Copy, deps=evs))
            if samp:
                for c in range(2):
                    U = U_ap(s_, c)
                    b = self.alloc_bank()
                    tt = self.TR(self.ps[0:48, b, 0:128], U[:, 128:176], self.ident[:],
                                 deps=evs_c[c] + self.bank_free[b], signal=True)
                    te = self.A(so_conv[0:48, c * 128:(c + 1) * 128], self.ps[0:48, b, 0:128], AF.Copy,
                                deps=[tt, st["so1"]])
                    self.release_bank(b, [te])
                    hist_toks += [te, tt]
                for r in range(3):
                    self.so_dma(self.o_lconv_s[:, r, blk * 256:(blk + 1) * 256], so_conv[r * 16:(r + 1) * 16, :],
                                deps=hist_toks, stream="lru1")
                st["so1"] = self.so_tok("lru1")
            st["s1"][blk] = dict(wg=wg, wgt=wgt, wgi=wgi, tap0=tap0, evs=evs_c, ureaders=list(hist_toks))

        def S1B(blk):
            s_ = blk % 2
            info = st["s1"][blk]
            ctoks_all = []
            for c in range(2):
                ch = 2 * blk + c
                U = U_ap(s_, c)
                xc = xc_ap(s_, c)
                regs, t0s = info["tap0"][c]
                for ri, (co, n, strd, xo) in enumerate(regs):
                    t = t0s[ri]
                    for k in range(3):
                        sh = (3 - k) * strd
                        t = self.Vstt(xc[:, xo:xo + n], U[:, co - sh:co - sh + n],
                                      cv[:, CV_WLC + 16 * k + ch:CV_WLC + 16 * k + ch + 1], xc[:, xo:xo + n],
                                      ALU.mult, ALU.add, deps=[t] + info["evs"][c])
                    ctoks_all.append(t)
            tb = self.Vcopy(xcb3(s_), xc3(s_), deps=ctoks_all + [st["xcb_free"][s_]])
            info["tb"] = tb
            info["ureaders"] += ctoks_all

        def P1(blk):
            s_ = blk % 2
            info = st["s1"][blk]
            wg, wgt, wgi, tb = info["wg"], info["wgt"], info["wgi"], info["tb"]
            gs_cur[0] = blk % NGS
            gfree = st["gate_free"][blk % NGS]
            tanh_r, tanh_i = [], []
            lastpe = None
            for j in range(2):
                ch = 2 * blk + j
                a = g_ap(j, 0)
                g = g_ap(j, 2)
                for (gi_, dst, brow, lst) in ((0, a, 0, tanh_r), (1, g, 1, tanh_i)):
                    banks, tok = self.job(cfg.groups, [(wg[:, gi_, kk, j * 128:(j + 1) * 128],
                                                        (lambda c0, c1, kk=kk: xcb_ap(s_, kk)[:, c0:c1]), [wgt, tb])
                                                       for kk in range(2)])
                    lastpe = tok
                    for gi, (c0, c1) in enumerate(cfg.groups):
                        t = self.A(dst[:, c0:c1], self.bank_ap(banks[gi], c1 - c0), AF.Tanh, scale=0.5,
                                   bias=self.dv[:, brow, ch:ch + 1], deps=[tok, gfree, self.t_dv])
                        self.release_bank(banks[gi], [t])
                        lst.append(t)
            self.wsm_rel[wgi] = lastpe
            st["xcb_free"][s_] = lastpe
            ta = []
            for j in range(2):
                ch = 2 * blk + j
                a = g_ap(j, 0)
                ta.append(self.A(a, a, AF.Exp, scale=self.dv[:, 3, ch:ch + 1], bias=self.dv[:, 3, ch:ch + 1],
                                 deps=tanh_r))
            tgx = self.Vstt(g3(2), g3(2), 1.0, xc3(s_), ALU.add, ALU.mult, deps=tanh_i + [tb])
            st["conv_free"][s_] = info["ureaders"] + [tgx, tb]
            tm = self.A(g3(1), g3(0), AF.Square, deps=ta + [gfree])
            info.update(ta=ta, tgx=tgx, tm=tm)

        def P2a(blk):
            gs_cur[0] = blk % NGS
            info = st["s1"][blk]
            tm = self.Vts(g3(1), g3(1), -0.25, 0.25, ALU.mult, ALU.add, deps=[info["tm"]])
            info["tsq"] = self.A(g3(1), g3(1), AF.Sqrt, deps=[tm])

        def P2(blk):
            gs_cur[0] = blk % NGS
            info = st["s1"][blk]
            ta, tgx, tsq = info["ta"], info["tgx"], info["tsq"]
            tuu = self.Vtt(g3(2), g3(2), g3(1), ALU.mult, deps=[tgx, tsq])
            fin = []
            for j in range(2):
                ch = 2 * blk + j
                a, hs, g = g_ap(j, 0), g_ap(j, 1), g_ap(j, 2)
                hc = self.h_carry[:, ch:ch + 1]
                if cfg.halo:
                    t1 = P.op("dve", lambda e, a=a, g=g, hs=hs, hc=hc: e.tensor_tensor_scan(
                        out=hs[:, p0:p0 + HALO], data0=a[:, p0:p0 + HALO], data1=g[:, p0:p0 + HALO], initial=hc,
                        op0=ALU.mult, op1=ALU.add), deps=[tuu] + ta + [self.t_init])
                    hm = self.R2[:, 16100 + j:16101 + j]
                    t2 = self.Vts(hm, hs[:, p0 + HALO - 1:p0 + HALO], self.sel[:, 0:1], None, ALU.mult, deps=[t1])
                    t3 = P.op("dve", lambda e, a=a, g=g, hs=hs, hm=hm: e.tensor_tensor_scan(
                        out=hs[:, p0 + HALO:p1], data0=a[:, p0 + HALO:p1], data1=g[:, p0 + HALO:p1], initial=hm,
                        op0=ALU.mult, op1=ALU.add), deps=[t2])
                else:
                    t3 = P.op("dve", lambda e, a=a, g=g, hs=hs, hc=hc: e.tensor_tensor_scan(
                        out=hs[:, p0:p1], data0=a[:, p0:p1], data1=g[:, p0:p1], initial=hc,
                        op0=ALU.mult, op1=ALU.add), deps=[tuu] + ta + [self.t_init])
                t4 = self.Vcopy(hc, hs[:, p1 - 1:p1], deps=[t3])
                fin += [t3, t4]
            if samp:
                a3, hs3, gg3 = g3(0), g3(1), g3(2)
                prev = h0s[:, 2 * blk:2 * blk + 2, :]
                t = [tuu] + ta
                for tstep in range(8):
                    sl_ = slice(tstep * 16, (tstep + 1) * 16)
                    t = [self.Vtt(hs3[:, :, sl_], a3[:, :, sl_], prev, ALU.mult, deps=t)]
                    t = [self.Vtt(hs3[:, :, sl_], hs3[:, :, sl_], gg3[:, :, sl_], ALU.add, deps=t)]
                    prev = hs3[:, :, sl_]
                fin += t
            info["fin"] = fin

        def P3(blk):
            gs_cur[0] = blk % NGS
            info = st["s1"].pop(blk)
            fin = info["fin"]
            if not cfg.lru_only:
                ty = self.A(y_lru[:, 2 * blk:2 * blk + 2, :], g3(1), AF.Copy, deps=fin)
                fin.append(ty)
            if samp:
                so_t = []
                for j in range(2):
                    hs = g_ap(j, 1)
                    b = self.alloc_bank()
                    tt = self.TR(self.ps[0:16, b, 0:128], hs[:, 112:128], self.ident[:],
                                 deps=fin + self.bank_free[b], signal=True)
                    te = self.A(so_conv[64:80, j * 128:(j + 1) * 128], self.ps[0:16, b, 0:128], AF.Copy,
                                deps=[tt, st["so2"]])
                    self.release_bank(b, [te])
                    so_t += [te, tt]
                self.so_dma(self.o_lh_s[:, blk * 256:(blk + 1) * 256], so_conv[64:80, :], deps=so_t, stream="lru2")
                st["so2"] = self.so_tok("lru2")
                fin += so_t
            st["gate_free"][blk % NGS] = fin

        S1A(0)
        S1B(0)
        S1A(1)
        S1B(1)
        for blk in range(8):
            P1(blk)
            P2a(blk)
            if blk + 2 < 8:
                S1A(blk + 2)
            P2(blk)
            if blk + 2 < 8:
                S1B(blk + 2)
            P3(blk)
        if samp:
            self.out_tokens += self.so_all()

    def stage_merge(self, cfg, h, y_pool, y_lru, merged):
        P = self.P
        nt = cfg.ntok
        base = 5376
        gb = [self.R4[:, base + i * NTMAX: base + i * NTMAX + nt] for i in range(4)]
        prev_done = None
        for q in range(8):
            specs = [(self.w_in[:, 3072 + q * 256:3072 + (q + 1) * 256], 16, h, 0),
                     (self.w_in[:, 5120 + q * 256:5120 + (q + 1) * 256], 16, h, 2)]
            for (src, kch, opnd, bi) in specs:
                wsl, wt, wi = self.wload(src, kch)
                for j in range(2):
                    banks, tok = self.job(cfg.groups, [(wsl[:, k, j * 128:(j + 1) * 128],
                                                        (lambda c0, c1, k=k, opnd=opnd: opnd[:, k, c0:c1]), [wt])
                                                       for k in range(kch)])
                    if j == 1:
                        self.wrelease(wi, tok)
                    for gi, (c0, c1) in enumerate(cfg.groups):
                        t = self.A(gb[bi + j][:, c0:c1], self.bank_ap(banks[gi], c1 - c0), AF.Tanh, scale=0.5,
                                   deps=[tok, prev_done])
                        self.release_bank(banks[gi], [t])
            tg = P.cur("act")
            specs = [(self.w_pup[:, q * 256:(q + 1) * 256], 8, y_pool, 0),
                     (self.w_lup[:, q * 256:(q + 1) * 256], 16, y_lru, 2)]
            for (src, kch, opnd, bi) in specs:
                wsl, wt, wi = self.wload(src, kch)
                for j in range(2):
                    banks, tok = self.job(cfg.groups, [(wsl[:, k, j * 128:(j + 1) * 128],
                                                        (lambda c0, c1, k=k, opnd=opnd: opnd[:, k, c0:c1]), [wt])
                                                       for k in range(kch)])
                    if j == 1:
                        self.wrelease(wi, tok)
                    for gi, (c0, c1) in enumerate(cfg.groups):
                        t = self.Vstt(gb[bi + j][:, c0:c1], gb[bi + j][:, c0:c1], 1.0,
                                      self.bank_ap(banks[gi], c1 - c0), ALU.add, ALU.mult, deps=[tok, tg])
                        self.release_bank(banks[gi], [t])
            tv = P.cur("dve")
            for j in range(2):
                self.Vtt(merged[:, 2 * q + j, :], gb[j][:, 0:nt], gb[2 + j][:, 0:nt], ALU.add, deps=[tv])
            prev_done = [P.cur("dve")]

    def stage_proj_norm(self, cfg, opnd, wsrc, kchunks, acc, evac_scale):
        P = self.P
        nt = cfg.ntok
        nparts = kchunks // 16
        sqb = [self.sq_scratch[:, i * 336:i * 336 + nt // 2].bitcast(BF16) for i in range(2)]
        sq_free = [None, None]
        sbanks = [self.alloc_bank() for _ in cfg.groups]
        for b in sbanks:
            self.bank_reserved.add(b)
        pend = None
        stok = None
        nsq = 0

        def flush(pend, first, last):
            (tq, s) = pend
            tk = None
            for gi, (c0, c1) in enumerate(cfg.groups):
                d = [tq] + (self.bank_free[sbanks[gi]] if first else [])
                tk = self.MM(self.bank_ap(sbanks[gi], c1 - c0), self.ones[:], sqb[s][:, c0:c1], first, last,
                             deps=d, signal=True)
            sq_free[s] = tk
            return tk

        for cb in range(8):
            open_jobs = []
            for j in range(2):
                open_jobs.append([self.alloc_bank() for _ in cfg.groups])
            toks = [None, None]
            for kp in range(nparts):
                src = wsrc[kp * 2048:(kp + 1) * 2048, cb * 256:(cb + 1) * 256]
                wsl, wt, wi = self.wload(src)
                for j in range(2):
                    banks = open_jobs[j]
                    for k in range(DK):
                        first = (kp == 0 and k == 0)
                        last = (kp == nparts - 1 and k == DK - 1)
                        for gi, (c0, c1) in enumerate(cfg.groups):
                            d = [wt] + (self.bank_free[banks[gi]] if first else [])
                            sig = (k == DK - 1 and gi == len(cfg.groups) - 1)
                            t = self.MM(self.bank_ap(banks[gi], c1 - c0), wsl[:, k, j * 128:(j + 1) * 128],
                                        opnd[:, kp * 16 + k, c0:c1], first, last, deps=d, signal=sig)
                            if sig:
                                toks[j] = t
                self.wrelease(wi, toks[1])
            for j in range(2):
                o = 2 * cb + j
                banks = open_jobs[j]
                evs = []
                for gi, (c0, c1) in enumerate(cfg.groups):
                    te = self.A(acc[:, o, c0:c1], self.bank_ap(banks[gi], c1 - c0), AF.Copy, scale=evac_scale,
                                deps=[toks[j]])
                    self.release_bank(banks[gi], [te])
                    evs.append(te)
                s = nsq % 2
                tq = self.Vtt(sqb[s][:, 0:nt], acc[:, o, :], acc[:, o, :], ALU.mult, deps=evs + [sq_free[s]])
                if pend is not None:
                    stok = flush(pend, nsq == 1, False)
                pend = (tq, s)
                nsq += 1
        stok = flush(pend, False, True)
        for gi, (c0, c1) in enumerate(cfg.groups):
            ta = self.A(self.rstd[:, c0:c1], self.bank_ap(sbanks[gi], c1 - c0), AF.Sqrt, scale=1.0 / D, bias=EPS,
                        deps=[stok])
            P.op("dve", lambda e, c0=c0, c1=c1: e.reciprocal(out=self.rstd[:, c0:c1], in_=self.rstd[:, c0:c1]),
                 deps=[ta])
            self.release_bank(sbanks[gi], [ta])
            self.bank_reserved.discard(sbanks[gi])

    def stage_ffn_up(self, cfg, h2, f):
        P = self.P
        nt = cfg.ntok
        p0, p1 = cfg.prm
        Lp = p1 - p0
        cv = self.cvec
        W, cur, prm, scur = self.ext_layout(cfg, 2)
        def U_ap(slot, gv):
            o = slot * 2752 + gv * 704
            return self.R4[:, o:o + W]
        def C_ap(slot, gv):
            if slot == 1 and gv == 1:
                return self.sq_scratch[:, 0:nt]
            o = slot * 2752 + 1408 + gv * 672
            return self.R4[:, o:o + nt]
        st_fs = [self.rstd[:, 0:256], self.rstd[:, 256:512]]
        so_f = self.cT[:].rearrange("p k j -> p (k j)")[:, 0:256]
        stf_rd = [[], []]
        stf_tok = [None, None]

        def prefetch_hist(jf_):
            bi = jf_ % 2
            for gv_ in range(2):
                for r in range(2):
                    self.stf_cnt[bi] += 16
                    dst = st_fs[bi][r * 16:(r + 1) * 16, gv_ * 128:(gv_ + 1) * 128]
                    src = self.st_fconv[:, r, gv_ * DFF + jf_ * 128: gv_ * DFF + (jf_ + 1) * 128]
                    P.dma("sp", lambda e, dst=dst, src=src: e.dma_start(out=dst, in_=src), self.s_stf[bi],
                          self.stf_cnt[bi], deps=stf_rd[bi] + stage_start)
            stf_tok[bi] = (self.s_stf[bi], self.stf_cnt[bi])
            stf_rd[bi] = []
        stage_start = list(self.last_barrier)
        slot_free = [None, None]
        grp_so = []
        grp_rd = []
        pending_tail = None
        wrel = []

        def emit_tail(tl):
            (Cg, tg, Cv, tvv, jf_, slot_) = tl
            tgl = self.A(Cg[:, 0:nt], Cg[:, 0:nt], AF.Gelu_apprx_tanh, deps=tg)
            tf = self.Vtt(f[:, jf_, :], Cg[:, 0:nt], Cv[:, 0:nt], ALU.mult, deps=[tgl] + tvv)
            slot_free[slot_] = slot_free[slot_] + [tf]

        it = 0
        for q in range(24):
            if self.ada_pending and q % 6 == 0:
                self.ada_item(self.ada_pending.pop(0))
            wg_, wgt, wgi = self.wload(self.w_fup[:, q * 256:(q + 1) * 256])
            wv_, wvt, wvi = self.wload(self.w_fup[:, DFF + q * 256:DFF + (q + 1) * 256])
            lastpe = None
            for j in range(2):
                jf = 2 * q + j
                slot = it % 2
                it += 1
                if cfg.samp is not None:
                    if jf == 0:
                        prefetch_hist(0)
                    if jf + 1 < 48:
                        prefetch_hist(jf + 1)
                sfree = slot_free[slot]
                jobs = []
                for gv, (wsl, wt) in enumerate(((wg_, wgt), (wv_, wvt))):
                    banks, tok = self.job(cfg.groups, [(wsl[:, k, j * 128:(j + 1) * 128],
                                                        (lambda c0, c1, k=k: h2[:, k, c0:c1]), [wt])
                                                       for k in range(DK)])
                    jobs.append((banks, tok))
                    lastpe = tok
                hist_ps = []
                if cfg.samp is not None:
                    st_f = st_fs[jf % 2]
                    b = self.alloc_bank()
                    for gv in range(2):
                        tt = self.TR(self.ps[:, b, gv * 32:(gv + 1) * 32], st_f[0:32, gv * 128:(gv + 1) * 128],
                                     self.ident[0:32, 0:32],
                                     deps=[stf_tok[jf % 2]] + (self.bank_free[b] if gv == 0 else []), signal=True)
                        stf_rd[jf % 2].append(tt)
                        hist_ps.append((b, tt))
                        lastpe = tt
                evs_all = []
                for gv in range(2):
                    U = U_ap(slot, gv)
                    banks, tok = jobs[gv]
                    evs = []
                    for gi, (c0, c1) in enumerate(cfg.groups):
                        te = self.A(U[:, cur + c0:cur + c1], self.bank_ap(banks[gi], c1 - c0), AF.Copy,
                                    deps=[tok, sfree])
                        self.release_bank(banks[gi], [te])
                        evs.append(te)
                    evs_all.append(evs)
                for gv in range(2):
                    U = U_ap(slot, gv)
                    chn = jf + gv * 48
                    if cfg.samp is not None:
                        b, tt = hist_ps[gv]
                        te = self.Vcopy(U[:, 0:32], self.ps[:, b, gv * 32:(gv + 1) * 32],
                                        deps=[hist_ps[0][1], hist_ps[1][1], sfree])
                        if gv == 0:
                            hist_rel = [te]
                        else:
                            self.release_bank(b, hist_rel + [te])
                    else:
                        te = self.Vcopy(U[:, 0:2], self.hist_up[:, chn, :], deps=[sfree, self.t_init])
                    evs_all[gv].append(te)
                regs = []
                if cfg.samp is not None:
                    regs.append((32, 128, 16, 0))
                    regs.append((prm, Lp, 1, p0))
                else:
                    regs.append((cur, Lp, 1, p0))
                c0t = [[], []]
                for gv in range(2):
                    U = U_ap(slot, gv)
                    C = C_ap(slot, gv)
                    chn = jf + gv * 48
                    for (co, n, strd, xo) in regs:
                        c0t[gv].append(self.A(C[:, xo:xo + n], U[:, co:co + n], AF.Identity,
                                              scale=cv[:, CV_WFC + 192 + chn:CV_WFC + 192 + chn + 1],
                                              bias=cv[:, CV_BFC + chn:CV_BFC + chn + 1], deps=evs_all[gv] + [sfree]))
                ctoks = [[], []]
                for gv in range(2):
                    U = U_ap(slot, gv)
                    C = C_ap(slot, gv)
                    chn = jf + gv * 48
                    for ri, (co, n, strd, xo) in enumerate(regs):
                        t = c0t[gv][ri]
                        for k in range(2):
                            sh = (2 - k) * strd
                            t = self.Vstt(C[:, xo:xo + n], U[:, co - sh:co - sh + n],
                                          cv[:, CV_WFC + 96 * k + chn:CV_WFC + 96 * k + chn + 1], C[:, xo:xo + n],
                                          ALU.mult, ALU.add, deps=[t] + evs_all[gv])
                        ctoks[gv].append(t)
                ureaders = []
                for gv in range(2):
                    U = U_ap(slot, gv)
                    chn = jf + gv * 48
                    tsv = self.A(self.hist_up[:, chn, :], U[:, W - 2:W], AF.Copy, deps=evs_all[gv])
                    ureaders.append(tsv)
                if cfg.samp is not None:
                    bso = self.alloc_bank()
                    tts = []
                    for gv in range(2):
                        U = U_ap(slot, gv)
                        tt = self.TR(self.ps[0:32, bso, gv * 128:(gv + 1) * 128], U[:, 128:160], self.ident[:],
                                     deps=evs_all[gv] + (self.bank_free[bso] if gv == 0 else []), signal=True)
                        tts.append(tt)
                        ureaders.append(tt)
                        lastpe = tt
                    te = self.A(so_f[0:32, 0:256], self.ps[0:32, bso, 0:256], AF.Copy, deps=tts + grp_so)
                    self.release_bank(bso, [te])
                slot_free[slot] = ureaders + ctoks[0] + ctoks[1]
                if cfg.samp is not None:
                    for gv in range(2):
                        for r in range(2):
                            self.so_dma(self.o_fconv_s[:, r, gv * DFF + jf * 128: gv * DFF + (jf + 1) * 128],
                                        so_f[r * 16:(r + 1) * 16, gv * 128:(gv + 1) * 128],
                                        deps=[P.cur("act")], stream="ffn")
                    grp_so = [self.so_tok("ffn")]
                if pending_tail is not None:
                    emit_tail(pending_tail)
                pending_tail = (C_ap(slot, 0), ctoks[0], C_ap(slot, 1), ctoks[1], jf, slot)
            self.wrelease(wgi, lastpe)
            self.wrelease(wvi, lastpe)
        emit_tail(pending_tail)
        if cfg.samp is not None:
            self.out_tokens += self.so_all()

    def stage_store_y(self, cfg):
        P = self.P
        xT = self.xT(cfg)
        ost = [self.R2[:, 0:2048], self.R2[:, 2048:4096]]
        for ti, (col0, yrow0, nr) in enumerate(cfg.out_tiles):
            s = ti % 2
            prev = (self.s_o[s], self.o_cnt[s]) if self.o_cnt[s] else None
            evs = []
            for g4 in range(4):
                b = self.alloc_bank()
                tt = None
                for qq in range(4):
                    k = g4 * 4 + qq
                    d = (self.bank_free[b] if qq == 0 else [])
                    tt = self.TR(self.ps[0:nr, b, qq * 128:(qq + 1) * 128], xT[:, k, col0:col0 + nr], self.ident[:],
                                 deps=d, signal=(qq == 3))
                dst = ost[s][0:nr, g4 * 512:(g4 + 1) * 512]
                if g4 % 2 == 0:
                    te = self.A(dst, self.ps[0:nr, b, :], AF.Copy, deps=[tt, prev])
                else:
                    te = self.Vcopy(dst, self.ps[0:nr, b, :], deps=[tt, prev])
                self.release_bank(b, [te])
                evs.append(te)
            self.o_cnt[s] += 16
            src = ost[s]
            dsty = self.y[yrow0:yrow0 + nr, :]
            src = ost[s][0:nr, :]
            t = P.dma("sp", lambda e, dsty=dsty, src=src: e.dma_start(out=dsty, in_=src), self.s_o[s],
                      self.o_cnt[s], deps=evs)
            self.out_tokens.append(t)

    def epilogue(self):
        P = self.P
        self.barrier()
        stg = self.R2[:, 0:12288]
        jobs = [(self.hist_pool, 8, 15, self.o_pool_p), (self.hist_lru, 16, 3, self.o_lconv_p),
                (self.hist_up, 96, 2, self.o_fconv_p)]
        prev = None
        for (src, nch, ncol, dst) in jobs:
            evs = []
            for c0 in range(0, nch, 4):
                b = self.alloc_bank()
                tt = None
                for qq in range(4):
                    d = (self.bank_free[b] if qq == 0 else [])
                    tt = self.TR(self.ps[0:ncol, b, qq * 128:(qq + 1) * 128], src[:, c0 + qq, :], self.ident[:],
                                 deps=d, signal=(qq == 3))
                te = self.A(stg[0:ncol, c0 * 128:(c0 + 4) * 128], self.ps[0:ncol, b, :], AF.Copy, deps=[tt, prev])
                self.release_bank(b, [te])
                evs.append(te)
            t = self.so_dma(dst[:, :], stg[0:ncol, 0:nch * 128], deps=evs, stream="epi")
            prev = self.so_tok("epi")
        evs = []
        for c0 in range(0, 16, 4):
            b = self.alloc_bank()
            tt = None
            for qq in range(4):
                d = (self.bank_free[b] if qq == 0 else [])
                tt = self.TR(self.ps[0:1, b, qq * 128:(qq + 1) * 128], self.h_carry[:, c0 + qq:c0 + qq + 1],
                             self.ident[:], deps=d, signal=(qq == 3))
            te = self.A(stg[0:1, c0 * 128:(c0 + 4) * 128], self.ps[0:1, b, :], AF.Copy, deps=[tt, prev])
            self.release_bank(b, [te])
            evs.append(te)
        self.so_dma(self.o_lh_p[:, :], stg[0:1, 0:2048], deps=evs, stream="epi")
        final = self.so_all() + [(self.s_o[s], self.o_cnt[s]) for s in range(2)]
        P.wait_only("sp", final + self.out_tokens)


_NC_CACHE = {}


def _get_nc():
    if "nc" not in _NC_CACHE:
        b = Builder()
        _NC_CACHE["nc"] = b
    return _NC_CACHE["nc"]


def _pack_vec(v):
    v = np.asarray(v, np.float32).reshape(-1)
    return np.ascontiguousarray(v.reshape(-1, 128).T)


def kernel(x_prompt, x_sample, c_prompt, c_sample, state_pool, state_lru_conv, state_lru_h, state_ffn_conv,
           w_ada, b_ada, g_pre1, g_post1, g_pre2, g_post2, w_in, w_pool_grp, pool_scale,
           w_lru_conv, b_lru_conv, w_rg, b_rg, w_ig, b_ig, lru_lambda,
           w_pool_up, w_lru_up, w_out, w_ffn_up, w_ffn_conv, b_ffn_conv, w_ffn_down):
    f32 = np.float32
    A = lambda a: np.ascontiguousarray(np.asarray(a, f32))
    x_prompt, x_sample = A(x_prompt), A(x_sample)
    c_prompt, c_sample = A(c_prompt), A(c_sample)
    cvec = np.zeros((128, NV), f32)
    cvec[:, CV_GPRE1:CV_GPRE1 + 16] = _pack_vec(g_pre1[0])
    cvec[:, CV_GPOST1:CV_GPOST1 + 16] = _pack_vec(g_post1[0])
    cvec[:, CV_GPRE2:CV_GPRE2 + 16] = _pack_vec(g_pre2[0])
    cvec[:, CV_GPOST2:CV_GPOST2 + 16] = _pack_vec(g_post2[0])
    cvec[:, CV_PSCALE:CV_PSCALE + 8] = _pack_vec(pool_scale[0])
    for k in range(4):
        cvec[:, CV_WLC + 16 * k:CV_WLC + 16 * (k + 1)] = _pack_vec(np.asarray(w_lru_conv)[0, k])
    cvec[:, CV_BLC:CV_BLC + 16] = _pack_vec(b_lru_conv[0])
    cvec[:, CV_BRG:CV_BRG + 16] = _pack_vec(b_rg[0])
    cvec[:, CV_BIG:CV_BIG + 16] = _pack_vec(b_ig[0])
    cvec[:, CV_LAM:CV_LAM + 16] = _pack_vec(lru_lambda[0])
    for k in range(3):
        cvec[:, CV_WFC + 96 * k:CV_WFC + 96 * (k + 1)] = _pack_vec(np.asarray(w_ffn_conv)[0, k])
    cvec[:, CV_BFC:CV_BFC + 96] = _pack_vec(b_ffn_conv[0])
    cvec[:, CV_BADA:CV_BADA + 96] = _pack_vec(b_ada[0])
    ident = np.eye(128, dtype=f32)
    weights = {
        "w_ada": A(w_ada)[0], "w_in": A(w_in)[0], "w_pool_grp": A(w_pool_grp)[0], "w_rg": A(w_rg)[0],
        "w_ig": A(w_ig)[0], "w_pool_up": A(w_pool_up)[0], "w_lru_up": A(w_lru_up)[0], "w_out": A(w_out)[0],
        "w_ffn_up": A(w_ffn_up)[0], "w_ffn_down": A(w_ffn_down)[0],
    }
    state_pool, state_lru_conv = A(state_pool)[0], A(state_lru_conv)[0]
    state_lru_h, state_ffn_conv = A(state_lru_h)[0], A(state_ffn_conv)[0]
    in_maps = []
    for c in range(NCORES):
        b, hf = c // 2, c % 2
        xq = np.zeros((2176, D), f32)
        if hf == 1:
            xq[0:1024] = x_prompt[b, 0:1024]
        xq[1024:2048] = x_prompt[b, hf * 1024:(hf + 1) * 1024]
        xs = x_sample[16 * c:16 * (c + 1)]
        xq[2048:2176] = xs.transpose(1, 0, 2).reshape(128, D)
        cc = np.concatenate([c_prompt[b:b + 1], c_sample[16 * c:16 * (c + 1)]], axis=0)
        cT = np.ascontiguousarray(cc.reshape(17, 16, 128).transpose(2, 1, 0))
        sel = np.full((128, 1), float(hf), f32)
        invc = np.zeros((128, 4, 16), f32)
        for g in range(4):
            w = 2 ** (g + 1)
            for j in range(16):
                cnt = w if hf == 1 else min(w, j + 1)
                invc[:, g, j] = 1.0 / cnt
        m = {"xq": xq, "cT": cT, "cvec": cvec, "sel": sel, "invc": invc, "ident": ident,
             "st_pool": np.ascontiguousarray(state_pool[16 * c:16 * (c + 1)]),
             "st_lconv": np.ascontiguousarray(state_lru_conv[16 * c:16 * (c + 1)]),
             "st_lh": np.ascontiguousarray(state_lru_h[16 * c:16 * (c + 1)]),
             "st_fconv": np.ascontiguousarray(state_ffn_conv[16 * c:16 * (c + 1)])}
        m.update(weights)
        in_maps.append(m)
    nc = build_nc()
    res = run_bass_kernel_spmd(nc, in_maps, core_ids=list(range(NCORES)))
    R = res.results
    y_p = np.zeros((4, 2048, D), f32)
    y_s = np.zeros((128, 8, D), f32)
    pool_p = np.zeros((1, 4, 15, PW), f32)
    lconv_p = np.zeros((1, 4, 3, D), f32)
    lh_p = np.zeros((1, 4, D), f32)
    fconv_p = np.zeros((1, 4, 2, 2 * DFF), f32)
    pool_s = np.zeros((1, 128, 15, PW), f32)
    lconv_s = np.zeros((1, 128, 3, D), f32)
    lh_s = np.zeros((1, 128, D), f32)
    fconv_s = np.zeros((1, 128, 2, 2 * DFF), f32)
    for c in range(NCORES):
        b, hf = c // 2, c % 2
        r = R[c]
        y_p[b, hf * 1024:(hf + 1) * 1024] = r["y"][0:1024]
        y_s[16 * c:16 * (c + 1)] = r["y"][1024:1152].reshape(8, 16, D).transpose(1, 0, 2)
        if hf == 1:
            pool_p[0, b] = r["o_pool_p"]
            lconv_p[0, b] = r["o_lconv_p"]
            lh_p[0, b] = r["o_lh_p"][0]
            fconv_p[0, b] = r["o_fconv_p"]
        pool_s[0, 16 * c:16 * (c + 1)] = r["o_pool_s"]
        lconv_s[0, 16 * c:16 * (c + 1)] = r["o_lconv_s"]
        lh_s[0, 16 * c:16 * (c + 1)] = r["o_lh_s"]
        fconv_s[0, 16 * c:16 * (c + 1)] = r["o_fconv_s"]
    return (y_p, y_s, pool_p, lconv_p, lh_p, fconv_p, pool_s, lconv_s, lh_s, fconv_s)


def build_nc():
    if "built" not in _NC_CACHE:
        b = Builder()
        _NC_CACHE["built"] = b.build()
    return _NC_CACHE["built"]
```

```python
import contextlib
import numpy as np
import concourse.bass as bass
import concourse.mybir as mybir
from concourse.bass_utils import run_bass_kernel_spmd

F32 = mybir.dt.float32
BF16 = mybir.dt.bfloat16
AF = mybir.ActivationFunctionType
ALU = mybir.AluOpType

D = 2048
DK = 16
PW = 1024
DFF = 6144
EPS = 1e-6
NCORES = 8
HALO = 32
NPRE = 992
ENGS = ("pe", "act", "dve", "pool", "sp")

CV_GPRE1, CV_GPOST1, CV_GPRE2, CV_GPOST2 = 0, 16, 32, 48
CV_PSCALE = 64
CV_WLC = 72
CV_BLC = 136
CV_BRG = 152
CV_BIG = 168
CV_LAM = 184
CV_WFC = 200
CV_BFC = 488
CV_BADA = 584
NV = 680


class Prog:
    def __init__(self, nc, stack):
        self.nc = nc
        self.stack = stack
        self.q = {e: [] for e in ENGS}
        self.sem = {e: stack.enter_context(nc.semaphore("prog_" + e)) for e in ENGS}
        self.cnt = {e: 0 for e in ENGS}
        self.waited = {e: {} for e in ENGS}

    def new_sem(self, name):
        return self.stack.enter_context(self.nc.semaphore(name))

    def _waits(self, eng, deps):
        out = []
        for d in deps:
            if d is None:
                continue
            if isinstance(d, list):
                out.extend(self._waits(eng, d))
                continue
            s, v = d
            key = id(s)
            prev = self.waited[eng].get(key, 0)
            if v > prev:
                self.waited[eng][key] = v
                out.append((s, v))
        return out

    def op(self, eng, fn, deps=(), signal=True):
        w = self._waits(eng, deps)
        tok = None
        if signal:
            self.cnt[eng] += 1
            tok = (self.sem[eng], self.cnt[eng])
        self.q[eng].append((fn, w, tok))
        return tok

    def dma(self, eng, fn, sem, val, deps=()):
        w = self._waits(eng, deps)
        self.q[eng].append((fn, w, ("dma", sem)))
        return (sem, val)

    def cur(self, eng):
        if self.cnt[eng] == 0:
            return None
        return (self.sem[eng], self.cnt[eng])

    def wait_only(self, eng, deps):
        w = self._waits(eng, deps)
        if w:
            self.q[eng].append((None, w, None))

    def replay(self, block):
        def run(name):
            def body(e):
                for fn, waits, tok in self.q[name]:
                    for (s, v) in waits:
                        e.wait_ge(s, v)
                    if fn is None:
                        continue
                    ins = fn(e)
                    if tok is not None:
                        if tok[0] == "dma":
                            ins.then_inc(tok[1], 16)
                        else:
                            ins.then_inc(tok[0], 1)
            return body
        block.tensor(run("pe"))
        block.scalar(run("act"))
        block.vector(run("dve"))
        block.gpsimd(run("pool"))
        block.sync(run("sp"))


class PassCfg:
    def __init__(self, name, ntok, groups, samp, prm, halo, xtiles, lru_only, out_tiles):
        self.name = name
        self.ntok = ntok
        self.groups = groups
        self.samp = samp
        self.prm = prm
        self.halo = halo
        self.xtiles = xtiles
        self.lru_only = lru_only
        self.out_tiles = out_tiles


PASSES = [
    PassCfg("p0", 992, [(0, 496), (496, 992)], None, (0, 992), 0,
            [(0, 128, 0), (128, 128, 128), (256, 128, 256), (384, 128, 384), (512, 128, 512), (640, 128, 640),
             (768, 128, 768), (896, 96, 896)], True, []),
    PassCfg("p1", 608, [(0, 160), (160, 608)], (0, 128), (128, 608), HALO,
            [(2048, 128, 0), (992, 32, 128), (1024, 128, 160), (1152, 128, 288), (1280, 128, 416), (1408, 64, 544)],
            False, [(0, 1024, 128), (160, 0, 128), (288, 128, 128), (416, 256, 128), (544, 384, 64)]),
    PassCfg("p2", 576, [(0, 512), (512, 576)], None, (0, 576), 0,
            [(1472, 128, 0), (1600, 128, 128), (1728, 128, 256), (1856, 128, 384), (1984, 64, 512)],
            False, [(0, 448, 128), (128, 576, 128), (256, 704, 128), (384, 832, 128), (512, 960, 64)]),
]
NTMAX = 672


class Builder:
    def __init__(self, debug=False):
        self.debug = debug
        nc = bass.Bass("TRN2", target_bir_lowering=False)
        self.nc = nc
        di = lambda n, s: nc.dram_tensor(n, s, F32, kind="ExternalInput").ap()
        do = lambda n, s: nc.dram_tensor(n, s, F32, kind="ExternalOutput").ap()
        self.xq = di("xq", [2176, D])
        self.cT_d = di("cT", [128, DK, 17])
        self.cvec_d = di("cvec", [128, NV])
        self.sel_d = di("sel", [128, 1])
        self.invc_d = di("invc", [128, 4, 16])
        self.ident_d = di("ident", [128, 128])
        self.st_pool = di("st_pool", [16, 15, PW])
        self.st_lconv = di("st_lconv", [16, 3, D])
        self.st_lh = di("st_lh", [16, D])
        self.st_fconv = di("st_fconv", [16, 2, 2 * DFF])
        self.w_ada = di("w_ada", [D, 6 * D])
        self.w_in = di("w_in", [D, 7168])
        self.w_grp = di("w_pool_grp", [4, 256, 256])
        self.w_rg = di("w_rg", [8, 256, 256])
        self.w_ig = di("w_ig", [8, 256, 256])
        self.w_pup = di("w_pool_up", [PW, D])
        self.w_lup = di("w_lru_up", [D, D])
        self.w_out = di("w_out", [D, D])
        self.w_fup = di("w_ffn_up", [D, 2 * DFF])
        self.w_fdn = di("w_ffn_down", [DFF, D])
        self.y = do("y", [1152, D])
        self.o_pool_p = do("o_pool_p", [15, PW])
        self.o_lconv_p = do("o_lconv_p", [3, D])
        self.o_lh_p = do("o_lh_p", [1, D])
        self.o_fconv_p = do("o_fconv_p", [2, 2 * DFF])
        self.o_pool_s = do("o_pool_s", [16, 15, PW])
        self.o_lconv_s = do("o_lconv_s", [16, 3, D])
        self.o_lh_s = do("o_lh_s", [16, D])
        self.o_fconv_s = do("o_fconv_s", [16, 2, 2 * DFF])

    def sb(self, name, shape, dt):
        return self.st.enter_context(self.nc.sbuf_tensor("sb_" + name, shape, dt))

    def A(self, out, in_, func, scale=None, bias=None, deps=()):
        kw = {}
        if scale is not None:
            kw["scale"] = scale
        if bias is not None:
            kw["bias"] = bias
        return self.P.op("act", lambda e: e.activation(out=out, in_=in_, func=func, **kw), deps=deps)

    def Vtt(self, out, in0, in1, op, deps=(), eng="dve"):
        return self.P.op(eng, lambda e: e.tensor_tensor(out=out, in0=in0, in1=in1, op=op), deps=deps)

    def Vstt(self, out, in0, scalar, in1, op0, op1, deps=()):
        return self.P.op("dve", lambda e: e.scalar_tensor_tensor(out=out, in0=in0, scalar=scalar, in1=in1,
                                                                  op0=op0, op1=op1), deps=deps)

    def Vts(self, out, in0, s1, s2, op0, op1=None, deps=()):
        if op1 is None:
            return self.P.op("dve", lambda e: e.tensor_scalar(out=out, in0=in0, scalar1=s1, scalar2=None, op0=op0),
                             deps=deps)
        return self.P.op("dve", lambda e: e.tensor_scalar(out=out, in0=in0, scalar1=s1, scalar2=s2, op0=op0, op1=op1),
                         deps=deps)

    def Vcopy(self, out, in_, deps=()):
        return self.P.op("dve", lambda e: e.tensor_copy(out=out, in_=in_), deps=deps)

    def MM(self, out, lhsT, rhs, start, stop, deps=(), signal=False):
        return self.P.op("pe", lambda e: e.matmul(out, lhsT=lhsT, rhs=rhs, start=start, stop=stop),
                         deps=deps, signal=signal)

    def TR(self, out, in_, ident, deps=(), signal=False):
        return self.P.op("pe", lambda e: e.transpose(out, in_, ident), deps=deps, signal=signal)

    def barrier(self, extra=()):
        P = self.P
        toks = [P.cur(e) for e in ("pe", "act", "dve", "pool")] + list(extra) + self.so_all()
        for e in ("pe", "act", "dve"):
            P.wait_only(e, toks)
        self.last_barrier = toks
        return toks

    def alloc_bank(self):
        for _ in range(8):
            b = self.bank_next
            self.bank_next = (self.bank_next + 1) % 8
            if b not in self.bank_reserved:
                if b in self.bank_busy:
                    raise RuntimeError("PSUM bank %d re-allocated before release" % b)
                self.bank_busy.add(b)
                return b
        raise RuntimeError("no bank")

    def bank_ap(self, b, n, p=128):
        return self.ps[0:p, b, 0:n]

    def release_bank(self, b, toks):
        self.bank_free[b] = list(toks)
        self.bank_busy.discard(b)

    def wload(self, src, kch=16, ncol=256):
        i = self.w_next
        self.w_next = (i + 1) % len(self.wslots)
        slot = self.wslots[i]
        self.w_cnt[i] += 16
        dst = slot[:, 0:kch, 0:ncol]
        srcv = src.rearrange("(k p) n -> p k n", p=128)
        tok = self.P.dma("pool", lambda e: e.dma_start(out=dst, in_=srcv), self.w_sem[i], self.w_cnt[i],
                         deps=[self.w_rel[i]])
        return dst, tok, i

    def wrelease(self, i, tok):
        self.w_rel[i] = tok

    def job(self, groups, parts):
        banks = [self.alloc_bank() for _ in groups]
        n = len(parts)
        tok = None
        for idx, (lhsT, rhs_fn, deps) in enumerate(parts):
            for gi, (c0, c1) in enumerate(groups):
                b = banks[gi]
                d = list(deps)
                if idx == 0:
                    d += self.bank_free[b]
                last = (idx == n - 1 and gi == len(groups) - 1)
                t = self.MM(self.bank_ap(b, c1 - c0), lhsT, rhs_fn(c0, c1), idx == 0, idx == n - 1, deps=d,
                            signal=last)
                if last:
                    tok = t
        return banks, tok

    def build(self):
        nc = self.nc
        with contextlib.ExitStack() as st:
            self.st = st
            self.P = P = Prog(nc, st)
            self.ident = self.sb("ident", [128, 128], F32)
            self.ones = self.sb("ones", [128, 128], BF16)
            self.cvec = self.sb("cvec", [128, NV], F32)
            self.dv = self.sb("dv", [128, 4, 16], F32)
            self.mod = self.sb("mod", [128, 6, 16, 17], F32)
            self.sel = self.sb("sel", [128, 1], F32)
            self.invc = self.sb("invc", [128, 4, 16], F32)
            self.cT = self.sb("cT", [128, DK, 17], F32)
            self.sl = self.sb("sl", [128, DK, 17], BF16)
            self.wgrp = self.sb("wgrp", [128, 4, 2, 256], BF16)
            self.hist_pool = self.sb("hist_pool", [128, 8, 15], F32)
            self.hist_lru = self.sb("hist_lru", [128, 16, 3], F32)
            self.h_carry = self.sb("h_carry", [128, 16], F32)
            self.hist_up = self.sb("hist_up", [128, 96, 2], F32)
            self.rstd = self.sb("rstd", [128, NTMAX], F32)
            self.sq_scratch = self.sb("sqs", [128, NTMAX], F32)
            self.ada_tm_buf = self.sb("ada_tm", [128, 512], F32)
            self.R1 = self.sb("R1", [128, 16 * NTMAX], F32)
            self.R2 = self.sb("R2", [128, 16128], F32)
            self.R4 = self.sb("R4", [128, 16 * NTMAX], F32)
            NW = 4
            self.wslots = [self.sb("w%d" % i, [128, 16, 256], BF16) for i in range(NW)]
            self.w_sem = [P.new_sem("wsem%d" % i) for i in range(NW)]
            self.w_cnt = [0] * NW
            self.w_rel = [None] * NW
            self.w_next = 0
            self.wsm = [self.sb("wsm%d" % i, [128, 2, 2, 256], BF16) for i in range(2)]
            self.wsm_sem = [P.new_sem("wsmsem%d" % i) for i in range(2)]
            self.wsm_cnt = [0, 0]
            self.wsm_rel = [None, None]
            self.wsm_next = 0
            self.ps = st.enter_context(nc.psum_tensor("ps_all", [128, 8, 512], F32))
            self.bank_next = 0
            self.bank_reserved = set()
            self.bank_busy = set()
            self.bank_free = [[] for _ in range(8)]
            self.s_misc = P.new_sem("misc")
            self.misc_cnt = 0
            self.s_x = [P.new_sem("xs0"), P.new_sem("xs1"), P.new_sem("xs2")]
            self.x_cnt = [0, 0, 0]
            self.s_o = [P.new_sem("os0"), P.new_sem("os1")]
            self.o_cnt = [0, 0]
            self.s_so = P.new_sem("so")
            self.s_stf = [P.new_sem("stf0"), P.new_sem("stf1")]
            self.stf_cnt = [0, 0]
            self.so_cnt = 0
            self.so_streams = {}
            self.out_tokens = []
            self.last_barrier = []

            self.prologue()
            for cfg in PASSES:
                self.run_pass(cfg)
            self.epilogue()
            with nc.Block() as block:
                P.replay(block)
        return nc

    def misc_dma(self, eng, out, in_, deps=()):
        self.misc_cnt += 16
        return self.P.dma(eng, lambda e: e.dma_start(out=out, in_=in_), self.s_misc, self.misc_cnt, deps=deps)

    def so_tok(self, stream):
        st_ = self.so_streams.get(stream)
        return (st_[0], st_[1]) if st_ else None

    def so_all(self):
        return [(v[0], v[1]) for v in self.so_streams.values()]

    def so_dma(self, out, in_, deps=(), stream="main"):
        if stream not in self.so_streams:
            self.so_streams[stream] = [self.P.new_sem("so_" + stream), 0]
        st_ = self.so_streams[stream]
        st_[1] += 16
        self.so_cnt += 16
        t = self.P.dma("sp", lambda e: e.dma_start(out=out, in_=in_), st_[0], st_[1], deps=deps)
        return t

    def prologue(self):
        P = self.P
        cv = self.cvec
        lds = []
        lds.append(self.misc_dma("sp", self.ident[:], self.ident_d))
        lds.append(self.misc_dma("sp", self.cvec[:], self.cvec_d))
        lds.append(self.misc_dma("sp", self.sel[:], self.sel_d))
        lds.append(self.misc_dma("sp", self.invc[:], self.invc_d))
        lds.append(self.misc_dma("sp", self.cT[:], self.cT_d))
        ld = lds[-1]
        ld = (self.s_misc, self.misc_cnt)
        s_wg = P.new_sem("wgsem")
        wgv = self.w_grp.rearrange("g (k p) n -> p g k n", p=128)
        self.t_wgrp = P.dma("pool", lambda e: e.dma_start(out=self.wgrp[:], in_=wgv), s_wg, 16)
        t0 = P.op("dve", lambda e: e.memset(self.ones[:], 1.0))
        P.op("dve", lambda e: e.memset(self.hist_pool[:], 0.0))
        P.op("dve", lambda e: e.memset(self.hist_lru[:], 0.0))
        P.op("dve", lambda e: e.memset(self.h_carry[:], 0.0))
        self.t_init = P.op("dve", lambda e: e.memset(self.hist_up[:], 0.0))
        self.Vts(self.dv[:, 0, :], cv[:, CV_BRG:CV_BRG + 16], 0.5, None, ALU.mult, deps=[ld])
        self.Vts(self.dv[:, 1, :], cv[:, CV_BIG:CV_BIG + 16], 0.5, None, ALU.mult)
        ta = self.A(self.dv[:, 2, :], cv[:, CV_LAM:CV_LAM + 16], AF.Exp, scale=-1.0, deps=[ld])
        ta = self.A(self.dv[:, 2, :], self.dv[:, 2, :], AF.Ln, bias=1.0, deps=[ta])
        tv = self.Vts(self.dv[:, 2, :], self.dv[:, 2, :], -8.0, None, ALU.mult, deps=[ta])
        self.t_dv = self.Vts(self.dv[:, 3, :], self.dv[:, 2, :], 0.5, None, ALU.mult, deps=[tv])
        th = self.R4[:, 0:DK * 17].rearrange("p (k j) -> p k j", k=DK)
        ta = self.A(th, self.cT[:], AF.Tanh, scale=0.5, deps=[ld])
        t_sl = self.Vstt(self.sl[:], th, 1.0, self.cT[:], ALU.add, ALU.mult, deps=[ta])
        self.t_sl = t_sl
        self.t_ld = ld
        self.ada_tm_free = None
        self.ada_last = None
        self.ada_pending = list(range(8, 24))
        self.ada_mid_done = False
        for cb in range(8):
            self.ada_item(cb)
        self.ada_finalize([0, 1])
        self.barrier([ld, self.t_wgrp])


    def ada_item(self, cb):
        ada_tm = self.ada_tm_buf
        modf = self.mod[:].rearrange("p m k j -> p (m k) j")
        b = self.alloc_bank()
        tok = None
        for half in range(2):
            src = self.w_ada[:, cb * 512 + half * 256: cb * 512 + (half + 1) * 256]
            w, wt, wi = self.wload(src)
            for k in range(DK):
                d = [wt, self.t_sl] + (self.bank_free[b] if (k == 0 and half == 0) else [])
                tok = self.MM(self.ps[0:17, b, half * 256:(half + 1) * 256], self.sl[:, k, :], w[:, k, :],
                              k == 0, k == DK - 1, deps=d, signal=(k == DK - 1))
            self.wrelease(wi, tok)
        te = self.A(ada_tm[0:17, :], self.ps[0:17, b, :], AF.Copy, scale=0.5, deps=[tok, self.ada_tm_free])
        self.release_bank(b, [te])
        b2 = self.alloc_bank()
        tt = None
        for qq in range(4):
            d = [te, self.t_ld] + (self.bank_free[b2] if qq == 0 else [])
            tt = self.TR(self.ps[:, b2, qq * 17:(qq + 1) * 17], ada_tm[0:17, qq * 128:(qq + 1) * 128],
                         self.ident[0:17, 0:17], deps=d, signal=(qq == 3))
        self.ada_tm_free = tt
        te2 = self.Vcopy(modf[:, cb * 4:(cb + 1) * 4, :],
                         self.ps[:, b2, 0:68].rearrange("p (q j) -> p q j", q=4), deps=[tt])
        self.release_bank(b2, [te2])
        self.ada_last = te2

    def ada_finalize(self, ms):
        cv = self.cvec
        t = self.ada_last
        for m in ms:
            bada = cv[:, CV_BADA + 16 * m:CV_BADA + 16 * (m + 1)].unsqueeze(2).broadcast_to([128, 16, 17])
            t = self.Vtt(self.mod[:, m], self.mod[:, m], bada, ALU.add, deps=[t, self.t_ld])
            goff = {1: CV_GPRE1, 2: CV_GPOST1, 4: CV_GPRE2, 5: CV_GPOST2}.get(m)
            if goff is not None:
                gbc = cv[:, goff:goff + 16].unsqueeze(2).broadcast_to([128, 16, 17])
                if m in (1, 4):
                    t = self.Vstt(self.mod[:, m], self.mod[:, m], 1.0, gbc, ALU.add, ALU.mult, deps=[t])
                else:
                    t = self.Vtt(self.mod[:, m], self.mod[:, m], gbc, ALU.mult, deps=[t])
        self.t_mod = t

    def ada_drain(self):
        while self.ada_pending:
            self.ada_item(self.ada_pending.pop(0))
        self.ada_finalize([2, 3, 4, 5])

    def xT(self, cfg):
        if cfg.lru_only:
            return None
        return self.R1[:, 0:16 * cfg.ntok].rearrange("p (k n) -> p k n", k=16)

    def bfview(self, region, off_bytes, nch, ntok):
        o = off_bytes // 4
        n32 = nch * ntok // 2
        return region[:, o:o + n32].bitcast(BF16).rearrange("p (k n) -> p k n", k=nch)

    def f32view(self, region, off_bytes, nch, ntok):
        o = off_bytes // 4
        return region[:, o:o + nch * ntok].rearrange("p (k n) -> p k n", k=nch)

    def stage_load_x(self, cfg):
        P = self.P
        xT = self.xT(cfg)
        stg = [self.R2[:, 8064:8064 + 2048], self.R2[:, 8064 + 2048:8064 + 4096]]
        stg_free = [None, None]
        last = []
        for ti, (row0, nrows, col0) in enumerate(cfg.xtiles):
            s = ti % 2
            self.x_cnt[s] += 16
            dst = stg[s][0:nrows, :]
            src = self.xq[row0:row0 + nrows, :]
            tl = P.dma("sp", lambda e, dst=dst, src=src: e.dma_start(out=dst, in_=src), self.s_x[s], self.x_cnt[s],
                       deps=[stg_free[s]] + self.last_barrier)
            evs = []
            trs = None
            for g4 in range(4):
                b = self.alloc_bank()
                for qq in range(4):
                    k = g4 * 4 + qq
                    d = [tl] + (self.bank_free[b] if qq == 0 else [])
                    trs = self.TR(self.ps[:, b, qq * 128: qq * 128 + nrows], stg[s][0:nrows, k * 128:(k + 1) * 128],
                                  self.ident[0:nrows, 0:nrows], deps=d, signal=(qq == 3))
                src_ps = self.ps[:, b, :].rearrange("p (q n) -> p q n", q=4)[:, :, 0:nrows]
                dst_x = xT[:, g4 * 4:(g4 + 1) * 4, col0:col0 + nrows]
                if g4 % 2 == 0:
                    te = self.A(dst_x, src_ps, AF.Copy, deps=[trs])
                else:
                    te = self.Vcopy(dst_x, src_ps, deps=[trs])
                self.release_bank(b, [te])
                evs.append(te)
            stg_free[s] = trs
            last = evs
        return [P.cur("act"), P.cur("dve")]

    def stage_front(self, cfg, h):
        P = self.P
        xT = self.xT(cfg)
        NSLOT = 3
        stg = [self.R2[:, 8064 + i * 2048:8064 + (i + 1) * 2048] for i in range(NSLOT)]
        sqb = self.R4[:, 0:2048]
        ssb = [self.R4[:, 2048 + i:2049 + i] for i in range(NSLOT)]
        stg_free = [None] * NSLOT
        AX = mybir.AxisListType.X
        sqf = [None]
        tinfo = {}
        def phaseA(ti):
            row0, nrows, col0 = cfg.xtiles[ti]
            sq_free = sqf[0]
            s = ti % NSLOT
            self.x_cnt[s] += 16
            dst = stg[s][0:nrows, :]
            src = self.xq[row0:row0 + nrows, :]
            tl = P.dma("sp", lambda e, dst=dst, src=src: e.dma_start(out=dst, in_=src), self.s_x[s], self.x_cnt[s],
                       deps=[stg_free[s]] + self.last_barrier)
            tq = self.A(sqb[0:nrows, :], stg[s][0:nrows, :], AF.Square, deps=[tl, sq_free])
            ss = ssb[s][0:nrows, :]
            tr_ = P.op("dve", lambda e, ss=ss, nrows=nrows: e.reduce_sum(out=ss, in_=sqb[0:nrows, :], axis=AX),
                       deps=[tq, stg_free[s]])
            sq_free = tr_
            ta = self.A(ss, ss, AF.Sqrt, scale=1.0 / D, bias=EPS, deps=[tr_])
            trc = P.op("dve", lambda e, ss=ss: e.reciprocal(out=ss, in_=ss), deps=[ta])
            raw_done = []
            if not cfg.lru_only:
                for g4 in range(4):
                    b = self.alloc_bank()
                    trs = None
                    for qq in range(4):
                        k = g4 * 4 + qq
                        d = [tl, self.t_ld] + (self.bank_free[b] if qq == 0 else [])
                        trs = self.TR(self.ps[:, b, qq * 128: qq * 128 + nrows],
                                      stg[s][0:nrows, k * 128:(k + 1) * 128],
                                      self.ident[0:nrows, 0:nrows], deps=d, signal=(qq == 3))
                    src_ps = self.ps[:, b, :].rearrange("p (q n) -> p q n", q=4)[:, :, 0:nrows]
                    dst_x = xT[:, g4 * 4:(g4 + 1) * 4, col0:col0 + nrows]
                    if g4 % 2 == 0:
                        te = self.A(dst_x, src_ps, AF.Copy, deps=[trs])
                    else:
                        te = self.Vcopy(dst_x, src_ps, deps=[trs])
                    self.release_bank(b, [te])
                    raw_done = [trs]
            tsc = self.Vts(stg[s][0:nrows, :], stg[s][0:nrows, :], ss, None, ALU.mult, deps=[trc, tl] + raw_done)
            sqf[0] = sq_free
            tinfo[ti] = (s, nrows, col0, tsc)

        def phaseB(ti):
            s, nrows, col0, tsc = tinfo.pop(ti)
            is_samp = cfg.samp is not None and col0 == 0
            last_tr = None
            for g4 in range(4):
                b = self.alloc_bank()
                trs = None
                for qq in range(4):
                    k = g4 * 4 + qq
                    d = [tsc, self.t_ld] + (self.bank_free[b] if qq == 0 else [])
                    trs = self.TR(self.ps[:, b, qq * 128: qq * 128 + nrows], stg[s][0:nrows, k * 128:(k + 1) * 128],
                                  self.ident[0:nrows, 0:nrows], deps=d, signal=(qq == 3))
                last_tr = trs
                rel = []
                for qq in range(4):
                    k = g4 * 4 + qq
                    src_ps = self.ps[:, b, qq * 128: qq * 128 + nrows]
                    dst_h = h[:, k, col0:col0 + nrows]
                    if is_samp:
                        s3 = src_ps.rearrange("p (t s) -> p t s", t=8)
                        d3 = dst_h.rearrange("p (t s) -> p t s", t=8)
                        scb = self.mod[:, 1, k, 1:17].unsqueeze(1).broadcast_to([128, 8, 16])
                        shb = self.mod[:, 0, k, 1:17].unsqueeze(1).broadcast_to([128, 8, 16])
                        tmp3 = self.R4[:, 2056 + qq * 128:2056 + (qq + 1) * 128].rearrange("p (t s) -> p t s", t=8)
                        t1 = self.Vtt(tmp3, s3, scb, ALU.mult, deps=[trs, self.t_mod])
                        t2 = self.Vtt(d3, tmp3, shb, ALU.add, deps=[t1])
                        rel += [t1, t2]
                    elif g4 % 2 == 0:
                        rel.append(self.A(dst_h, src_ps, AF.Identity, scale=self.mod[:, 1, k, 0:1],
                                          bias=self.mod[:, 0, k, 0:1], deps=[trs, self.t_mod]))
                    else:
                        rel.append(self.Vts(dst_h, src_ps, self.mod[:, 1, k, 0:1], self.mod[:, 0, k, 0:1],
                                            ALU.mult, ALU.add, deps=[trs, self.t_mod]))
                self.release_bank(b, rel)
            stg_free[s] = last_tr
        nt_ = len(cfg.xtiles)
        phaseA(0)
        for ti in range(nt_):
            if ti + 1 < nt_:
                phaseA(ti + 1)
            phaseB(ti)
        toks = [P.cur("act"), P.cur("dve")]
        if cfg.halo:
            p0 = cfg.prm[0]
            hv = h[:, :, p0:p0 + cfg.halo]
            t = self.Vts(hv, hv, self.sel[:, 0:1], None, ALU.mult, deps=toks)
            toks = toks + [t]
        return toks

    def stage_stats(self, cfg, src, src_ready, pre_scale=1.0):
        P = self.P
        ntok = cfg.ntok
        sq = [self.R4[:, 0:ntok // 2].bitcast(BF16), self.R4[:, 512:512 + ntok // 2].bitcast(BF16)]
        sq_free = [None, None]
        banks = [self.alloc_bank() for _ in cfg.groups]
        for b in banks:
            self.bank_reserved.add(b)
        tok = None
        for k in range(DK):
            s = k % 2
            if k % 2 == 0:
                tq = self.A(sq[s][:, 0:ntok], src[:, k, :], AF.Square, deps=[src_ready, sq_free[s]])
            else:
                tq = self.Vtt(sq[s][:, 0:ntok], src[:, k, :], src[:, k, :], ALU.mult, deps=[src_ready, sq_free[s]])
            for gi, (c0, c1) in enumerate(cfg.groups):
                d = [tq] + (self.bank_free[banks[gi]] if k == 0 else [])
                tok = self.MM(self.bank_ap(banks[gi], c1 - c0), self.ones[:], sq[s][:, c0:c1], k == 0, k == DK - 1,
                              deps=d, signal=True)
            sq_free[s] = tok
        toks = []
        for gi, (c0, c1) in enumerate(cfg.groups):
            ta = self.A(self.rstd[:, c0:c1], self.bank_ap(banks[gi], c1 - c0), AF.Sqrt, scale=1.0 / D, bias=EPS,
                        deps=[tok])
            tv = P.op("dve", lambda e, c0=c0, c1=c1: e.reciprocal(out=self.rstd[:, c0:c1], in_=self.rstd[:, c0:c1]),
                      deps=[ta])
            self.release_bank(banks[gi], [ta])
            self.bank_reserved.discard(banks[gi])
            toks.append(tv)
        return toks

    def stage_normmod(self, cfg, src, dst, mi_shift, mi_scale, deps):
        P = self.P
        ntok = cfg.ntok
        p0, p1 = cfg.prm
        tmp = [self.R4[:, 1024:1024 + ntok], self.R4[:, 1024 + NTMAX:1024 + NTMAX + ntok]]
        tmp_free = [None, None]
        for k in range(DK):
            s = k % 2
            t1 = self.Vtt(tmp[s], src[:, k, :], self.rstd[:, 0:ntok], ALU.mult, deps=[deps, tmp_free[s]])
            ta = self.A(dst[:, k, p0:p1], tmp[s][:, p0:p1], AF.Identity, scale=self.mod[:, mi_scale, k, 0:1],
                        bias=self.mod[:, mi_shift, k, 0:1], deps=[t1, self.t_mod])
            rel = [ta]
            if cfg.samp is not None:
                v3 = tmp[s][:, 0:128].rearrange("p (t s) -> p t s", t=8)
                scb = self.mod[:, mi_scale, k, 1:17].unsqueeze(1).broadcast_to([128, 8, 16])
                shb = self.mod[:, mi_shift, k, 1:17].unsqueeze(1).broadcast_to([128, 8, 16])
                t2 = self.Vtt(v3, v3, scb, ALU.mult, deps=[t1, self.t_mod])
                t3 = self.Vtt(dst[:, k, 0:128].rearrange("p (t s) -> p t s", t=8), v3, shb, ALU.add, deps=[t2])
                rel.append(t3)
            tmp_free[s] = rel
        toks = [P.cur("act"), P.cur("dve")]
        if cfg.halo:
            hv = dst[:, :, p0:p0 + cfg.halo]
            t = self.Vts(hv, hv, self.sel[:, 0:1], None, ALU.mult, deps=toks)
            toks = [t]
        return toks

    def stage_resid(self, cfg, acc, mi_gate, deps, fuse_stats=False):
        P = self.P
        xT = self.xT(cfg)
        ntok = cfg.ntok
        p0, p1 = cfg.prm
        if fuse_stats:
            sq = [self.R4[:, 0:ntok // 2].bitcast(BF16), self.R4[:, 512:512 + ntok // 2].bitcast(BF16)]
            sq_free = [None, None]
            banks = [self.alloc_bank() for _ in cfg.groups]
            for b in banks:
                self.bank_reserved.add(b)
            tok = None
        for k in range(DK):
            t1 = self.Vtt(acc[:, k, :], acc[:, k, :], self.rstd[:, 0:ntok], ALU.mult, deps=[deps],
                          eng=("pool" if k % 2 == 1 else "dve"))
            done = [self.Vstt(xT[:, k, p0:p1], acc[:, k, p0:p1], self.mod[:, mi_gate, k, 0:1], xT[:, k, p0:p1],
                              ALU.mult, ALU.add, deps=[t1, self.t_mod])]
            if cfg.samp is not None:
                v3 = acc[:, k, 0:128].rearrange("p (t s) -> p t s", t=8)
                gtb = self.mod[:, mi_gate, k, 1:17].unsqueeze(1).broadcast_to([128, 8, 16])
                t2 = self.Vtt(v3, v3, gtb, ALU.mult, deps=[t1, self.t_mod])
                x3 = xT[:, k, 0:128].rearrange("p (t s) -> p t s", t=8)
                done.append(self.Vtt(x3, x3, v3, ALU.add, deps=[t2]))
            if fuse_stats:
                s_ = k % 2
                tq = self.A(sq[s_][:, 0:ntok], xT[:, k, :], AF.Square, deps=done + [sq_free[s_]])
                for gi, (c0, c1) in enumerate(cfg.groups):
                    d = [tq] + (self.bank_free[banks[gi]] if k == 0 else [])
                    tok = self.MM(self.bank_ap(banks[gi], c1 - c0), self.ones[:], sq[s_][:, c0:c1], k == 0,
                                  k == DK - 1, deps=d, signal=True)
                sq_free[s_] = tok
        if not fuse_stats:
            return [P.cur("dve"), P.cur("pool")]
        toks = []
        last_dve = [P.cur("dve"), P.cur("pool")]
        for gi, (c0, c1) in enumerate(cfg.groups):
            ta = self.A(self.rstd[:, c0:c1], self.bank_ap(banks[gi], c1 - c0), AF.Sqrt, scale=1.0 / D, bias=EPS,
                        deps=[tok, last_dve])
            tv = P.op("dve", lambda e, c0=c0, c1=c1: e.reciprocal(out=self.rstd[:, c0:c1], in_=self.rstd[:, c0:c1]),
                      deps=[ta])
            self.release_bank(banks[gi], [ta])
            self.bank_reserved.discard(banks[gi])
            toks.append(tv)
        return toks

    def ext_layout(self, cfg, H):
        if cfg.samp is not None:
            hs = H * 16
            return hs + cfg.ntok, hs, hs + 128, hs
        return H + cfg.ntok, H, H, None

    def run_pass(self, cfg):
        P = self.P
        nt = cfg.ntok
        p0, p1 = cfg.prm
        Lp = p1 - p0
        groups = cfg.groups
        cv = self.cvec
        xT = self.xT(cfg)
        h = self.bfview(self.R2, 0, 16, nt)
        if cfg.lru_only:
            th = self.stage_front(cfg, h)
            self.barrier()
            self.stage_lru(cfg, h, th, None)
            self.barrier()
            return
        y_pool = self.bfview(self.R2, 21504, 8, nt)
        y_lru = self.bfview(self.R2, 32256, 16, nt)
        o_sb = self.f32view(self.R2, 0, 16, nt)
        merged = self.bfview(self.R4, 0, 16, nt)
        h2 = self.bfview(self.R4, 21504, 16, nt)
        d_sb = self.f32view(self.R4, 0, 16, nt)
        f = self.bfview(self.R2, 0, 48, nt)

        th = self.stage_front(cfg, h)
        self.barrier()
        h_ready = th

        if not cfg.lru_only:
            self.stage_pool(cfg, h, h_ready, y_pool)
            self.barrier()
        self.stage_lru(cfg, h, h_ready, y_lru)
        self.barrier()
        if cfg.lru_only:
            return
        self.stage_merge(cfg, h, y_pool, y_lru, merged)
        self.barrier()
        self.stage_proj_norm(cfg, merged, self.w_out, 16, o_sb, 0.5)
        if self.ada_pending is not None and not self.ada_mid_done:
            while self.ada_pending and self.ada_pending[0] < 20:
                self.ada_item(self.ada_pending.pop(0))
            self.ada_finalize([2, 3, 4])
            self.ada_mid_done = True
        self.barrier()
        ts = self.stage_resid(cfg, o_sb, 2, [P.cur("dve"), P.cur("act")], fuse_stats=True)
        th2 = self.stage_normmod(cfg, xT, h2, 3, 4, ts)
        self.barrier()
        self.stage_ffn_up(cfg, h2, f)
        if self.ada_pending is not None:
            while self.ada_pending:
                self.ada_item(self.ada_pending.pop(0))
            self.ada_finalize([5])
            self.ada_pending = None
        self.barrier(self.so_all())
        self.stage_proj_norm(cfg, f, self.w_fdn, 48, d_sb, 1.0)
        self.barrier()
        self.stage_resid(cfg, d_sb, 5, [P.cur("dve"), P.cur("act")])
        self.barrier()
        self.stage_store_y(cfg)
        self.barrier(self.out_tokens)

    def stage_pool(self, cfg, h, h_ready, y_pool):
        P = self.P
        nt = cfg.ntok
        p0, p1 = cfg.prm
        Lp = p1 - p0
        cv = self.cvec
        W, cur, prm, scur = self.ext_layout(cfg, 15)
        def u_ap(c, i):
            o = (c * 3 + i) * 928
            return self.R4[:, o:o + W]
        dbuf = [self.R4[:, 6 * 928 + c * 336: 6 * 928 + c * 336 + nt // 2].bitcast(BF16) for c in range(2)]
        sstage = self.R4[:, 7256:7256 + 2048]
        t_hl = None
        if cfg.samp is not None:
            for r in range(15):
                tno, rr = (0, r) if r < 8 else (1, r - 8)
                self.misc_dma("sp", sstage[rr * 16:(rr + 1) * 16, tno * 1024:(tno + 1) * 1024], self.st_pool[:, r, :],
                              deps=self.last_barrier)
            t_hl = (self.s_misc, self.misc_cnt)
        fix = self.R4[:, 6 * 928 + 2 * 336 + 512: 6 * 928 + 2 * 336 + 512 + 16]
        so_stage = self.R4[:, 7000:7000 + 256]
        prev_done = None
        for g in range(4):
            w = 2 ** (g + 1)
            if self.ada_pending and cfg.samp is not None and self.ada_pending[0] < 16:
                self.ada_item(self.ada_pending.pop(0))
            wsl, wt, wi = self.wload(self.w_in[:, g * 256:(g + 1) * 256])
            dtoks = []
            hist_b = []
            if cfg.samp is not None:
                for c in range(2):
                    b = self.alloc_bank()
                    tt = None
                    for tno, nrow in enumerate((128, 112)):
                        d = [t_hl] + (self.bank_free[b] if tno == 0 else [])
                        tt = self.TR(self.ps[:, b, tno * 128: tno * 128 + nrow],
                                     sstage[0:nrow, tno * 1024 + (2 * g + c) * 128: tno * 1024 + (2 * g + c + 1) * 128],
                                     self.ident[0:nrow, 0:nrow], deps=d, signal=(tno == 1))
                    hist_b.append((b, tt))
            zjobs = []
            for c in range(2):
                banks, tok = self.job(cfg.groups, [(wsl[:, k, c * 128:(c + 1) * 128],
                                                    (lambda c0, c1, k=k: h[:, k, c0:c1]), [wt, h_ready])
                                                   for k in range(DK)])
                zjobs.append((banks, tok))
            self.wrelease(wi, zjobs[1][1])
            evs_c = []
            for c in range(2):
                ch = 2 * g + c
                U = u_ap(c, 0)
                banks, tok = zjobs[c]
                evs = []
                for gi, (c0, c1) in enumerate(cfg.groups):
                    evs.append(self.A(U[:, cur + c0:cur + c1], self.bank_ap(banks[gi], c1 - c0), AF.Copy,
                                      deps=[tok, prev_done]))
                    self.release_bank(banks[gi], [evs[-1]])
                if cfg.samp is not None:
                    b, tt = hist_b[c]
                    tevh = self.Vcopy(U[:, 0:240], self.ps[:, b, 0:240], deps=[tt, prev_done])
                    self.release_bank(b, [tevh])
                    evs.append(tevh)
                else:
                    evs.append(self.Vcopy(U[:, 0:15], self.hist_pool[:, ch, :], deps=[prev_done, self.t_init]))
                evs_c.append(evs)
                st_ = 16 if cfg.samp is not None else 1
                regions = []
                if cfg.samp is not None:
                    regions.append((0, 240 + 128, 16, 240))
                    regions.append((prm - 15, W, 1, prm))
                else:
                    regions.append((0, W, 1, cur))
                bufs = [U, u_ap(c, 1), u_ap(c, 2)]
                tlast = evs
                for (r0, r1, strd, fo) in regions:
                    srcb = U
                    di = 1
                    sh = 1
                    tl = tlast
                    lo = r0
                    while sh < w:
                        dstb = bufs[di]
                        lo2 = lo + sh * strd
                        tl = [self.Vtt(dstb[:, lo2:r1], srcb[:, lo2:r1], srcb[:, lo2 - sh * strd:r1 - sh * strd],
                                       ALU.add, deps=tl)]
                        srcb = dstb
                        di = 2 if di == 1 else 1
                        lo = lo2
                        sh *= 2
                    n = r1 - fo
                    dcol = 0 if (cfg.samp is not None and strd == 16) else p0
                    td = self.Vstt(dbuf[c][:, dcol:dcol + n], srcb[:, fo:r1], 1.0 / w, U[:, fo:r1], ALU.mult,
                                   ALU.subtract, deps=tl)
                    dtoks.append(td)
                    tlast = evs + [td]
                    if cfg.halo and strd == 1:
                        m0 = fo + cfg.halo
                        tf = self.Vtt(fix, srcb[:, m0:m0 + 16], self.invc[:, g, :], ALU.mult, deps=tl)
                        td2 = self.Vtt(dbuf[c][:, p0 + cfg.halo:p0 + cfg.halo + 16], fix, U[:, m0:m0 + 16],
                                       ALU.subtract, deps=[tf, td])
                        dtoks.append(td2)
                tsv = self.A(self.hist_pool[:, ch, :], U[:, W - 15:W], AF.Copy, deps=evs)
                dtoks.append(tsv)
            if cfg.samp is not None:
                for c in range(2):
                    U = u_ap(c, 0)
                    b = self.alloc_bank()
                    tt = self.TR(self.ps[:, b, 0:128], U[:, 240:368], self.ident[:],
                                 deps=evs_c[c] + self.bank_free[b], signal=True)
                    te = self.A(so_stage[:, c * 128:(c + 1) * 128], self.ps[:, b, 0:128], AF.Copy,
                                deps=[tt, prev_done])
                    self.release_bank(b, [te])
                    dtoks += [te, tt]
                so_toks = []
                for t in range(8):
                    so_toks.append(self.so_dma(self.o_pool_s[:, 7 + t, g * 256:(g + 1) * 256],
                                               so_stage[t * 16:(t + 1) * 16, :], deps=dtoks, stream="pool"))
                t_so = self.so_tok("pool")
            ytoks = []
            for j in range(2):
                banks, tok = self.job(cfg.groups, [(self.wgrp[:, g, kk, j * 128:(j + 1) * 128],
                                                    (lambda c0, c1, kk=kk: dbuf[kk][:, c0:c1]),
                                                    [self.t_wgrp] + dtoks) for kk in range(2)])
                for gi, (c0, c1) in enumerate(cfg.groups):
                    te = self.A(y_pool[:, 2 * g + j, c0:c1], self.bank_ap(banks[gi], c1 - c0), AF.Copy,
                                scale=cv[:, CV_PSCALE + 2 * g + j:CV_PSCALE + 2 * g + j + 1], deps=[tok])
                    self.release_bank(banks[gi], [te])
                    ytoks.append(te)
            prev_done = [P.cur("pe"), P.cur("dve"), P.cur("act")]
            if cfg.samp is not None:
                prev_done = prev_done + [t_so]
        if cfg.samp is not None:
            self.so_dma(self.o_pool_s[:, 0:7, :], self.st_pool[:, 8:15, :], stream="carry")
            self.out_tokens += self.so_all()

    def wsm_load(self, blk):
        i = self.wsm_next
        self.wsm_next = (i + 1) % 2
        self.wsm_cnt[i] += 32
        P = self.P
        dst0 = self.wsm[i][:, 0]
        dst1 = self.wsm[i][:, 1]
        s0 = self.w_rg[blk].rearrange("(k p) n -> p k n", p=128)
        s1 = self.w_ig[blk].rearrange("(k p) n -> p k n", p=128)
        P.dma("pool", lambda e: e.dma_start(out=dst0, in_=s0), self.wsm_sem[i], 0, deps=[self.wsm_rel[i]])
        tok = P.dma("pool", lambda e: e.dma_start(out=dst1, in_=s1), self.wsm_sem[i], self.wsm_cnt[i])
        return self.wsm[i], tok, i

    def stage_lru(self, cfg, h, h_ready, y_lru):
        P = self.P
        nt = cfg.ntok
        p0, p1 = cfg.prm
        Lp = p1 - p0
        cv = self.cvec
        W, cur, prm, scur = self.ext_layout(cfg, 3)
        samp = cfg.samp is not None
        big = nt > NTMAX
        NTP = 992 if big else NTMAX
        UW = 1000 if big else 768
        CS, GSZ = UW + NTP, 3 * NTP
        def U_ap(s_, c):
            o = (s_ * 2 + c) * CS
            return self.R4[:, o:o + W]
        def xc_ap(s_, c):
            o = (s_ * 2 + c) * CS + UW
            return self.R4[:, o:o + nt]
        def ap3(base, stride):
            return bass.AP(base.tensor, base.offset, [list(base.ap[0]), [stride, 2], [1, nt]])
        def xc3(s_):
            return ap3(xc_ap(s_, 0), CS)
        NGS = 2 if cfg.lru_only else 1
        gs_cur = [0]
        def g_ap(c, i):
            if big:
                reg, base = (self.R1, 0) if gs_cur[0] == 0 else (self.R2, 8064)
                o = base + c * GSZ + i * NTP
                return reg[:, o:o + nt]
            if gs_cur[0] == 1:
                o = c * GSZ + i * NTP
                return self.R1[:, o:o + nt]
            o = 5760 + c * GSZ + i * NTP
            return self.R4[:, o:o + nt]
        def g3(i):
            return ap3(g_ap(0, i), GSZ)
        XH = NTP // 2
        if big:
            xcb_t = [self.R1[:, 5952:5952 + 2 * XH], self.R1[:, 5952 + 2 * XH:5952 + 4 * XH]]
        else:
            xcb_t = [self.rstd, self.sq_scratch]
        def xcb_ap(s_, c):
            return xcb_t[s_][:, c * XH:c * XH + nt // 2].bitcast(BF16)
        def xcb3(s_):
            return xcb_t[s_][:, 0:2 * XH].bitcast(BF16).rearrange("p (c n) -> p c n", c=2)[:, :, 0:nt]
        stage_start = list(self.last_barrier)
        if samp:
            st_l = self.R2[:, 13440:13440 + 2048]
            for r in range(3):
                self.misc_dma("sp", st_l[r * 16:(r + 1) * 16, :], self.st_lconv[:, r, :], deps=stage_start)
            self.misc_dma("sp", st_l[48:64, :], self.st_lh, deps=stage_start)
            t_stl = (self.s_misc, self.misc_cnt)
            h0s = self.R2[:, 15488:15488 + 256].rearrange("p (k s) -> p k s", k=16)
            so_conv = self.R2[:, 15744:15744 + 256]
        st = {"conv_free": [None, None], "xcb_free": [None, None], "gate_free": [None, None], "so1": None, "so2": None,
              "s1": {}}

        def S1A(blk):
            s_ = blk % 2
            if self.ada_pending:
                if cfg.lru_only and blk % 2 == 0 and self.ada_pending[0] < 12:
                    self.ada_item(self.ada_pending.pop(0))
                elif samp and blk % 2 == 0 and self.ada_pending[0] < 20:
                    self.ada_item(self.ada_pending.pop(0))
            wsl, wt, wi = self.wload(self.w_in[:, 1024 + blk * 256:1024 + (blk + 1) * 256])
            wg, wgt, wgi = self.wsm_load(blk)
            cfree = st["conv_free"][s_]
            tap0 = []
            evs_c = []
            hist_toks = []
            hist_tr = []
            if samp:
                for c in range(2):
                    ch = 2 * blk + c
                    b = self.alloc_bank()
                    tt = self.TR(self.ps[:, b, 0:64], st_l[0:64, ch * 128:(ch + 1) * 128], self.ident[0:64, 0:64],
                                 deps=[t_stl] + self.bank_free[b], signal=True)
                    hist_tr.append((b, tt))
            jobs = []
            for c in range(2):
                banks, tok = self.job(cfg.groups, [(wsl[:, k, c * 128:(c + 1) * 128],
                                                    (lambda c0, c1, k=k: h[:, k, c0:c1]), [wt, h_ready])
                                                   for k in range(DK)])
                jobs.append((banks, tok))
            self.wrelease(wi, jobs[1][1])
            for c in range(2):
                ch = 2 * blk + c
                U = U_ap(s_, c)
                xc = xc_ap(s_, c)
                banks, tok = jobs[c]
                evs = []
                if samp:
                    b, tt = hist_tr[c]
                    te1 = self.Vcopy(U[:, 0:48], self.ps[:, b, 0:48], deps=[tt, cfree])
                    te2 = self.Vcopy(h0s[:, ch, :], self.ps[:, b, 48:64], deps=[tt])
                    self.release_bank(b, [te1, te2])
                    evs += [te1, te2]
                else:
                    evs.append(self.Vcopy(U[:, 0:3], self.hist_lru[:, ch, :], deps=[cfree, self.t_init]))
                for gi, (c0, c1) in enumerate(cfg.groups):
                    evs.append(self.A(U[:, cur + c0:cur + c1], self.bank_ap(banks[gi], c1 - c0), AF.Copy,
                                      deps=[tok, cfree]))
                    self.release_bank(banks[gi], [evs[-1]])
                wl3 = cv[:, CV_WLC + 48 + ch:CV_WLC + 48 + ch + 1]
                bl = cv[:, CV_BLC + ch:CV_BLC + ch + 1]
                regs = []
                if samp:
                    regs.append((48, 128, 16, 0))
                    regs.append((prm, Lp, 1, p0))
                else:
                    regs.append((cur, Lp, 1, p0))
                t0s = []
                for (co, n, strd, xo) in regs:
                    t0s.append(self.A(xc[:, xo:xo + n], U[:, co:co + n], AF.Identity, scale=wl3, bias=bl,
                                      deps=evs + [cfree]))
                tap0.append((regs, t0s))
                evs_c.append(evs)
                hist_toks.append(self.A(self.hist_lru[:, ch, :], U[:, W - 3:W], AF.Copy, deps=evs))
            if samp:
                for c in range(2):
                    U = U_ap(s_, c)
                    b = self.alloc_bank()
                    tt = self.TR(self.ps[0:48, b, 0:128], U[:, 128:176], self.ident[:],
                                 deps=evs_c[c] + self.bank_free[b], signal=True)
                    te = self.A(so_conv[0:48, c * 128:(c + 1) * 128], self.ps[0:48, b, 0:128], AF.Copy,
                                deps=[tt, st["so1"]])
                    self.release_bank(b, [te])
                    hist_toks += [te, tt]
                for r in range(3):
                    self.so_dma(self.o_lconv_s[:, r, blk * 256:(blk + 1) * 256], so_conv[r * 16:(r + 1) * 16, :],
                                deps=hist_toks, stream="lru1")
                st["so1"] = self.so_tok("lru1")
            st["s1"][blk] = dict(wg=wg, wgt=wgt, wgi=wgi, tap0=tap0, evs=evs_c, ureaders=list(hist_toks))

        def S1B(blk):
            s_ = blk % 2
            info = st["s1"][blk]
            ctoks_all = []
            for c in range(2):
                ch = 2 * blk + c
                U = U_ap(s_, c)
                xc = xc_ap(s_, c)
                regs, t0s = info["tap0"][c]
                for ri, (co, n, strd, xo) in enumerate(regs):
                    t = t0s[ri]
                    for k in range(3):
                        sh = (3 - k) * strd
                        t = self.Vstt(xc[:, xo:xo + n], U[:, co - sh:co - sh + n],
                                      cv[:, CV_WLC + 16 * k + ch:CV_WLC + 16 * k + ch + 1], xc[:, xo:xo + n],
                                      ALU.mult, ALU.add, deps=[t] + info["evs"][c])
                    ctoks_all.append(t)
            tb = self.Vcopy(xcb3(s_), xc3(s_), deps=ctoks_all + [st["xcb_free"][s_]])
            info["tb"] = tb
            info["ureaders"] += ctoks_all

        def P1(blk):
            s_ = blk % 2
            info = st["s1"][blk]
            wg, wgt, wgi, tb = info["wg"], info["wgt"], info["wgi"], info["tb"]
            gs_cur[0] = blk % NGS
            gfree = st["gate_free"][blk % NGS]
            tanh_r, tanh_i = [], []
            lastpe = None
            for j in range(2):
                ch = 2 * blk + j
                a = g_ap(j, 0)
                g = g_ap(j, 2)
                for (gi_, dst, brow, lst) in ((0, a, 0, tanh_r), (1, g, 1, tanh_i)):
                    banks, tok = self.job(cfg.groups, [(wg[:, gi_, kk, j * 128:(j + 1) * 128],
                                                        (lambda c0, c1, kk=kk: xcb_ap(s_, kk)[:, c0:c1]), [wgt, tb])
                                                       for kk in range(2)])
                    lastpe = tok
                    for gi, (c0, c1) in enumerate(cfg.groups):
                        t = self.A(dst[:, c0:c1], self.bank_ap(banks[gi], c1 - c0), AF.Tanh, scale=0.5,
                                   bias=self.dv[:, brow, ch:ch + 1], deps=[tok, gfree, self.t_dv])
                        self.release_bank(banks[gi], [t])
                        lst.append(t)
            self.wsm_rel[wgi] = lastpe
            st["xcb_free"][s_] = lastpe
            ta = []
            for j in range(2):
                ch = 2 * blk + j
                a = g_ap(j, 0)
                ta.append(self.A(a, a, AF.Exp, scale=self.dv[:, 3, ch:ch + 1], bias=self.dv[:, 3, ch:ch + 1],
                                 deps=tanh_r))
            tgx = self.Vstt(g3(2), g3(2), 1.0, xc3(s_), ALU.add, ALU.mult, deps=tanh_i + [tb])
            st["conv_free"][s_] = info["ureaders"] + [tgx, tb]
            tm = self.A(g3(1), g3(0), AF.Square, deps=ta + [gfree])
            info.update(ta=ta, tgx=tgx, tm=tm)

        def P2a(blk):
            gs_cur[0] = blk % NGS
            info = st["s1"][blk]
            tm = self.Vts(g3(1), g3(1), -0.25, 0.25, ALU.mult, ALU.add, deps=[info["tm"]])
            info["tsq"] = self.A(g3(1), g3(1), AF.Sqrt, deps=[tm])

        def P2(blk):
            gs_cur[0] = blk % NGS
            info = st["s1"][blk]
            ta, tgx, tsq = info["ta"], info["tgx"], info["tsq"]
            tuu = self.Vtt(g3(2), g3(2), g3(1), ALU.mult, deps=[tgx, tsq])
            fin = []
            for j in range(2):
                ch = 2 * blk + j
                a, hs, g = g_ap(j, 0), g_ap(j, 1), g_ap(j, 2)
                hc = self.h_carry[:, ch:ch + 1]
                if cfg.halo:
                    t1 = P.op("dve", lambda e, a=a, g=g, hs=hs, hc=hc: e.tensor_tensor_scan(
                        out=hs[:, p0:p0 + HALO], data0=a[:, p0:p0 + HALO], data1=g[:, p0:p0 + HALO], initial=hc,
                        op0=ALU.mult, op1=ALU.add), deps=[tuu] + ta + [self.t_init])
                    hm = self.R2[:, 16100 + j:16101 + j]
                    t2 = self.Vts(hm, hs[:, p0 + HALO - 1:p0 + HALO], self.sel[:, 0:1], None, ALU.mult, deps=[t1])
                    t3 = P.op("dve", lambda e, a=a, g=g, hs=hs, hm=hm: e.tensor_tensor_scan(
                        out=hs[:, p0 + HALO:p1], data0=a[:, p0 + HALO:p1], data1=g[:, p0 + HALO:p1], initial=hm,
                        op0=ALU.mult, op1=ALU.add), deps=[t2])
                else:
                    t3 = P.op("dve", lambda e, a=a, g=g, hs=hs, hc=hc: e.tensor_tensor_scan(
                        out=hs[:, p0:p1], data0=a[:, p0:p1], data1=g[:, p0:p1], initial=hc,
                        op0=ALU.mult, op1=ALU.add), deps=[tuu] + ta + [self.t_init])
                t4 = self.Vcopy(hc, hs[:, p1 - 1:p1], deps=[t3])
                fin += [t3, t4]
            if samp:
                a3, hs3, gg3 = g3(0), g3(1), g3(2)
                prev = h0s[:, 2 * blk:2 * blk + 2, :]
                t = [tuu] + ta
                for tstep in range(8):
                    sl_ = slice(tstep * 16, (tstep + 1) * 16)
                    t = [self.Vtt(hs3[:, :, sl_], a3[:, :, sl_], prev, ALU.mult, deps=t)]
                    t = [self.Vtt(hs3[:, :, sl_], hs3[:, :, sl_], gg3[:, :, sl_], ALU.add, deps=t)]
                    prev = hs3[:, :, sl_]
                fin += t
            info["fin"] = fin

        def P3(blk):
            gs_cur[0] = blk % NGS
            info = st["s1"].pop(blk)
            fin = info["fin"]
            if not cfg.lru_only:
                ty = self.A(y_lru[:, 2 * blk:2 * blk + 2, :], g3(1), AF.Copy, deps=fin)
                fin.append(ty)
            if samp:
                so_t = []
                for j in range(2):
                    hs = g_ap(j, 1)
                    b = self.alloc_bank()
                    tt = self.TR(self.ps[0:16, b, 0:128], hs[:, 112:128], self.ident[:],
                                 deps=fin + self.bank_free[b], signal=True)
                    te = self.A(so_conv[64:80, j * 128:(j + 1) * 128], self.ps[0:16, b, 0:128], AF.Copy,
                                deps=[tt, st["so2"]])
                    self.release_bank(b, [te])
                    so_t += [te, tt]
                self.so_dma(self.o_lh_s[:, blk * 256:(blk + 1) * 256], so_conv[64:80, :], deps=so_t, stream="lru2")
                st["so2"] = self.so_tok("lru2")
                fin += so_t
            st["gate_free"][blk % NGS] = fin

        S1A(0)
        S1B(0)
        S1A(1)
        S1B(1)
        for blk in range(8):
            P1(blk)
            P2a(blk)
            if blk + 2 < 8:
                S1A(blk + 2)
            P2(blk)
            if blk + 2 < 8:
                S1B(blk + 2)
            P3(blk)
        if samp:
            self.out_tokens += self.so_all()

    def stage_merge(self, cfg, h, y_pool, y_lru, merged):
        P = self.P
        nt = cfg.ntok
        base = 5376
        gb = [self.R4[:, base + i * NTMAX: base + i * NTMAX + nt] for i in range(4)]
        prev_done = None
        for q in range(8):
            specs = [(self.w_in[:, 3072 + q * 256:3072 + (q + 1) * 256], 16, h, 0),
                     (self.w_in[:, 5120 + q * 256:5120 + (q + 1) * 256], 16, h, 2)]
            for (src, kch, opnd, bi) in specs:
                wsl, wt, wi = self.wload(src, kch)
                for j in range(2):
                    banks, tok = self.job(cfg.groups, [(wsl[:, k, j * 128:(j + 1) * 128],
                                                        (lambda c0, c1, k=k, opnd=opnd: opnd[:, k, c0:c1]), [wt])
                                                       for k in range(kch)])
                    if j == 1:
                        self.wrelease(wi, tok)
                    for gi, (c0, c1) in enumerate(cfg.groups):
                        t = self.A(gb[bi + j][:, c0:c1], self.bank_ap(banks[gi], c1 - c0), AF.Tanh, scale=0.5,
                                   deps=[tok, prev_done])
                        self.release_bank(banks[gi], [t])
            tg = P.cur("act")
            specs = [(self.w_pup[:, q * 256:(q + 1) * 256], 8, y_pool, 0),
                     (self.w_lup[:, q * 256:(q + 1) * 256], 16, y_lru, 2)]
            for (src, kch, opnd, bi) in specs:
                wsl, wt, wi = self.wload(src, kch)
                for j in range(2):
                    banks, tok = self.job(cfg.groups, [(wsl[:, k, j * 128:(j + 1) * 128],
                                                        (lambda c0, c1, k=k, opnd=opnd: opnd[:, k, c0:c1]), [wt])
                                                       for k in range(kch)])
                    if j == 1:
                        self.wrelease(wi, tok)
                    for gi, (c0, c1) in enumerate(cfg.groups):
                        t = self.Vstt(gb[bi + j][:, c0:c1], gb[bi + j][:, c0:c1], 1.0,
                                      self.bank_ap(banks[gi], c1 - c0), ALU.add, ALU.mult, deps=[tok, tg])
                        self.release_bank(banks[gi], [t])
            tv = P.cur("dve")
            for j in range(2):
                self.Vtt(merged[:, 2 * q + j, :], gb[j][:, 0:nt], gb[2 + j][:, 0:nt], ALU.add, deps=[tv])
            prev_done = [P.cur("dve")]

    def stage_proj_norm(self, cfg, opnd, wsrc, kchunks, acc, evac_scale):
        P = self.P
        nt = cfg.ntok
        nparts = kchunks // 16
        sqb = [self.sq_scratch[:, i * 336:i * 336 + nt // 2].bitcast(BF16) for i in range(2)]
        sq_free = [None, None]
        sbanks = [self.alloc_bank() for _ in cfg.groups]
        for b in sbanks:
            self.bank_reserved.add(b)
        pend = None
        stok = None
        nsq = 0

        def flush(pend, first, last):
            (tq, s) = pend
            tk = None
            for gi, (c0, c1) in enumerate(cfg.groups):
                d = [tq] + (self.bank_free[sbanks[gi]] if first else [])
                tk = self.MM(self.bank_ap(sbanks[gi], c1 - c0), self.ones[:], sqb[s][:, c0:c1], first, last,
                             deps=d, signal=True)
            sq_free[s] = tk
            return tk

        for cb in range(8):
            open_jobs = []
            for j in range(2):
                open_jobs.append([self.alloc_bank() for _ in cfg.groups])
            toks = [None, None]
            for kp in range(nparts):
                src = wsrc[kp * 2048:(kp + 1) * 2048, cb * 256:(cb + 1) * 256]
                wsl, wt, wi = self.wload(src)
                for j in range(2):
                    banks = open_jobs[j]
                    for k in range(DK):
                        first = (kp == 0 and k == 0)
                        last = (kp == nparts - 1 and k == DK - 1)
                        for gi, (c0, c1) in enumerate(cfg.groups):
                            d = [wt] + (self.bank_free[banks[gi]] if first else [])
                            sig = (k == DK - 1 and gi == len(cfg.groups) - 1)
                            t = self.MM(self.bank_ap(banks[gi], c1 - c0), wsl[:, k, j * 128:(j + 1) * 128],
                                        opnd[:, kp * 16 + k, c0:c1], first, last, deps=d, signal=sig)
                            if sig:
                                toks[j] = t
                self.wrelease(wi, toks[1])
            for j in range(2):
                o = 2 * cb + j
                banks = open_jobs[j]
                evs = []
                for gi, (c0, c1) in enumerate(cfg.groups):
                    te = self.A(acc[:, o, c0:c1], self.bank_ap(banks[gi], c1 - c0), AF.Copy, scale=evac_scale,
                                deps=[toks[j]])
                    self.release_bank(banks[gi], [te])
                    evs.append(te)
                s = nsq % 2
                tq = self.Vtt(sqb[s][:, 0:nt], acc[:, o, :], acc[:, o, :], ALU.mult, deps=evs + [sq_free[s]])
                if pend is not None:
                    stok = flush(pend, nsq == 1, False)
                pend = (tq, s)
                nsq += 1
        stok = flush(pend, False, True)
        for gi, (c0, c1) in enumerate(cfg.groups):
            ta = self.A(self.rstd[:, c0:c1], self.bank_ap(sbanks[gi], c1 - c0), AF.Sqrt, scale=1.0 / D, bias=EPS,
                        deps=[stok])
            P.op("dve", lambda e, c0=c0, c1=c1: e.reciprocal(out=self.rstd[:, c0:c1], in_=self.rstd[:, c0:c1]),
                 deps=[ta])
            self.release_bank(sbanks[gi], [ta])
            self.bank_reserved.discard(sbanks[gi])

    def stage_ffn_up(self, cfg, h2, f):
        P = self.P
        nt = cfg.ntok
        p0, p1 = cfg.prm
        Lp = p1 - p0
        cv = self.cvec
        W, cur, prm, scur = self.ext_layout(cfg, 2)
        def U_ap(slot, gv):
            o = slot * 2752 + gv * 704
            return self.R4[:, o:o + W]
        def C_ap(slot, gv):
            if slot == 1 and gv == 1:
                return self.sq_scratch[:, 0:nt]
            o = slot * 2752 + 1408 + gv * 672
            return self.R4[:, o:o + nt]
        st_fs = [self.rstd[:, 0:256], self.rstd[:, 256:512]]
        so_f = self.cT[:].rearrange("p k j -> p (k j)")[:, 0:256]
        stf_rd = [[], []]
        stf_tok = [None, None]

        def prefetch_hist(jf_):
            bi = jf_ % 2
            for gv_ in range(2):
                for r in range(2):
                    self.stf_cnt[bi] += 16
                    dst = st_fs[bi][r * 16:(r + 1) * 16, gv_ * 128:(gv_ + 1) * 128]
                    src = self.st_fconv[:, r, gv_ * DFF + jf_ * 128: gv_ * DFF + (jf_ + 1) * 128]
                    P.dma("sp", lambda e, dst=dst, src=src: e.dma_start(out=dst, in_=src), self.s_stf[bi],
                          self.stf_cnt[bi], deps=stf_rd[bi] + stage_start)
            stf_tok[bi] = (self.s_stf[bi], self.stf_cnt[bi])
            stf_rd[bi] = []
        stage_start = list(self.last_barrier)
        slot_free = [None, None]
        grp_so = []
        grp_rd = []
        pending_tail = None
        wrel = []

        def emit_tail(tl):
            (Cg, tg, Cv, tvv, jf_, slot_) = tl
            tgl = self.A(Cg[:, 0:nt], Cg[:, 0:nt], AF.Gelu_apprx_tanh, deps=tg)
            tf = self.Vtt(f[:, jf_, :], Cg[:, 0:nt], Cv[:, 0:nt], ALU.mult, deps=[tgl] + tvv)
            slot_free[slot_] = slot_free[slot_] + [tf]

        it = 0
        for q in range(24):
            if self.ada_pending and q % 6 == 0:
                self.ada_item(self.ada_pending.pop(0))
            wg_, wgt, wgi = self.wload(self.w_fup[:, q * 256:(q + 1) * 256])
            wv_, wvt, wvi = self.wload(self.w_fup[:, DFF + q * 256:DFF + (q + 1) * 256])
            lastpe = None
            for j in range(2):
                jf = 2 * q + j
                slot = it % 2
                it += 1
                if cfg.samp is not None:
                    if jf == 0:
                        prefetch_hist(0)
                    if jf + 1 < 48:
                        prefetch_hist(jf + 1)
                sfree = slot_free[slot]
                jobs = []
                for gv, (wsl, wt) in enumerate(((wg_, wgt), (wv_, wvt))):
                    banks, tok = self.job(cfg.groups, [(wsl[:, k, j * 128:(j + 1) * 128],
                                                        (lambda c0, c1, k=k: h2[:, k, c0:c1]), [wt])
                                                       for k in range(DK)])
                    jobs.append((banks, tok))
                    lastpe = tok
                hist_ps = []
                if cfg.samp is not None:
                    st_f = st_fs[jf % 2]
                    b = self.alloc_bank()
                    for gv in range(2):
                        tt = self.TR(self.ps[:, b, gv * 32:(gv + 1) * 32], st_f[0:32, gv * 128:(gv + 1) * 128],
                                     self.ident[0:32, 0:32],
                                     deps=[stf_tok[jf % 2]] + (self.bank_free[b] if gv == 0 else []), signal=True)
                        stf_rd[jf % 2].append(tt)
                        hist_ps.append((b, tt))
                        lastpe = tt
                evs_all = []
                for gv in range(2):
                    U = U_ap(slot, gv)
                    banks, tok = jobs[gv]
                    evs = []
                    for gi, (c0, c1) in enumerate(cfg.groups):
                        te = self.A(U[:, cur + c0:cur + c1], self.bank_ap(banks[gi], c1 - c0), AF.Copy,
                                    deps=[tok, sfree])
                        self.release_bank(banks[gi], [te])
                        evs.append(te)
                    evs_all.append(evs)
                for gv in range(2):
                    U = U_ap(slot, gv)
                    chn = jf + gv * 48
                    if cfg.samp is not None:
                        b, tt = hist_ps[gv]
                        te = self.Vcopy(U[:, 0:32], self.ps[:, b, gv * 32:(gv + 1) * 32],
                                        deps=[hist_ps[0][1], hist_ps[1][1], sfree])
                        if gv == 0:
                            hist_rel = [te]
                        else:
                            self.release_bank(b, hist_rel + [te])
                    else:
                        te = self.Vcopy(U[:, 0:2], self.hist_up[:, chn, :], deps=[sfree, self.t_init])
                    evs_all[gv].append(te)
                regs = []
                if cfg.samp is not None:
                    regs.append((32, 128, 16, 0))
                    regs.append((prm, Lp, 1, p0))
                else:
                    regs.append((cur, Lp, 1, p0))
                c0t = [[], []]
                for gv in range(2):
                    U = U_ap(slot, gv)
                    C = C_ap(slot, gv)
                    chn = jf + gv * 48
                    for (co, n, strd, xo) in regs:
                        c0t[gv].append(self.A(C[:, xo:xo + n], U[:, co:co + n], AF.Identity,
                                              scale=cv[:, CV_WFC + 192 + chn:CV_WFC + 192 + chn + 1],
                                              bias=cv[:, CV_BFC + chn:CV_BFC + chn + 1], deps=evs_all[gv] + [sfree]))
                ctoks = [[], []]
                for gv in range(2):
                    U = U_ap(slot, gv)
                    C = C_ap(slot, gv)
                    chn = jf + gv * 48
                    for ri, (co, n, strd, xo) in enumerate(regs):
                        t = c0t[gv][ri]
                        for k in range(2):
                            sh = (2 - k) * strd
                            t = self.Vstt(C[:, xo:xo + n], U[:, co - sh:co - sh + n],
                                          cv[:, CV_WFC + 96 * k + chn:CV_WFC + 96 * k + chn + 1], C[:, xo:xo + n],
                                          ALU.mult, ALU.add, deps=[t] + evs_all[gv])
                        ctoks[gv].append(t)
                ureaders = []
                for gv in range(2):
                    U = U_ap(slot, gv)
                    chn = jf + gv * 48
                    tsv = self.A(self.hist_up[:, chn, :], U[:, W - 2:W], AF.Copy, deps=evs_all[gv])
                    ureaders.append(tsv)
                if cfg.samp is not None:
                    bso = self.alloc_bank()
                    tts = []
                    for gv in range(2):
                        U = U_ap(slot, gv)
                        tt = self.TR(self.ps[0:32, bso, gv * 128:(gv + 1) * 128], U[:, 128:160], self.ident[:],
                                     deps=evs_all[gv] + (self.bank_free[bso] if gv == 0 else []), signal=True)
                        tts.append(tt)
                        ureaders.append(tt)
                        lastpe = tt
                    te = self.A(so_f[0:32, 0:256], self.ps[0:32, bso, 0:256], AF.Copy, deps=tts + grp_so)
                    self.release_bank(bso, [te])
                slot_free[slot] = ureaders + ctoks[0] + ctoks[1]
                if cfg.samp is not None:
                    for gv in range(2):
                        for r in range(2):
                            self.so_dma(self.o_fconv_s[:, r, gv * DFF + jf * 128: gv * DFF + (jf + 1) * 128],
                                        so_f[r * 16:(r + 1) * 16, gv * 128:(gv + 1) * 128],
                                        deps=[P.cur("act")], stream="ffn")
                    grp_so = [self.so_tok("ffn")]
                if pending_tail is not None:
                    emit_tail(pending_tail)
                pending_tail = (C_ap(slot, 0), ctoks[0], C_ap(slot, 1), ctoks[1], jf, slot)
            self.wrelease(wgi, lastpe)
            self.wrelease(wvi, lastpe)
        emit_tail(pending_tail)
        if cfg.samp is not None:
            self.out_tokens += self.so_all()

    def stage_store_y(self, cfg):
        P = self.P
        xT = self.xT(cfg)
        ost = [self.R2[:, 0:2048], self.R2[:, 2048:4096]]
        for ti, (col0, yrow0, nr) in enumerate(cfg.out_tiles):
            s = ti % 2
            prev = (self.s_o[s], self.o_cnt[s]) if self.o_cnt[s] else None
            evs = []
            for g4 in range(4):
                b = self.alloc_bank()
                tt = None
                for qq in range(4):
                    k = g4 * 4 + qq
                    d = (self.bank_free[b] if qq == 0 else [])
                    tt = self.TR(self.ps[0:nr, b, qq * 128:(qq + 1) * 128], xT[:, k, col0:col0 + nr], self.ident[:],
                                 deps=d, signal=(qq == 3))
                dst = ost[s][0:nr, g4 * 512:(g4 + 1) * 512]
                if g4 % 2 == 0:
                    te = self.A(dst, self.ps[0:nr, b, :], AF.Copy, deps=[tt, prev])
                else:
                    te = self.Vcopy(dst, self.ps[0:nr, b, :], deps=[tt, prev])
                self.release_bank(b, [te])
                evs.append(te)
            self.o_cnt[s] += 16
            src = ost[s]
            dsty = self.y[yrow0:yrow0 + nr, :]
            src = ost[s][0:nr, :]
            t = P.dma("sp", lambda e, dsty=dsty, src=src: e.dma_start(out=dsty, in_=src), self.s_o[s],
                      self.o_cnt[s], deps=evs)
            self.out_tokens.append(t)

    def epilogue(self):
        P = self.P
        self.barrier()
        stg = self.R2[:, 0:12288]
        jobs = [(self.hist_pool, 8, 15, self.o_pool_p), (self.hist_lru, 16, 3, self.o_lconv_p),
                (self.hist_up, 96, 2, self.o_fconv_p)]
        prev = None
        for (src, nch, ncol, dst) in jobs:
            evs = []
            for c0 in range(0, nch, 4):
                b = self.alloc_bank()
                tt = None
                for qq in range(4):
                    d = (self.bank_free[b] if qq == 0 else [])
                    tt = self.TR(self.ps[0:ncol, b, qq * 128:(qq + 1) * 128], src[:, c0 + qq, :], self.ident[:],
                                 deps=d, signal=(qq == 3))
                te = self.A(stg[0:ncol, c0 * 128:(c0 + 4) * 128], self.ps[0:ncol, b, :], AF.Copy, deps=[tt, prev])
                self.release_bank(b, [te])
                evs.append(te)
            t = self.so_dma(dst[:, :], stg[0:ncol, 0:nch * 128], deps=evs, stream="epi")
            prev = self.so_tok("epi")
        evs = []
        for c0 in range(0, 16, 4):
            b = self.alloc_bank()
            tt = None
            for qq in range(4):
                d = (self.bank_free[b] if qq == 0 else [])
                tt = self.TR(self.ps[0:1, b, qq * 128:(qq + 1) * 128], self.h_carry[:, c0 + qq:c0 + qq + 1],
                             self.ident[:], deps=d, signal=(qq == 3))
            te = self.A(stg[0:1, c0 * 128:(c0 + 4) * 128], self.ps[0:1, b, :], AF.Copy, deps=[tt, prev])
            self.release_bank(b, [te])
            evs.append(te)
        self.so_dma(self.o_lh_p[:, :], stg[0:1, 0:2048], deps=evs, stream="epi")
        final = self.so_all() + [(self.s_o[s], self.o_cnt[s]) for s in range(2)]
        P.wait_only("sp", final + self.out_tokens)


_NC_CACHE = {}


def _get_nc():
    if "nc" not in _NC_CACHE:
        b = Builder()
        _NC_CACHE["nc"] = b
    return _NC_CACHE["nc"]


def _pack_vec(v):
    v = np.asarray(v, np.float32).reshape(-1)
    return np.ascontiguousarray(v.reshape(-1, 128).T)


def kernel(x_prompt, x_sample, c_prompt, c_sample, state_pool, state_lru_conv, state_lru_h, state_ffn_conv,
           w_ada, b_ada, g_pre1, g_post1, g_pre2, g_post2, w_in, w_pool_grp, pool_scale,
           w_lru_conv, b_lru_conv, w_rg, b_rg, w_ig, b_ig, lru_lambda,
           w_pool_up, w_lru_up, w_out, w_ffn_up, w_ffn_conv, b_ffn_conv, w_ffn_down):
    f32 = np.float32
    A = lambda a: np.ascontiguousarray(np.asarray(a, f32))
    x_prompt, x_sample = A(x_prompt), A(x_sample)
    c_prompt, c_sample = A(c_prompt), A(c_sample)
    cvec = np.zeros((128, NV), f32)
    cvec[:, CV_GPRE1:CV_GPRE1 + 16] = _pack_vec(g_pre1[0])
    cvec[:, CV_GPOST1:CV_GPOST1 + 16] = _pack_vec(g_post1[0])
    cvec[:, CV_GPRE2:CV_GPRE2 + 16] = _pack_vec(g_pre2[0])
    cvec[:, CV_GPOST2:CV_GPOST2 + 16] = _pack_vec(g_post2[0])
    cvec[:, CV_PSCALE:CV_PSCALE + 8] = _pack_vec(pool_scale[0])
    for k in range(4):
        cvec[:, CV_WLC + 16 * k:CV_WLC + 16 * (k + 1)] = _pack_vec(np.asarray(w_lru_conv)[0, k])
    cvec[:, CV_BLC:CV_BLC + 16] = _pack_vec(b_lru_conv[0])
    cvec[:, CV_BRG:CV_BRG + 16] = _pack_vec(b_rg[0])
    cvec[:, CV_BIG:CV_BIG + 16] = _pack_vec(b_ig[0])
    cvec[:, CV_LAM:CV_LAM + 16] = _pack_vec(lru_lambda[0])
    for k in range(3):
        cvec[:, CV_WFC + 96 * k:CV_WFC + 96 * (k + 1)] = _pack_vec(np.asarray(w_ffn_conv)[0, k])
    cvec[:, CV_BFC:CV_BFC + 96] = _pack_vec(b_ffn_conv[0])
    cvec[:, CV_BADA:CV_BADA + 96] = _pack_vec(b_ada[0])
    ident = np.eye(128, dtype=f32)
    weights = {
        "w_ada": A(w_ada)[0], "w_in": A(w_in)[0], "w_pool_grp": A(w_pool_grp)[0], "w_rg": A(w_rg)[0],
        "w_ig": A(w_ig)[0], "w_pool_up": A(w_pool_up)[0], "w_lru_up": A(w_lru_up)[0], "w_out": A(w_out)[0],
        "w_ffn_up": A(w_ffn_up)[0], "w_ffn_down": A(w_ffn_down)[0],
    }
    state_pool, state_lru_conv = A(state_pool)[0], A(state_lru_conv)[0]
    state_lru_h, state_ffn_conv = A(state_lru_h)[0], A(state_ffn_conv)[0]
    in_maps = []
    for c in range(NCORES):
        b, hf = c // 2, c % 2
        xq = np.zeros((2176, D), f32)
        if hf == 1:
            xq[0:1024] = x_prompt[b, 0:1024]
        xq[1024:2048] = x_prompt[b, hf * 1024:(hf + 1) * 1024]
        xs = x_sample[16 * c:16 * (c + 1)]
        xq[2048:2176] = xs.transpose(1, 0, 2).reshape(128, D)
        cc = np.concatenate([c_prompt[b:b + 1], c_sample[16 * c:16 * (c + 1)]], axis=0)
        cT = np.ascontiguousarray(cc.reshape(17, 16, 128).transpose(2, 1, 0))
        sel = np.full((128, 1), float(hf), f32)
        invc = np.zeros((128, 4, 16), f32)
        for g in range(4):
            w = 2 ** (g + 1)
            for j in range(16):
                cnt = w if hf == 1 else min(w, j + 1)
                invc[:, g, j] = 1.0 / cnt
        m = {"xq": xq, "cT": cT, "cvec": cvec, "sel": sel, "invc": invc, "ident": ident,
             "st_pool": np.ascontiguousarray(state_pool[16 * c:16 * (c + 1)]),
             "st_lconv": np.ascontiguousarray(state_lru_conv[16 * c:16 * (c + 1)]),
             "st_lh": np.ascontiguousarray(state_lru_h[16 * c:16 * (c + 1)]),
             "st_fconv": np.ascontiguousarray(state_ffn_conv[16 * c:16 * (c + 1)])}
        m.update(weights)
        in_maps.append(m)
    nc = build_nc()
    res = run_bass_kernel_spmd(nc, in_maps, core_ids=list(range(NCORES)))
    R = res.results
    y_p = np.zeros((4, 2048, D), f32)
    y_s = np.zeros((128, 8, D), f32)
    pool_p = np.zeros((1, 4, 15, PW), f32)
    lconv_p = np.zeros((1, 4, 3, D), f32)
    lh_p = np.zeros((1, 4, D), f32)
    fconv_p = np.zeros((1, 4, 2, 2 * DFF), f32)
    pool_s = np.zeros((1, 128, 15, PW), f32)
    lconv_s = np.zeros((1, 128, 3, D), f32)
    lh_s = np.zeros((1, 128, D), f32)
    fconv_s = np.zeros((1, 128, 2, 2 * DFF), f32)
    for c in range(NCORES):
        b, hf = c // 2, c % 2
        r = R[c]
        y_p[b, hf * 1024:(hf + 1) * 1024] = r["y"][0:1024]
        y_s[16 * c:16 * (c + 1)] = r["y"][1024:1152].reshape(8, 16, D).transpose(1, 0, 2)
        if hf == 1:
            pool_p[0, b] = r["o_pool_p"]
            lconv_p[0, b] = r["o_lconv_p"]
            lh_p[0, b] = r["o_lh_p"][0]
            fconv_p[0, b] = r["o_fconv_p"]
        pool_s[0, 16 * c:16 * (c + 1)] = r["o_pool_s"]
        lconv_s[0, 16 * c:16 * (c + 1)] = r["o_lconv_s"]
        lh_s[0, 16 * c:16 * (c + 1)] = r["o_lh_s"]
        fconv_s[0, 16 * c:16 * (c + 1)] = r["o_fconv_s"]
    return (y_p, y_s, pool_p, lconv_p, lh_p, fconv_p, pool_s, lconv_s, lh_s, fconv_s)


def build_nc():
    if "built" not in _NC_CACHE:
        b = Builder()
        _NC_CACHE["built"] = b.build()
    return _NC_CACHE["built"]
```

```python
import contextlib
import numpy as np
import concourse.bass as bass
import concourse.mybir as mybir
from concourse.bass_utils import run_bass_kernel_spmd

F32 = mybir.dt.float32
BF16 = mybir.dt.bfloat16
AF = mybir.ActivationFunctionType
ALU = mybir.AluOpType

D = 2048
DK = 16
PW = 1024
DFF = 6144
EPS = 1e-6
NCORES = 8
HALO = 32
NPRE = 992
ENGS = ("pe", "act", "dve", "pool", "sp")

CV_GPRE1, CV_GPOST1, CV_GPRE2, CV_GPOST2 = 0, 16, 32, 48
CV_PSCALE = 64
CV_WLC = 72
CV_BLC = 136
CV_BRG = 152
CV_BIG = 168
CV_LAM = 184
CV_WFC = 200
CV_BFC = 488
CV_BADA = 584
NV = 680


class Prog:
    def __init__(self, nc, stack):
        self.nc = nc
        self.stack = stack
        self.q = {e: [] for e in ENGS}
        self.sem = {e: stack.enter_context(nc.semaphore("prog_" + e)) for e in ENGS}
        self.cnt = {e: 0 for e in ENGS}
        self.waited = {e: {} for e in ENGS}

    def new_sem(self, name):
        return self.stack.enter_context(self.nc.semaphore(name))

    def _waits(self, eng, deps):
        out = []
        for d in deps:
            if d is None:
                continue
            if isinstance(d, list):
                out.extend(self._waits(eng, d))
                continue
            s, v = d
            key = id(s)
            prev = self.waited[eng].get(key, 0)
            if v > prev:
                self.waited[eng][key] = v
                out.append((s, v))
        return out

    def op(self, eng, fn, deps=(), signal=True):
        w = self._waits(eng, deps)
        tok = None
        if signal:
            self.cnt[eng] += 1
            tok = (self.sem[eng], self.cnt[eng])
        self.q[eng].append((fn, w, tok))
        return tok

    def dma(self, eng, fn, sem, val, deps=()):
        w = self._waits(eng, deps)
        self.q[eng].append((fn, w, ("dma", sem)))
        return (sem, val)

    def cur(self, eng):
        if self.cnt[eng] == 0:
            return None
        return (self.sem[eng], self.cnt[eng])

    def wait_only(self, eng, deps):
        w = self._waits(eng, deps)
        if w:
            self.q[eng].append((None, w, None))

    def replay(self, block):
        def run(name):
            def body(e):
                for fn, waits, tok in self.q[name]:
                    for (s, v) in waits:
                        e.wait_ge(s, v)
                    if fn is None:
                        continue
                    ins = fn(e)
                    if tok is not None:
                        if tok[0] == "dma":
                            ins.then_inc(tok[1], 16)
                        else:
                            ins.then_inc(tok[0], 1)
            return body
        block.tensor(run("pe"))
        block.scalar(run("act"))
        block.vector(run("dve"))
        block.gpsimd(run("pool"))
        block.sync(run("sp"))


class PassCfg:
    def __init__(self, name, ntok, groups, samp, prm, halo, xtiles, lru_only, out_tiles):
        self.name = name
        self.ntok = ntok
        self.groups = groups
        self.samp = samp
        self.prm = prm
        self.halo = halo
        self.xtiles = xtiles
        self.lru_only = lru_only
        self.out_tiles = out_tiles


PASSES = [
    PassCfg("p0", 992, [(0, 496), (496, 992)], None, (0, 992), 0,
            [(0, 128, 0), (128, 128, 128), (256, 128, 256), (384, 128, 384), (512, 128, 512), (640, 128, 640),
             (768, 128, 768), (896, 96, 896)], True, []),
    PassCfg("p1", 608, [(0, 160), (160, 608)], (0, 128), (128, 608), HALO,
            [(2048, 128, 0), (992, 32, 128), (1024, 128, 160), (1152, 128, 288), (1280, 128, 416), (1408, 64, 544)],
            False, [(0, 1024, 128), (160, 0, 128), (288, 128, 128), (416, 256, 128), (544, 384, 64)]),
    PassCfg("p2", 576, [(0, 512), (512, 576)], None, (0, 576), 0,
            [(1472, 128, 0), (1600, 128, 128), (1728, 128, 256), (1856, 128, 384), (1984, 64, 512)],
            False, [(0, 448, 128), (128, 576, 128), (256, 704, 128), (384, 832, 128), (512, 960, 64)]),
]
NTMAX = 672


class Builder:
    def __init__(self, debug=False):
        self.debug = debug
        nc = bass.Bass("TRN2", target_bir_lowering=False)
        self.nc = nc
        di = lambda n, s: nc.dram_tensor(n, s, F32, kind="ExternalInput").ap()
        do = lambda n, s: nc.dram_tensor(n, s, F32, kind="ExternalOutput").ap()
        self.xq = di("xq", [2176, D])
        self.cT_d = di("cT", [128, DK, 17])
        self.cvec_d = di("cvec", [128, NV])
        self.sel_d = di("sel", [128, 1])
        self.invc_d = di("invc", [128, 4, 16])
        self.ident_d = di("ident", [128, 128])
        self.st_pool = di("st_pool", [16, 15, PW])
        self.st_lconv = di("st_lconv", [16, 3, D])
        self.st_lh = di("st_lh", [16, D])
        self.st_fconv = di("st_fconv", [16, 2, 2 * DFF])
        self.w_ada = di("w_ada", [D, 6 * D])
        self.w_in = di("w_in", [D, 7168])
        self.w_grp = di("w_pool_grp", [4, 256, 256])
        self.w_rg = di("w_rg", [8, 256, 256])
        self.w_ig = di("w_ig", [8, 256, 256])
        self.w_pup = di("w_pool_up", [PW, D])
        self.w_lup = di("w_lru_up", [D, D])
        self.w_out = di("w_out", [D, D])
        self.w_fup = di("w_ffn_up", [D, 2 * DFF])
        self.w_fdn = di("w_ffn_down", [DFF, D])
        self.y = do("y", [1152, D])
        self.o_pool_p = do("o_pool_p", [15, PW])
        self.o_lconv_p = do("o_lconv_p", [3, D])
        self.o_lh_p = do("o_lh_p", [1, D])
        self.o_fconv_p = do("o_fconv_p", [2, 2 * DFF])
        self.o_pool_s = do("o_pool_s", [16, 15, PW])
        self.o_lconv_s = do("o_lconv_s", [16, 3, D])
        self.o_lh_s = do("o_lh_s", [16, D])
        self.o_fconv_s = do("o_fconv_s", [16, 2, 2 * DFF])

    def sb(self, name, shape, dt):
        return self.st.enter_context(self.nc.sbuf_tensor("sb_" + name, shape, dt))

    def A(self, out, in_, func, scale=None, bias=None, deps=()):
        kw = {}
        if scale is not None:
            kw["scale"] = scale
        if bias is not None:
            kw["bias"] = bias
        return self.P.op("act", lambda e: e.activation(out=out, in_=in_, func=func, **kw), deps=deps)

    def Vtt(self, out, in0, in1, op, deps=(), eng="dve"):
        return self.P.op(eng, lambda e: e.tensor_tensor(out=out, in0=in0, in1=in1, op=op), deps=deps)

    def Vstt(self, out, in0, scalar, in1, op0, op1, deps=()):
        return self.P.op("dve", lambda e: e.scalar_tensor_tensor(out=out, in0=in0, scalar=scalar, in1=in1,
                                                                  op0=op0, op1=op1), deps=deps)

    def Vts(self, out, in0, s1, s2, op0, op1=None, deps=()):
        if op1 is None:
            return self.P.op("dve", lambda e: e.tensor_scalar(out=out, in0=in0, scalar1=s1, scalar2=None, op0=op0),
                             deps=deps)
        return self.P.op("dve", lambda e: e.tensor_scalar(out=out, in0=in0, scalar1=s1, scalar2=s2, op0=op0, op1=op1),
                         deps=deps)

    def Vcopy(self, out, in_, deps=()):
        return self.P.op("dve", lambda e: e.tensor_copy(out=out, in_=in_), deps=deps)

    def MM(self, out, lhsT, rhs, start, stop, deps=(), signal=False):
        return self.P.op("pe", lambda e: e.matmul(out, lhsT=lhsT, rhs=rhs, start=start, stop=stop),
                         deps=deps, signal=signal)

    def TR(self, out, in_, ident, deps=(), signal=False):
        return self.P.op("pe", lambda e: e.transpose(out, in_, ident), deps=deps, signal=signal)

    def barrier(self, extra=()):
        P = self.P
        toks = [P.cur(e) for e in ("pe", "act", "dve", "pool")] + list(extra) + self.so_all()
        for e in ("pe", "act", "dve"):
            P.wait_only(e, toks)
        self.last_barrier = toks
        return toks

    def alloc_bank(self):
        for _ in range(8):
            b = self.bank_next
            self.bank_next = (self.bank_next + 1) % 8
            if b not in self.bank_reserved:
                if b in self.bank_busy:
                    raise RuntimeError("PSUM bank %d re-allocated before release" % b)
                self.bank_busy.add(b)
                return b
        raise RuntimeError("no bank")

    def bank_ap(self, b, n, p=128):
        return self.ps[0:p, b, 0:n]

    def release_bank(self, b, toks):
        self.bank_free[b] = list(toks)
        self.bank_busy.discard(b)

    def wload(self, src, kch=16, ncol=256):
        i = self.w_next
        self.w_next = (i + 1) % len(self.wslots)
        slot = self.wslots[i]
        self.w_cnt[i] += 16
        dst = slot[:, 0:kch, 0:ncol]
        srcv = src.rearrange("(k p) n -> p k n", p=128)
        tok = self.P.dma("pool", lambda e: e.dma_start(out=dst, in_=srcv), self.w_sem[i], self.w_cnt[i],
                         deps=[self.w_rel[i]])
        return dst, tok, i

    def wrelease(self, i, tok):
        self.w_rel[i] = tok

    def job(self, groups, parts):
        banks = [self.alloc_bank() for _ in groups]
        n = len(parts)
        tok = None
        for idx, (lhsT, rhs_fn, deps) in enumerate(parts):
            for gi, (c0, c1) in enumerate(groups):
                b = banks[gi]
                d = list(deps)
                if idx == 0:
                    d += self.bank_free[b]
                last = (idx == n - 1 and gi == len(groups) - 1)
                t = self.MM(self.bank_ap(b, c1 - c0), lhsT, rhs_fn(c0, c1), idx == 0, idx == n - 1, deps=d,
                            signal=last)
                if last:
                    tok = t
        return banks, tok

    def build(self):
        nc = self.nc
        with contextlib.ExitStack() as st:
            self.st = st
            self.P = P = Prog(nc, st)
            self.ident = self.sb("ident", [128, 128], F32)
            self.ones = self.sb("ones", [128, 128], BF16)
            self.cvec = self.sb("cvec", [128, NV], F32)
            self.dv = self.sb("dv", [128, 4, 16], F32)
            self.mod = self.sb("mod", [128, 6, 16, 17], F32)
            self.sel = self.sb("sel", [128, 1], F32)
            self.invc = self.sb("invc", [128, 4, 16], F32)
            self.cT = self.sb("cT", [128, DK, 17], F32)
            self.sl = self.sb("sl", [128, DK, 17], BF16)
            self.wgrp = self.sb("wgrp", [128, 4, 2, 256], BF16)
            self.hist_pool = self.sb("hist_pool", [128, 8, 15], F32)
            self.hist_lru = self.sb("hist_lru", [128, 16, 3], F32)
            self.h_carry = self.sb("h_carry", [128, 16], F32)
            self.hist_up = self.sb("hist_up", [128, 96, 2], F32)
            self.rstd = self.sb("rstd", [128, NTMAX], F32)
            self.sq_scratch = self.sb("sqs", [128, NTMAX], F32)
            self.ada_tm_buf = self.sb("ada_tm", [128, 512], F32)
            self.R1 = self.sb("R1", [128, 16 * NTMAX], F32)
            self.R2 = self.sb("R2", [128, 16128], F32)
            self.R4 = self.sb("R4", [128, 16 * NTMAX], F32)
            NW = 4
            self.wslots = [self.sb("w%d" % i, [128, 16, 256], BF16) for i in range(NW)]
            self.w_sem = [P.new_sem("wsem%d" % i) for i in range(NW)]
            self.w_cnt = [0] * NW
            self.w_rel = [None] * NW
            self.w_next = 0
            self.wsm = [self.sb("wsm%d" % i, [128, 2, 2, 256], BF16) for i in range(2)]
            self.wsm_sem = [P.new_sem("wsmsem%d" % i) for i in range(2)]
            self.wsm_cnt = [0, 0]
            self.wsm_rel = [None, None]
            self.wsm_next = 0
            self.ps = st.enter_context(nc.psum_tensor("ps_all", [128, 8, 512], F32))
            self.bank_next = 0
            self.bank_reserved = set()
            self.bank_busy = set()
            self.bank_free = [[] for _ in range(8)]
            self.s_misc = P.new_sem("misc")
            self.misc_cnt = 0
            self.s_x = [P.new_sem("xs0"), P.new_sem("xs1"), P.new_sem("xs2")]
            self.x_cnt = [0, 0, 0]
            self.s_o = [P.new_sem("os0"), P.new_sem("os1")]
            self.o_cnt = [0, 0]
            self.s_so = P.new_sem("so")
            self.s_stf = [P.new_sem("stf0"), P.new_sem("stf1")]
            self.stf_cnt = [0, 0]
            self.so_cnt = 0
            self.so_streams = {}
            self.out_tokens = []
            self.last_barrier = []

            self.prologue()
            for cfg in PASSES:
                self.run_pass(cfg)
            self.epilogue()
            with nc.Block() as block:
                P.replay(block)
        return nc

    def misc_dma(self, eng, out, in_, deps=()):
        self.misc_cnt += 16
        return self.P.dma(eng, lambda e: e.dma_start(out=out, in_=in_), self.s_misc, self.misc_cnt, deps=deps)

    def so_tok(self, stream):
        st_ = self.so_streams.get(stream)
        return (st_[0], st_[1]) if st_ else None

    def so_all(self):
        return [(v[0], v[1]) for v in self.so_streams.values()]

    def so_dma(self, out, in_, deps=(), stream="main"):
        if stream not in self.so_streams:
            self.so_streams[stream] = [self.P.new_sem("so_" + stream), 0]
        st_ = self.so_streams[stream]
        st_[1] += 16
        self.so_cnt += 16
        t = self.P.dma("sp", lambda e: e.dma_start(out=out, in_=in_), st_[0], st_[1], deps=deps)
        return t

    def prologue(self):
        P = self.P
        cv = self.cvec
        lds = []
        lds.append(self.misc_dma("sp", self.ident[:], self.ident_d))
        lds.append(self.misc_dma("sp", self.cvec[:], self.cvec_d))
        lds.append(self.misc_dma("sp", self.sel[:], self.sel_d))
        lds.append(self.misc_dma("sp", self.invc[:], self.invc_d))
        lds.append(self.misc_dma("sp", self.cT[:], self.cT_d))
        ld = lds[-1]
        ld = (self.s_misc, self.misc_cnt)
        s_wg = P.new_sem("wgsem")
        wgv = self.w_grp.rearrange("g (k p) n -> p g k n", p=128)
        self.t_wgrp = P.dma("pool", lambda e: e.dma_start(out=self.wgrp[:], in_=wgv), s_wg, 16)
        t0 = P.op("dve", lambda e: e.memset(self.ones[:], 1.0))
        P.op("dve", lambda e: e.memset(self.hist_pool[:], 0.0))
        P.op("dve", lambda e: e.memset(self.hist_lru[:], 0.0))
        P.op("dve", lambda e: e.memset(self.h_carry[:], 0.0))
        self.t_init = P.op("dve", lambda e: e.memset(self.hist_up[:], 0.0))
        self.Vts(self.dv[:, 0, :], cv[:, CV_BRG:CV_BRG + 16], 0.5, None, ALU.mult, deps=[ld])
        self.Vts(self.dv[:, 1, :], cv[:, CV_BIG:CV_BIG + 16], 0.5, None, ALU.mult)
        ta = self.A(self.dv[:, 2, :], cv[:, CV_LAM:CV_LAM + 16], AF.Exp, scale=-1.0, deps=[ld])
        ta = self.A(self.dv[:, 2, :], self.dv[:, 2, :], AF.Ln, bias=1.0, deps=[ta])
        tv = self.Vts(self.dv[:, 2, :], self.dv[:, 2, :], -8.0, None, ALU.mult, deps=[ta])
        self.t_dv = self.Vts(self.dv[:, 3, :], self.dv[:, 2, :], 0.5, None, ALU.mult, deps=[tv])
        th = self.R4[:, 0:DK * 17].rearrange("p (k j) -> p k j", k=DK)
        ta = self.A(th, self.cT[:], AF.Tanh, scale=0.5, deps=[ld])
        t_sl = self.Vstt(self.sl[:], th, 1.0, self.cT[:], ALU.add, ALU.mult, deps=[ta])
        self.t_sl = t_sl
        self.t_ld = ld
        self.ada_tm_free = None
        self.ada_last = None
        self.ada_pending = list(range(8, 24))
        self.ada_mid_done = False
        for cb in range(8):
            self.ada_item(cb)
        self.ada_finalize([0, 1])
        self.barrier([ld, self.t_wgrp])


    def ada_item(self, cb):
        ada_tm = self.ada_tm_buf
        modf = self.mod[:].rearrange("p m k j -> p (m k) j")
        b = self.alloc_bank()
        tok = None
        for half in range(2):
            src = self.w_ada[:, cb * 512 + half * 256: cb * 512 + (half + 1) * 256]
            w, wt, wi = self.wload(src)
            for k in range(DK):
                d = [wt, self.t_sl] + (self.bank_free[b] if (k == 0 and half == 0) else [])
                tok = self.MM(self.ps[0:17, b, half * 256:(half + 1) * 256], self.sl[:, k, :], w[:, k, :],
                              k == 0, k == DK - 1, deps=d, signal=(k == DK - 1))
            self.wrelease(wi, tok)
        te = self.A(ada_tm[0:17, :], self.ps[0:17, b, :], AF.Copy, scale=0.5, deps=[tok, self.ada_tm_free])
        self.release_bank(b, [te])
        b2 = self.alloc_bank()
        tt = None
        for qq in range(4):
            d = [te, self.t_ld] + (self.bank_free[b2] if qq == 0 else [])
            tt = self.TR(self.ps[:, b2, qq * 17:(qq + 1) * 17], ada_tm[0:17, qq * 128:(qq + 1) * 128],
                         self.ident[0:17, 0:17], deps=d, signal=(qq == 3))
        self.ada_tm_free = tt
        te2 = self.Vcopy(modf[:, cb * 4:(cb + 1) * 4, :],
                         self.ps[:, b2, 0:68].rearrange("p (q j) -> p q j", q=4), deps=[tt])
        self.release_bank(b2, [te2])
        self.ada_last = te2

    def ada_finalize(self, ms):
        cv = self.cvec
        t = self.ada_last
        for m in ms:
            bada = cv[:, CV_BADA + 16 * m:CV_BADA + 16 * (m + 1)].unsqueeze(2).broadcast_to([128, 16, 17])
            t = self.Vtt(self.mod[:, m], self.mod[:, m], bada, ALU.add, deps=[t, self.t_ld])
            goff = {1: CV_GPRE1, 2: CV_GPOST1, 4: CV_GPRE2, 5: CV_GPOST2}.get(m)
            if goff is not None:
                gbc = cv[:, goff:goff + 16].unsqueeze(2).broadcast_to([128, 16, 17])
                if m in (1, 4):
                    t = self.Vstt(self.mod[:, m], self.mod[:, m], 1.0, gbc, ALU.add, ALU.mult, deps=[t])
                else:
                    t = self.Vtt(self.mod[:, m], self.mod[:, m], gbc, ALU.mult, deps=[t])
        self.t_mod = t

    def ada_drain(self):
        while self.ada_pending:
            self.ada_item(self.ada_pending.pop(0))
        self.ada_finalize([2, 3, 4, 5])

    def xT(self, cfg):
        if cfg.lru_only:
            return None
        return self.R1[:, 0:16 * cfg.ntok].rearrange("p (k n) -> p k n", k=16)

    def bfview(self, region, off_bytes, nch, ntok):
        o = off_bytes // 4
        n32 = nch * ntok // 2
        return region[:, o:o + n32].bitcast(BF16).rearrange("p (k n) -> p k n", k=nch)

    def f32view(self, region, off_bytes, nch, ntok):
        o = off_bytes // 4
        return region[:, o:o + nch * ntok].rearrange("p (k n) -> p k n", k=nch)

    def stage_load_x(self, cfg):
        P = self.P
        xT = self.xT(cfg)
        stg = [self.R2[:, 8064:8064 + 2048], self.R2[:, 8064 + 2048:8064 + 4096]]
        stg_free = [None, None]
        last = []
        for ti, (row0, nrows, col0) in enumerate(cfg.xtiles):
            s = ti % 2
            self.x_cnt[s] += 16
            dst = stg[s][0:nrows, :]
            src = self.xq[row0:row0 + nrows, :]
            tl = P.dma("sp", lambda e, dst=dst, src=src: e.dma_start(out=dst, in_=src), self.s_x[s], self.x_cnt[s],
                       deps=[stg_free[s]] + self.last_barrier)
            evs = []
            trs = None
            for g4 in range(4):
                b = self.alloc_bank()
                for qq in range(4):
                    k = g4 * 4 + qq
                    d = [tl] + (self.bank_free[b] if qq == 0 else [])
                    trs = self.TR(self.ps[:, b, qq * 128: qq * 128 + nrows], stg[s][0:nrows, k * 128:(k + 1) * 128],
                                  self.ident[0:nrows, 0:nrows], deps=d, signal=(qq == 3))
                src_ps = self.ps[:, b, :].rearrange("p (q n) -> p q n", q=4)[:, :, 0:nrows]
                dst_x = xT[:, g4 * 4:(g4 + 1) * 4, col0:col0 + nrows]
                if g4 % 2 == 0:
                    te = self.A(dst_x, src_ps, AF.Copy, deps=[trs])
                else:
                    te = self.Vcopy(dst_x, src_ps, deps=[trs])
                self.release_bank(b, [te])
                evs.append(te)
            stg_free[s] = trs
            last = evs
        return [P.cur("act"), P.cur("dve")]

    def stage_front(self, cfg, h):
        P = self.P
        xT = self.xT(cfg)
        NSLOT = 3
        stg = [self.R2[:, 8064 + i * 2048:8064 + (i + 1) * 2048] for i in range(NSLOT)]
        sqb = self.R4[:, 0:2048]
        ssb = [self.R4[:, 2048 + i:2049 + i] for i in range(NSLOT)]
        stg_free = [None] * NSLOT
        AX = mybir.AxisListType.X
        sqf = [None]
        tinfo = {}
        def phaseA(ti):
            row0, nrows, col0 = cfg.xtiles[ti]
            sq_free = sqf[0]
            s = ti % NSLOT
            self.x_cnt[s] += 16
            dst = stg[s][0:nrows, :]
            src = self.xq[row0:row0 + nrows, :]
            tl = P.dma("sp", lambda e, dst=dst, src=src: e.dma_start(out=dst, in_=src), self.s_x[s], self.x_cnt[s],
                       deps=[stg_free[s]] + self.last_barrier)
            tq = self.A(sqb[0:nrows, :], stg[s][0:nrows, :], AF.Square, deps=[tl, sq_free])
            ss = ssb[s][0:nrows, :]
            tr_ = P.op("dve", lambda e, ss=ss, nrows=nrows: e.reduce_sum(out=ss, in_=sqb[0:nrows, :], axis=AX),
                       deps=[tq, stg_free[s]])
            sq_free = tr_
            ta = self.A(ss, ss, AF.Sqrt, scale=1.0 / D, bias=EPS, deps=[tr_])
            trc = P.op("dve", lambda e, ss=ss: e.reciprocal(out=ss, in_=ss), deps=[ta])
            raw_done = []
            if not cfg.lru_only:
                for g4 in range(4):
                    b = self.alloc_bank()
                    trs = None
                    for qq in range(4):
                        k = g4 * 4 + qq
                        d = [tl, self.t_ld] + (self.bank_free[b] if qq == 0 else [])
                        trs = self.TR(self.ps[:, b, qq * 128: qq * 128 + nrows],
                                      stg[s][0:nrows, k * 128:(k + 1) * 128],
                                      self.ident[0:nrows, 0:nrows], deps=d, signal=(qq == 3))
                    src_ps = self.ps[:, b, :].rearrange("p (q n) -> p q n", q=4)[:, :, 0:nrows]
                    dst_x = xT[:, g4 * 4:(g4 + 1) * 4, col0:col0 + nrows]
                    if g4 % 2 == 0:
                        te = self.A(dst_x, src_ps, AF.Copy, deps=[trs])
                    else:
                        te = self.Vcopy(dst_x, src_ps, deps=[trs])
                    self.release_bank(b, [te])
                    raw_done = [trs]
            tsc = self.Vts(stg[s][0:nrows, :], stg[s][0:nrows, :], ss, None, ALU.mult, deps=[trc, tl] + raw_done)
            sqf[0] = sq_free
            tinfo[ti] = (s, nrows, col0, tsc)

        def phaseB(ti):
            s, nrows, col0, tsc = tinfo.pop(ti)
            is_samp = cfg.samp is not None and col0 == 0
            last_tr = None
            for g4 in range(4):
                b = self.alloc_bank()
                trs = None
                for qq in range(4):
                    k = g4 * 4 + qq
                    d = [tsc, self.t_ld] + (self.bank_free[b] if qq == 0 else [])
                    trs = self.TR(self.ps[:, b, qq * 128: qq * 128 + nrows], stg[s][0:nrows, k * 128:(k + 1) * 128],
                                  self.ident[0:nrows, 0:nrows], deps=d, signal=(qq == 3))
                last_tr = trs
                rel = []
                for qq in range(4):
                    k = g4 * 4 + qq
                    src_ps = self.ps[:, b, qq * 128: qq * 128 + nrows]
                    dst_h = h[:, k, col0:col0 + nrows]
                    if is_samp:
                        s3 = src_ps.rearrange("p (t s) -> p t s", t=8)
                        d3 = dst_h.rearrange("p (t s) -> p t s", t=8)
                        scb = self.mod[:, 1, k, 1:17].unsqueeze(1).broadcast_to([128, 8, 16])
                        shb = self.mod[:, 0, k, 1:17].unsqueeze(1).broadcast_to([128, 8, 16])
                        tmp3 = self.R4[:, 2056 + qq * 128:2056 + (qq + 1) * 128].rearrange("p (t s) -> p t s", t=8)
                        t1 = self.Vtt(tmp3, s3, scb, ALU.mult, deps=[trs, self.t_mod])
                        t2 = self.Vtt(d3, tmp3, shb, ALU.add, deps=[t1])
                        rel += [t1, t2]
                    elif g4 % 2 == 0:
                        rel.append(self.A(dst_h, src_ps, AF.Identity, scale=self.mod[:, 1, k, 0:1],
                                          bias=self.mod[:, 0, k, 0:1], deps=[trs, self.t_mod]))
                    else:
                        rel.append(self.Vts(dst_h, src_ps, self.mod[:, 1, k, 0:1], self.mod[:, 0, k, 0:1],
                                            ALU.mult, ALU.add, deps=[trs, self.t_mod]))
                self.release_bank(b, rel)
            stg_free[s] = last_tr
        nt_ = len(cfg.xtiles)
        phaseA(0)
        for ti in range(nt_):
            if ti + 1 < nt_:
                phaseA(ti + 1)
            phaseB(ti)
        toks = [P.cur("act"), P.cur("dve")]
        if cfg.halo:
            p0 = cfg.prm[0]
            hv = h[:, :, p0:p0 + cfg.halo]
            t = self.Vts(hv, hv, self.sel[:, 0:1], None, ALU.mult, deps=toks)
            toks = toks + [t]
        return toks

    def stage_stats(self, cfg, src, src_ready, pre_scale=1.0):
        P = self.P
        ntok = cfg.ntok
        sq = [self.R4[:, 0:ntok // 2].bitcast(BF16), self.R4[:, 512:512 + ntok // 2].bitcast(BF16)]
        sq_free = [None, None]
        banks = [self.alloc_bank() for _ in cfg.groups]
        for b in banks:
            self.bank_reserved.add(b)
        tok = None
        for k in range(DK):
            s = k % 2
            if k % 2 == 0:
                tq = self.A(sq[s][:, 0:ntok], src[:, k, :], AF.Square, deps=[src_ready, sq_free[s]])
            else:
                tq = self.Vtt(sq[s][:, 0:ntok], src[:, k, :], src[:, k, :], ALU.mult, deps=[src_ready, sq_free[s]])
            for gi, (c0, c1) in enumerate(cfg.groups):
                d = [tq] + (self.bank_free[banks[gi]] if k == 0 else [])
                tok = self.MM(self.bank_ap(banks[gi], c1 - c0), self.ones[:], sq[s][:, c0:c1], k == 0, k == DK - 1,
                              deps=d, signal=True)
            sq_free[s] = tok
        toks = []
        for gi, (c0, c1) in enumerate(cfg.groups):
            ta = self.A(self.rstd[:, c0:c1], self.bank_ap(banks[gi], c1 - c0), AF.Sqrt, scale=1.0 / D, bias=EPS,
                        deps=[tok])
            tv = P.op("dve", lambda e, c0=c0, c1=c1: e.reciprocal(out=self.rstd[:, c0:c1], in_=self.rstd[:, c0:c1]),
                      deps=[ta])
            self.release_bank(banks[gi], [ta])
            self.bank_reserved.discard(banks[gi])
            toks.append(tv)
        return toks

    def stage_normmod(self, cfg, src, dst, mi_shift, mi_scale, deps):
        P = self.P
        ntok = cfg.ntok
        p0, p1 = cfg.prm
        tmp = [self.R4[:, 1024:1024 + ntok], self.R4[:, 1024 + NTMAX:1024 + NTMAX + ntok]]
        tmp_free = [None, None]
        for k in range(DK):
            s = k % 2
            t1 = self.Vtt(tmp[s], src[:, k, :], self.rstd[:, 0:ntok], ALU.mult, deps=[deps, tmp_free[s]],
                          eng=("pool" if k % 2 == 1 else "dve"))
            ta = self.A(dst[:, k, p0:p1], tmp[s][:, p0:p1], AF.Identity, scale=self.mod[:, mi_scale, k, 0:1],
                        bias=self.mod[:, mi_shift, k, 0:1], deps=[t1, self.t_mod])
            rel = [ta]
            if cfg.samp is not None:
                v3 = tmp[s][:, 0:128].rearrange("p (t s) -> p t s", t=8)
                scb = self.mod[:, mi_scale, k, 1:17].unsqueeze(1).broadcast_to([128, 8, 16])
                shb = self.mod[:, mi_shift, k, 1:17].unsqueeze(1).broadcast_to([128, 8, 16])
                t2 = self.Vtt(v3, v3, scb, ALU.mult, deps=[t1, self.t_mod])
                t3 = self.Vtt(dst[:, k, 0:128].rearrange("p (t s) -> p t s", t=8), v3, shb, ALU.add, deps=[t2])
                rel.append(t3)
            tmp_free[s] = rel
        toks = [P.cur("act"), P.cur("dve"), P.cur("pool")]
        if cfg.halo:
            hv = dst[:, :, p0:p0 + cfg.halo]
            t = self.Vts(hv, hv, self.sel[:, 0:1], None, ALU.mult, deps=toks)
            toks = [t]
        return toks

    def stage_resid(self, cfg, acc, mi_gate, deps, fuse_stats=False):
        P = self.P
        xT = self.xT(cfg)
        ntok = cfg.ntok
        p0, p1 = cfg.prm
        if fuse_stats:
            sq = [self.R4[:, 0:ntok // 2].bitcast(BF16), self.R4[:, 512:512 + ntok // 2].bitcast(BF16)]
            sq_free = [None, None]
            banks = [self.alloc_bank() for _ in cfg.groups]
            for b in banks:
                self.bank_reserved.add(b)
            tok = None
        for k in range(DK):
            t1 = self.Vtt(acc[:, k, :], acc[:, k, :], self.rstd[:, 0:ntok], ALU.mult, deps=[deps],
                          eng=("pool" if k % 2 == 1 else "dve"))
            done = [self.Vstt(xT[:, k, p0:p1], acc[:, k, p0:p1], self.mod[:, mi_gate, k, 0:1], xT[:, k, p0:p1],
                              ALU.mult, ALU.add, deps=[t1, self.t_mod])]
            if cfg.samp is not None:
                v3 = acc[:, k, 0:128].rearrange("p (t s) -> p t s", t=8)
                gtb = self.mod[:, mi_gate, k, 1:17].unsqueeze(1).broadcast_to([128, 8, 16])
                t2 = self.Vtt(v3, v3, gtb, ALU.mult, deps=[t1, self.t_mod])
                x3 = xT[:, k, 0:128].rearrange("p (t s) -> p t s", t=8)
                done.append(self.Vtt(x3, x3, v3, ALU.add, deps=[t2]))
            if fuse_stats:
                s_ = k % 2
                tq = self.A(sq[s_][:, 0:ntok], xT[:, k, :], AF.Square, deps=done + [sq_free[s_]])
                for gi, (c0, c1) in enumerate(cfg.groups):
                    d = [tq] + (self.bank_free[banks[gi]] if k == 0 else [])
                    tok = self.MM(self.bank_ap(banks[gi], c1 - c0), self.ones[:], sq[s_][:, c0:c1], k == 0,
                                  k == DK - 1, deps=d, signal=True)
                sq_free[s_] = tok
        if not fuse_stats:
            return [P.cur("dve"), P.cur("pool")]
        toks = []
        last_dve = [P.cur("dve"), P.cur("pool")]
        for gi, (c0, c1) in enumerate(cfg.groups):
            ta = self.A(self.rstd[:, c0:c1], self.bank_ap(banks[gi], c1 - c0), AF.Sqrt, scale=1.0 / D, bias=EPS,
                        deps=[tok, last_dve])
            tv = P.op("dve", lambda e, c0=c0, c1=c1: e.reciprocal(out=self.rstd[:, c0:c1], in_=self.rstd[:, c0:c1]),
                      deps=[ta])
            self.release_bank(banks[gi], [ta])
            self.bank_reserved.discard(banks[gi])
            toks.append(tv)
        return toks

    def ext_layout(self, cfg, H):
        if cfg.samp is not None:
            hs = H * 16
            return hs + cfg.ntok, hs, hs + 128, hs
        return H + cfg.ntok, H, H, None

    def run_pass(self, cfg):
        P = self.P
        nt = cfg.ntok
        p0, p1 = cfg.prm
        Lp = p1 - p0
        groups = cfg.groups
        cv = self.cvec
        xT = self.xT(cfg)
        h = self.bfview(self.R2, 0, 16, nt)
        if cfg.lru_only:
            th = self.stage_front(cfg, h)
            self.barrier()
            self.stage_lru(cfg, h, th, None)
            self.barrier()
            return
        y_pool = self.bfview(self.R2, 21504, 8, nt)
        y_lru = self.bfview(self.R2, 32256, 16, nt)
        o_sb = self.f32view(self.R2, 0, 16, nt)
        merged = self.bfview(self.R4, 0, 16, nt)
        h2 = self.bfview(self.R4, 21504, 16, nt)
        d_sb = self.f32view(self.R4, 0, 16, nt)
        f = self.bfview(self.R2, 0, 48, nt)

        th = self.stage_front(cfg, h)
        self.barrier()
        h_ready = th

        if not cfg.lru_only:
            self.stage_pool(cfg, h, h_ready, y_pool)
            self.barrier()
        self.stage_lru(cfg, h, h_ready, y_lru)
        self.barrier()
        if cfg.lru_only:
            return
        self.stage_merge(cfg, h, y_pool, y_lru, merged)
        self.barrier()
        self.stage_proj_norm(cfg, merged, self.w_out, 16, o_sb, 0.5)
        if self.ada_pending is not None and not self.ada_mid_done:
            while self.ada_pending and self.ada_pending[0] < 20:
                self.ada_item(self.ada_pending.pop(0))
            self.ada_finalize([2, 3, 4])
            self.ada_mid_done = True
        self.barrier()
        ts = self.stage_resid(cfg, o_sb, 2, [P.cur("dve"), P.cur("act")], fuse_stats=True)
        th2 = self.stage_normmod(cfg, xT, h2, 3, 4, ts)
        self.barrier()
        self.stage_ffn_up(cfg, h2, f)
        if self.ada_pending is not None:
            while self.ada_pending:
                self.ada_item(self.ada_pending.pop(0))
            self.ada_finalize([5])
            self.ada_pending = None
        self.barrier(self.so_all())
        self.stage_proj_norm(cfg, f, self.w_fdn, 48, d_sb, 1.0)
        self.barrier()
        self.stage_resid(cfg, d_sb, 5, [P.cur("dve"), P.cur("act")])
        self.barrier()
        self.stage_store_y(cfg)
        self.barrier(self.out_tokens)

    def stage_pool(self, cfg, h, h_ready, y_pool):
        P = self.P
        nt = cfg.ntok
        p0, p1 = cfg.prm
        Lp = p1 - p0
        cv = self.cvec
        W, cur, prm, scur = self.ext_layout(cfg, 15)
        def u_ap(c, i):
            o = (c * 3 + i) * 928
            return self.R4[:, o:o + W]
        dbuf = [self.R4[:, 6 * 928 + c * 336: 6 * 928 + c * 336 + nt // 2].bitcast(BF16) for c in range(2)]
        sstage = self.R4[:, 7256:7256 + 2048]
        t_hl = None
        if cfg.samp is not None:
            for r in range(15):
                tno, rr = (0, r) if r < 8 else (1, r - 8)
                self.misc_dma("sp", sstage[rr * 16:(rr + 1) * 16, tno * 1024:(tno + 1) * 1024], self.st_pool[:, r, :],
                              deps=self.last_barrier)
            t_hl = (self.s_misc, self.misc_cnt)
        fix = self.R4[:, 6 * 928 + 2 * 336 + 512: 6 * 928 + 2 * 336 + 512 + 16]
        so_stage = self.R4[:, 7000:7000 + 256]
        prev_done = None
        for g in range(4):
            w = 2 ** (g + 1)
            if self.ada_pending and cfg.samp is not None and self.ada_pending[0] < 16:
                self.ada_item(self.ada_pending.pop(0))
            wsl, wt, wi = self.wload(self.w_in[:, g * 256:(g + 1) * 256])
            dtoks = []
            hist_b = []
            if cfg.samp is not None:
                for c in range(2):
                    b = self.alloc_bank()
                    tt = None
                    for tno, nrow in enumerate((128, 112)):
                        d = [t_hl] + (self.bank_free[b] if tno == 0 else [])
                        tt = self.TR(self.ps[:, b, tno * 128: tno * 128 + nrow],
                                     sstage[0:nrow, tno * 1024 + (2 * g + c) * 128: tno * 1024 + (2 * g + c + 1) * 128],
                                     self.ident[0:nrow, 0:nrow], deps=d, signal=(tno == 1))
                    hist_b.append((b, tt))
            zjobs = []
            for c in range(2):
                banks, tok = self.job(cfg.groups, [(wsl[:, k, c * 128:(c + 1) * 128],
                                                    (lambda c0, c1, k=k: h[:, k, c0:c1]), [wt, h_ready])
                                                   for k in range(DK)])
                zjobs.append((banks, tok))
            self.wrelease(wi, zjobs[1][1])
            evs_c = []
            for c in range(2):
                ch = 2 * g + c
                U = u_ap(c, 0)
                banks, tok = zjobs[c]
                evs = []
                for gi, (c0, c1) in enumerate(cfg.groups):
                    evs.append(self.A(U[:, cur + c0:cur + c1], self.bank_ap(banks[gi], c1 - c0), AF.Copy,
                                      deps=[tok, prev_done]))
                    self.release_bank(banks[gi], [evs[-1]])
                if cfg.samp is not None:
                    b, tt = hist_b[c]
                    tevh = self.Vcopy(U[:, 0:240], self.ps[:, b, 0:240], deps=[tt, prev_done])
                    self.release_bank(b, [tevh])
                    evs.append(tevh)
                else:
                    evs.append(self.Vcopy(U[:, 0:15], self.hist_pool[:, ch, :], deps=[prev_done, self.t_init]))
                evs_c.append(evs)
                st_ = 16 if cfg.samp is not None else 1
                regions = []
                if cfg.samp is not None:
                    regions.append((0, 240 + 128, 16, 240))
                    regions.append((prm - 15, W, 1, prm))
                else:
                    regions.append((0, W, 1, cur))
                bufs = [U, u_ap(c, 1), u_ap(c, 2)]
                tlast = evs
                for (r0, r1, strd, fo) in regions:
                    srcb = U
                    di = 1
                    sh = 1
                    tl = tlast
                    lo = r0
                    while sh < w:
                        dstb = bufs[di]
                        lo2 = lo + sh * strd
                        tl = [self.Vtt(dstb[:, lo2:r1], srcb[:, lo2:r1], srcb[:, lo2 - sh * strd:r1 - sh * strd],
                                       ALU.add, deps=tl)]
                        srcb = dstb
                        di = 2 if di == 1 else 1
                        lo = lo2
                        sh *= 2
                    n = r1 - fo
                    dcol = 0 if (cfg.samp is not None and strd == 16) else p0
                    td = self.Vstt(dbuf[c][:, dcol:dcol + n], srcb[:, fo:r1], 1.0 / w, U[:, fo:r1], ALU.mult,
                                   ALU.subtract, deps=tl)
                    dtoks.append(td)
                    tlast = evs + [td]
                    if cfg.halo and strd == 1:
                        m0 = fo + cfg.halo
                        tf = self.Vtt(fix, srcb[:, m0:m0 + 16], self.invc[:, g, :], ALU.mult, deps=tl)
                        td2 = self.Vtt(dbuf[c][:, p0 + cfg.halo:p0 + cfg.halo + 16], fix, U[:, m0:m0 + 16],
                                       ALU.subtract, deps=[tf, td])
                        dtoks.append(td2)
                tsv = self.A(self.hist_pool[:, ch, :], U[:, W - 15:W], AF.Copy, deps=evs)
                dtoks.append(tsv)
            if cfg.samp is not None:
                for c in range(2):
                    U = u_ap(c, 0)
                    b = self.alloc_bank()
                    tt = self.TR(self.ps[:, b, 0:128], U[:, 240:368], self.ident[:],
                                 deps=evs_c[c] + self.bank_free[b], signal=True)
                    te = self.A(so_stage[:, c * 128:(c + 1) * 128], self.ps[:, b, 0:128], AF.Copy,
                                deps=[tt, prev_done])
                    self.release_bank(b, [te])
                    dtoks += [te, tt]
                so_toks = []
                for t in range(8):
                    so_toks.append(self.so_dma(self.o_pool_s[:, 7 + t, g * 256:(g + 1) * 256],
                                               so_stage[t * 16:(t + 1) * 16, :], deps=dtoks, stream="pool"))
                t_so = self.so_tok("pool")
            ytoks = []
            for j in range(2):
                banks, tok = self.job(cfg.groups, [(self.wgrp[:, g, kk, j * 128:(j + 1) * 128],
                                                    (lambda c0, c1, kk=kk: dbuf[kk][:, c0:c1]),
                                                    [self.t_wgrp] + dtoks) for kk in range(2)])
                for gi, (c0, c1) in enumerate(cfg.groups):
                    te = self.A(y_pool[:, 2 * g + j, c0:c1], self.bank_ap(banks[gi], c1 - c0), AF.Copy,
                                scale=cv[:, CV_PSCALE + 2 * g + j:CV_PSCALE + 2 * g + j + 1], deps=[tok])
                    self.release_bank(banks[gi], [te])
                    ytoks.append(te)
            prev_done = [P.cur("pe"), P.cur("dve"), P.cur("act")]
            if cfg.samp is not None:
                prev_done = prev_done + [t_so]
        if cfg.samp is not None:
            self.so_dma(self.o_pool_s[:, 0:7, :], self.st_pool[:, 8:15, :], stream="carry")
            self.out_tokens += self.so_all()

    def wsm_load(self, blk):
        i = self.wsm_next
        self.wsm_next = (i + 1) % 2
        self.wsm_cnt[i] += 32
        P = self.P
        dst0 = self.wsm[i][:, 0]
        dst1 = self.wsm[i][:, 1]
        s0 = self.w_rg[blk].rearrange("(k p) n -> p k n", p=128)
        s1 = self.w_ig[blk].rearrange("(k p) n -> p k n", p=128)
        P.dma("pool", lambda e: e.dma_start(out=dst0, in_=s0), self.wsm_sem[i], 0, deps=[self.wsm_rel[i]])
        tok = P.dma("pool", lambda e: e.dma_start(out=dst1, in_=s1), self.wsm_sem[i], self.wsm_cnt[i])
        return self.wsm[i], tok, i

    def stage_lru(self, cfg, h, h_ready, y_lru):
        P = self.P
        nt = cfg.ntok
        p0, p1 = cfg.prm
        Lp = p1 - p0
        cv = self.cvec
        W, cur, prm, scur = self.ext_layout(cfg, 3)
        samp = cfg.samp is not None
        big = nt > NTMAX
        NTP = 992 if big else NTMAX
        UW = 1000 if big else 768
        CS, GSZ = UW + NTP, 3 * NTP
        def U_ap(s_, c):
            o = (s_ * 2 + c) * CS
            return self.R4[:, o:o + W]
        def xc_ap(s_, c):
            o = (s_ * 2 + c) * CS + UW
            return self.R4[:, o:o + nt]
        def ap3(base, stride):
            return bass.AP(base.tensor, base.offset, [list(base.ap[0]), [stride, 2], [1, nt]])
        def xc3(s_):
            return ap3(xc_ap(s_, 0), CS)
        NGS = 2 if cfg.lru_only else 1
        gs_cur = [0]
        def g_ap(c, i):
            if big:
                reg, base = (self.R1, 0) if gs_cur[0] == 0 else (self.R2, 8064)
                o = base + c * GSZ + i * NTP
                return reg[:, o:o + nt]
            if gs_cur[0] == 1:
                o = c * GSZ + i * NTP
                return self.R1[:, o:o + nt]
            o = 5760 + c * GSZ + i * NTP
            return self.R4[:, o:o + nt]
        def g3(i):
            return ap3(g_ap(0, i), GSZ)
        XH = NTP // 2
        if big:
            xcb_t = [self.R1[:, 5952:5952 + 2 * XH], self.R1[:, 5952 + 2 * XH:5952 + 4 * XH]]
        else:
            xcb_t = [self.rstd, self.sq_scratch]
        def xcb_ap(s_, c):
            return xcb_t[s_][:, c * XH:c * XH + nt // 2].bitcast(BF16)
        def xcb3(s_):
            return xcb_t[s_][:, 0:2 * XH].bitcast(BF16).rearrange("p (c n) -> p c n", c=2)[:, :, 0:nt]
        stage_start = list(self.last_barrier)
        if samp:
            st_l = self.R2[:, 13440:13440 + 2048]
            for r in range(3):
                self.misc_dma("sp", st_l[r * 16:(r + 1) * 16, :], self.st_lconv[:, r, :], deps=stage_start)
            self.misc_dma("sp", st_l[48:64, :], self.st_lh, deps=stage_start)
            t_stl = (self.s_misc, self.misc_cnt)
            h0s = self.R2[:, 15488:15488 + 256].rearrange("p (k s) -> p k s", k=16)
            so_conv = self.R2[:, 15744:15744 + 256]
        st = {"conv_free": [None, None], "xcb_free": [None, None], "gate_free": [None, None], "so1": None, "so2": None,
              "s1": {}}

        def S1A(blk):
            s_ = blk % 2
            if self.ada_pending:
                if cfg.lru_only and blk % 2 == 0 and self.ada_pending[0] < 12:
                    self.ada_item(self.ada_pending.pop(0))
                elif samp and blk % 2 == 0 and self.ada_pending[0] < 20:
                    self.ada_item(self.ada_pending.pop(0))
            wsl, wt, wi = self.wload(self.w_in[:, 1024 + blk * 256:1024 + (blk + 1) * 256])
            wg, wgt, wgi = self.wsm_load(blk)
            cfree = st["conv_free"][s_]
            tap0 = []
            evs_c = []
            hist_toks = []
            hist_tr = []
            if samp:
                for c in range(2):
                    ch = 2 * blk + c
                    b = self.alloc_bank()
                    tt = self.TR(self.ps[:, b, 0:64], st_l[0:64, ch * 128:(ch + 1) * 128], self.ident[0:64, 0:64],
                                 deps=[t_stl] + self.bank_free[b], signal=True)
                    hist_tr.append((b, tt))
            jobs = []
            for c in range(2):
                banks, tok = self.job(cfg.groups, [(wsl[:, k, c * 128:(c + 1) * 128],
                                                    (lambda c0, c1, k=k: h[:, k, c0:c1]), [wt, h_ready])
                                                   for k in range(DK)])
                jobs.append((banks, tok))
            self.wrelease(wi, jobs[1][1])
            for c in range(2):
                ch = 2 * blk + c
                U = U_ap(s_, c)
                xc = xc_ap(s_, c)
                banks, tok = jobs[c]
                evs = []
                if samp:
                    b, tt = hist_tr[c]
                    te1 = self.Vcopy(U[:, 0:48], self.ps[:, b, 0:48], deps=[tt, cfree])
                    te2 = self.Vcopy(h0s[:, ch, :], self.ps[:, b, 48:64], deps=[tt])
                    self.release_bank(b, [te1, te2])
                    evs += [te1, te2]
                else:
                    evs.append(self.Vcopy(U[:, 0:3], self.hist_lru[:, ch, :], deps=[cfree, self.t_init]))
                for gi, (c0, c1) in enumerate(cfg.groups):
                    evs.append(self.A(U[:, cur + c0:cur + c1], self.bank_ap(banks[gi], c1 - c0), AF.Copy,
                                      deps=[tok, cfree]))
                    self.release_bank(banks[gi], [evs[-1]])
                wl3 = cv[:, CV_WLC + 48 + ch:CV_WLC + 48 + ch + 1]
                bl = cv[:, CV_BLC + ch:CV_BLC + ch + 1]
                regs = []
                if samp:
                    regs.append((48, 128, 16, 0))
                    regs.append((prm, Lp, 1, p0))
                else:
                    regs.append((cur, Lp, 1, p0))
                t0s = []
                for (co, n, strd, xo) in regs:
                    t0s.append(self.A(xc[:, xo:xo + n], U[:, co:co + n], AF.Identity, scale=wl3, bias=bl,
                                      deps=evs + [cfree]))
                tap0.append((regs, t0s))
                evs_c.append(evs)
                hist_toks.append(self.A(self.hist_lru[:, ch, :], U[:, W - 3:W], AF.Copy, deps=evs))
            if samp:
                for c in range(2):
                    U = U_ap(s_, c)
                    b = self.alloc_bank()
                    tt = self.TR(self.ps[0:48, b, 0:128], U[:, 128:176], self.ident[:],
                                 deps=evs_c[c] + self.bank_free[b], signal=True)
                    te = self.A(so_conv[0:48, c * 128:(c + 1) * 128], self.ps[0:48, b, 0:128], AF.Copy,
                                deps=[tt, st["so1"]])
                    self.release_bank(b, [te])
                    hist_toks += [te, tt]
                for r in range(3):
                    self.so_dma(self.o_lconv_s[:, r, blk * 256:(blk + 1) * 256], so_conv[r * 16:(r + 1) * 16, :],
                                deps=hist_toks, stream="lru1")
                st["so1"] = self.so_tok("lru1")
            st["s1"][blk] = dict(wg=wg, wgt=wgt, wgi=wgi, tap0=tap0, evs=evs_c, ureaders=list(hist_toks))

        def S1B(blk):
            s_ = blk % 2
            info = st["s1"][blk]
            ctoks_all = []
            for c in range(2):
                ch = 2 * blk + c
                U = U_ap(s_, c)
                xc = xc_ap(s_, c)
                regs, t0s = info["tap0"][c]
                for ri, (co, n, strd, xo) in enumerate(regs):
                    t = t0s[ri]
                    for k in range(3):
                        sh = (3 - k) * strd
                        t = self.Vstt(xc[:, xo:xo + n], U[:, co - sh:co - sh + n],
                                      cv[:, CV_WLC + 16 * k + ch:CV_WLC + 16 * k + ch + 1], xc[:, xo:xo + n],
                                      ALU.mult, ALU.add, deps=[t] + info["evs"][c])
                    ctoks_all.append(t)
            tb = self.Vcopy(xcb3(s_), xc3(s_), deps=ctoks_all + [st["xcb_free"][s_]])
            info["tb"] = tb
            info["ureaders"] += ctoks_all

        def P1(blk):
            s_ = blk % 2
            info = st["s1"][blk]
            wg, wgt, wgi, tb = info["wg"], info["wgt"], info["wgi"], info["tb"]
            gs_cur[0] = blk % NGS
            gfree = st["gate_free"][blk % NGS]
            tanh_r, tanh_i = [], []
            lastpe = None
            for j in range(2):
                ch = 2 * blk + j
                a = g_ap(j, 0)
                g = g_ap(j, 2)
                for (gi_, dst, brow, lst) in ((0, a, 0, tanh_r), (1, g, 1, tanh_i)):
                    banks, tok = self.job(cfg.groups, [(wg[:, gi_, kk, j * 128:(j + 1) * 128],
                                                        (lambda c0, c1, kk=kk: xcb_ap(s_, kk)[:, c0:c1]), [wgt, tb])
                                                       for kk in range(2)])
                    lastpe = tok
                    for gi, (c0, c1) in enumerate(cfg.groups):
                        t = self.A(dst[:, c0:c1], self.bank_ap(banks[gi], c1 - c0), AF.Tanh, scale=0.5,
                                   bias=self.dv[:, brow, ch:ch + 1], deps=[tok, gfree, self.t_dv])
                        self.release_bank(banks[gi], [t])
                        lst.append(t)
            self.wsm_rel[wgi] = lastpe
            st["xcb_free"][s_] = lastpe
            ta = []
            for j in range(2):
                ch = 2 * blk + j
                a = g_ap(j, 0)
                ta.append(self.A(a, a, AF.Exp, scale=self.dv[:, 3, ch:ch + 1], bias=self.dv[:, 3, ch:ch + 1],
                                 deps=tanh_r))
            tgx = self.Vstt(g3(2), g3(2), 1.0, xc3(s_), ALU.add, ALU.mult, deps=tanh_i + [tb])
            st["conv_free"][s_] = info["ureaders"] + [tgx, tb]
            tm = self.A(g3(1), g3(0), AF.Square, deps=ta + [gfree])
            info.update(ta=ta, tgx=tgx, tm=tm)

        def P2a(blk):
            gs_cur[0] = blk % NGS
            info = st["s1"][blk]
            info["tsq"] = self.A(g3(1), g3(1), AF.Sqrt, scale=-0.25, bias=0.25, deps=[info["tm"]])

        def P2(blk):
            gs_cur[0] = blk % NGS
            info = st["s1"][blk]
            ta, tgx, tsq = info["ta"], info["tgx"], info["tsq"]
            tuu = self.Vtt(g3(2), g3(2), g3(1), ALU.mult, deps=[tgx, tsq])
            fin = []
            for j in range(2):
                ch = 2 * blk + j
                a, hs, g = g_ap(j, 0), g_ap(j, 1), g_ap(j, 2)
                hc = self.h_carry[:, ch:ch + 1]
                if cfg.halo:
                    t1 = P.op("dve", lambda e, a=a, g=g, hs=hs, hc=hc: e.tensor_tensor_scan(
                        out=hs[:, p0:p0 + HALO], data0=a[:, p0:p0 + HALO], data1=g[:, p0:p0 + HALO], initial=hc,
                        op0=ALU.mult, op1=ALU.add), deps=[tuu] + ta + [self.t_init])
                    hm = self.R2[:, 16100 + j:16101 + j]
                    t2 = self.Vts(hm, hs[:, p0 + HALO - 1:p0 + HALO], self.sel[:, 0:1], None, ALU.mult, deps=[t1])
                    t3 = P.op("dve", lambda e, a=a, g=g, hs=hs, hm=hm: e.tensor_tensor_scan(
                        out=hs[:, p0 + HALO:p1], data0=a[:, p0 + HALO:p1], data1=g[:, p0 + HALO:p1], initial=hm,
                        op0=ALU.mult, op1=ALU.add), deps=[t2])
                else:
                    t3 = P.op("dve", lambda e, a=a, g=g, hs=hs, hc=hc: e.tensor_tensor_scan(
                        out=hs[:, p0:p1], data0=a[:, p0:p1], data1=g[:, p0:p1], initial=hc,
                        op0=ALU.mult, op1=ALU.add), deps=[tuu] + ta + [self.t_init])
                t4 = self.Vcopy(hc, hs[:, p1 - 1:p1], deps=[t3])
                fin += [t3, t4]
            if samp:
                a3, hs3, gg3 = g3(0), g3(1), g3(2)
                prev = h0s[:, 2 * blk:2 * blk + 2, :]
                t = [tuu] + ta
                for tstep in range(8):
                    sl_ = slice(tstep * 16, (tstep + 1) * 16)
                    t = [self.Vtt(hs3[:, :, sl_], a3[:, :, sl_], prev, ALU.mult, deps=t)]
                    t = [self.Vtt(hs3[:, :, sl_], hs3[:, :, sl_], gg3[:, :, sl_], ALU.add, deps=t)]
                    prev = hs3[:, :, sl_]
                fin += t
            info["fin"] = fin

        def P3(blk):
            gs_cur[0] = blk % NGS
            info = st["s1"].pop(blk)
            fin = info["fin"]
            if not cfg.lru_only:
                ty = self.A(y_lru[:, 2 * blk:2 * blk + 2, :], g3(1), AF.Copy, deps=fin)
                fin.append(ty)
            if samp:
                so_t = []
                for j in range(2):
                    hs = g_ap(j, 1)
                    b = self.alloc_bank()
                    tt = self.TR(self.ps[0:16, b, 0:128], hs[:, 112:128], self.ident[:],
                                 deps=fin + self.bank_free[b], signal=True)
                    te = self.A(so_conv[64:80, j * 128:(j + 1) * 128], self.ps[0:16, b, 0:128], AF.Copy,
                                deps=[tt, st["so2"]])
                    self.release_bank(b, [te])
                    so_t += [te, tt]
                self.so_dma(self.o_lh_s[:, blk * 256:(blk + 1) * 256], so_conv[64:80, :], deps=so_t, stream="lru2")
                st["so2"] = self.so_tok("lru2")
                fin += so_t
            st["gate_free"][blk % NGS] = fin

        S1A(0)
        S1B(0)
        S1A(1)
        S1B(1)
        for blk in range(8):
            P1(blk)
            P2a(blk)
            if blk + 2 < 8:
                S1A(blk + 2)
            P2(blk)
            if blk + 2 < 8:
                S1B(blk + 2)
            P3(blk)
        if samp:
            self.out_tokens += self.so_all()

    def stage_merge(self, cfg, h, y_pool, y_lru, merged):
        P = self.P
        nt = cfg.ntok
        base = 5376
        gb = [self.R4[:, base + i * NTMAX: base + i * NTMAX + nt] for i in range(4)]
        prev_done = None
        for q in range(8):
            specs = [(self.w_in[:, 3072 + q * 256:3072 + (q + 1) * 256], 16, h, 0),
                     (self.w_in[:, 5120 + q * 256:5120 + (q + 1) * 256], 16, h, 2)]
            for (src, kch, opnd, bi) in specs:
                wsl, wt, wi = self.wload(src, kch)
                for j in range(2):
                    banks, tok = self.job(cfg.groups, [(wsl[:, k, j * 128:(j + 1) * 128],
                                                        (lambda c0, c1, k=k, opnd=opnd: opnd[:, k, c0:c1]), [wt])
                                                       for k in range(kch)])
                    if j == 1:
                        self.wrelease(wi, tok)
                    for gi, (c0, c1) in enumerate(cfg.groups):
                        t = self.A(gb[bi + j][:, c0:c1], self.bank_ap(banks[gi], c1 - c0), AF.Tanh, scale=0.5,
                                   deps=[tok, prev_done])
                        self.release_bank(banks[gi], [t])
            tg = P.cur("act")
            specs = [(self.w_pup[:, q * 256:(q + 1) * 256], 8, y_pool, 0),
                     (self.w_lup[:, q * 256:(q + 1) * 256], 16, y_lru, 2)]
            for (src, kch, opnd, bi) in specs:
                wsl, wt, wi = self.wload(src, kch)
                for j in range(2):
                    banks, tok = self.job(cfg.groups, [(wsl[:, k, j * 128:(j + 1) * 128],
                                                        (lambda c0, c1, k=k, opnd=opnd: opnd[:, k, c0:c1]), [wt])
                                                       for k in range(kch)])
                    if j == 1:
                        self.wrelease(wi, tok)
                    for gi, (c0, c1) in enumerate(cfg.groups):
                        t = self.Vstt(gb[bi + j][:, c0:c1], gb[bi + j][:, c0:c1], 1.0,
                                      self.bank_ap(banks[gi], c1 - c0), ALU.add, ALU.mult, deps=[tok, tg])
                        self.release_bank(banks[gi], [t])
            tv = P.cur("dve")
            for j in range(2):
                self.Vtt(merged[:, 2 * q + j, :], gb[j][:, 0:nt], gb[2 + j][:, 0:nt], ALU.add, deps=[tv])
            prev_done = [P.cur("dve")]

    def stage_proj_norm(self, cfg, opnd, wsrc, kchunks, acc, evac_scale):
        P = self.P
        nt = cfg.ntok
        nparts = kchunks // 16
        sqb = [self.sq_scratch[:, i * 336:i * 336 + nt // 2].bitcast(BF16) for i in range(2)]
        sq_free = [None, None]
        sbanks = [self.alloc_bank() for _ in cfg.groups]
        for b in sbanks:
            self.bank_reserved.add(b)
        pend = None
        stok = None
        nsq = 0

        def flush(pend, first, last):
            (tq, s) = pend
            tk = None
            for gi, (c0, c1) in enumerate(cfg.groups):
                d = [tq] + (self.bank_free[sbanks[gi]] if first else [])
                tk = self.MM(self.bank_ap(sbanks[gi], c1 - c0), self.ones[:], sqb[s][:, c0:c1], first, last,
                             deps=d, signal=True)
            sq_free[s] = tk
            return tk

        for cb in range(8):
            open_jobs = []
            for j in range(2):
                open_jobs.append([self.alloc_bank() for _ in cfg.groups])
            toks = [None, None]
            for kp in range(nparts):
                src = wsrc[kp * 2048:(kp + 1) * 2048, cb * 256:(cb + 1) * 256]
                wsl, wt, wi = self.wload(src)
                for j in range(2):
                    banks = open_jobs[j]
                    for k in range(DK):
                        first = (kp == 0 and k == 0)
                        last = (kp == nparts - 1 and k == DK - 1)
                        for gi, (c0, c1) in enumerate(cfg.groups):
                            d = [wt] + (self.bank_free[banks[gi]] if first else [])
                            sig = (k == DK - 1 and gi == len(cfg.groups) - 1)
                            t = self.MM(self.bank_ap(banks[gi], c1 - c0), wsl[:, k, j * 128:(j + 1) * 128],
                                        opnd[:, kp * 16 + k, c0:c1], first, last, deps=d, signal=sig)
                            if sig:
                                toks[j] = t
                self.wrelease(wi, toks[1])
            for j in range(2):
                o = 2 * cb + j
                banks = open_jobs[j]
                evs = []
                for gi, (c0, c1) in enumerate(cfg.groups):
                    te = self.A(acc[:, o, c0:c1], self.bank_ap(banks[gi], c1 - c0), AF.Copy, scale=evac_scale,
                                deps=[toks[j]])
                    self.release_bank(banks[gi], [te])
                    evs.append(te)
                s = nsq % 2
                tq = self.Vtt(sqb[s][:, 0:nt], acc[:, o, :], acc[:, o, :], ALU.mult, deps=evs + [sq_free[s]])
                if pend is not None:
                    stok = flush(pend, nsq == 1, False)
                pend = (tq, s)
                nsq += 1
        stok = flush(pend, False, True)
        for gi, (c0, c1) in enumerate(cfg.groups):
            ta = self.A(self.rstd[:, c0:c1], self.bank_ap(sbanks[gi], c1 - c0), AF.Sqrt, scale=1.0 / D, bias=EPS,
                        deps=[stok])
            P.op("dve", lambda e, c0=c0, c1=c1: e.reciprocal(out=self.rstd[:, c0:c1], in_=self.rstd[:, c0:c1]),
                 deps=[ta])
            self.release_bank(sbanks[gi], [ta])
            self.bank_reserved.discard(sbanks[gi])

    def stage_ffn_up(self, cfg, h2, f):
        P = self.P
        nt = cfg.ntok
        p0, p1 = cfg.prm
        Lp = p1 - p0
        cv = self.cvec
        W, cur, prm, scur = self.ext_layout(cfg, 2)
        def U_ap(slot, gv):
            o = slot * 2752 + gv * 704
            return self.R4[:, o:o + W]
        def C_ap(slot, gv):
            if slot == 1 and gv == 1:
                return self.sq_scratch[:, 0:nt]
            o = slot * 2752 + 1408 + gv * 672
            return self.R4[:, o:o + nt]
        st_fs = [self.rstd[:, 0:256], self.rstd[:, 256:512]]
        so_f = self.cT[:].rearrange("p k j -> p (k j)")[:, 0:256]
        stf_rd = [[], []]
        stf_tok = [None, None]

        def prefetch_hist(jf_):
            bi = jf_ % 2
            for gv_ in range(2):
                for r in range(2):
                    self.stf_cnt[bi] += 16
                    dst = st_fs[bi][r * 16:(r + 1) * 16, gv_ * 128:(gv_ + 1) * 128]
                    src = self.st_fconv[:, r, gv_ * DFF + jf_ * 128: gv_ * DFF + (jf_ + 1) * 128]
                    P.dma("sp", lambda e, dst=dst, src=src: e.dma_start(out=dst, in_=src), self.s_stf[bi],
                          self.stf_cnt[bi], deps=stf_rd[bi] + stage_start)
            stf_tok[bi] = (self.s_stf[bi], self.stf_cnt[bi])
            stf_rd[bi] = []
        stage_start = list(self.last_barrier)
        slot_free = [None, None]
        grp_so = []
        grp_rd = []
        pending_tail = None
        wrel = []

        def emit_tail(tl):
            (Cg, tg, Cv, tvv, jf_, slot_) = tl
            tgl = self.A(Cg[:, 0:nt], Cg[:, 0:nt], AF.Gelu_apprx_tanh, deps=tg)
            tf = self.Vtt(f[:, jf_, :], Cg[:, 0:nt], Cv[:, 0:nt], ALU.mult, deps=[tgl] + tvv)
            slot_free[slot_] = slot_free[slot_] + [tf]

        it = 0
        for q in range(24):
            if self.ada_pending and q % 6 == 0:
                self.ada_item(self.ada_pending.pop(0))
            wg_, wgt, wgi = self.wload(self.w_fup[:, q * 256:(q + 1) * 256])
            wv_, wvt, wvi = self.wload(self.w_fup[:, DFF + q * 256:DFF + (q + 1) * 256])
            lastpe = None
            for j in range(2):
                jf = 2 * q + j
                slot = it % 2
                it += 1
                if cfg.samp is not None:
                    if jf == 0:
                        prefetch_hist(0)
                    if jf + 1 < 48:
                        prefetch_hist(jf + 1)
                sfree = slot_free[slot]
                jobs = []
                for gv, (wsl, wt) in enumerate(((wg_, wgt), (wv_, wvt))):
                    banks, tok = self.job(cfg.groups, [(wsl[:, k, j * 128:(j + 1) * 128],
                                                        (lambda c0, c1, k=k: h2[:, k, c0:c1]), [wt])
                                                       for k in range(DK)])
                    jobs.append((banks, tok))
                    lastpe = tok
                hist_ps = []
                if cfg.samp is not None:
                    st_f = st_fs[jf % 2]
                    b = self.alloc_bank()
                    for gv in range(2):
                        tt = self.TR(self.ps[:, b, gv * 32:(gv + 1) * 32], st_f[0:32, gv * 128:(gv + 1) * 128],
                                     self.ident[0:32, 0:32],
                                     deps=[stf_tok[jf % 2]] + (self.bank_free[b] if gv == 0 else []), signal=True)
                        stf_rd[jf % 2].append(tt)
                        hist_ps.append((b, tt))
                        lastpe = tt
                evs_all = []
                for gv in range(2):
                    U = U_ap(slot, gv)
                    banks, tok = jobs[gv]
                    evs = []
                    for gi, (c0, c1) in enumerate(cfg.groups):
                        te = self.A(U[:, cur + c0:cur + c1], self.bank_ap(banks[gi], c1 - c0), AF.Copy,
                                    deps=[tok, sfree])
                        self.release_bank(banks[gi], [te])
                        evs.append(te)
                    evs_all.append(evs)
                for gv in range(2):
                    U = U_ap(slot, gv)
                    chn = jf + gv * 48
                    if cfg.samp is not None:
                        b, tt = hist_ps[gv]
                        te = self.Vcopy(U[:, 0:32], self.ps[:, b, gv * 32:(gv + 1) * 32],
                                        deps=[hist_ps[0][1], hist_ps[1][1], sfree])
                        if gv == 0:
                            hist_rel = [te]
                        else:
                            self.release_bank(b, hist_rel + [te])
                    else:
                        te = self.Vcopy(U[:, 0:2], self.hist_up[:, chn, :], deps=[sfree, self.t_init])
                    evs_all[gv].append(te)
                regs = []
                if cfg.samp is not None:
                    regs.append((32, 128, 16, 0))
                    regs.append((prm, Lp, 1, p0))
                else:
                    regs.append((cur, Lp, 1, p0))
                c0t = [[], []]
                for gv in range(2):
                    U = U_ap(slot, gv)
                    C = C_ap(slot, gv)
                    chn = jf + gv * 48
                    for (co, n, strd, xo) in regs:
                        c0t[gv].append(self.A(C[:, xo:xo + n], U[:, co:co + n], AF.Identity,
                                              scale=cv[:, CV_WFC + 192 + chn:CV_WFC + 192 + chn + 1],
                                              bias=cv[:, CV_BFC + chn:CV_BFC + chn + 1], deps=evs_all[gv] + [sfree]))
                ctoks = [[], []]
                for gv in range(2):
                    U = U_ap(slot, gv)
                    C = C_ap(slot, gv)
                    chn = jf + gv * 48
                    for ri, (co, n, strd, xo) in enumerate(regs):
                        t = c0t[gv][ri]
                        for k in range(2):
                            sh = (2 - k) * strd
                            t = self.Vstt(C[:, xo:xo + n], U[:, co - sh:co - sh + n],
                                          cv[:, CV_WFC + 96 * k + chn:CV_WFC + 96 * k + chn + 1], C[:, xo:xo + n],
                                          ALU.mult, ALU.add, deps=[t] + evs_all[gv])
                        ctoks[gv].append(t)
                ureaders = []
                for gv in range(2):
                    U = U_ap(slot, gv)
                    chn = jf + gv * 48
                    tsv = self.A(self.hist_up[:, chn, :], U[:, W - 2:W], AF.Copy, deps=evs_all[gv])
                    ureaders.append(tsv)
                if cfg.samp is not None:
                    bso = self.alloc_bank()
                    tts = []
                    for gv in range(2):
                        U = U_ap(slot, gv)
                        tt = self.TR(self.ps[0:32, bso, gv * 128:(gv + 1) * 128], U[:, 128:160], self.ident[:],
                                     deps=evs_all[gv] + (self.bank_free[bso] if gv == 0 else []), signal=True)
                        tts.append(tt)
                        ureaders.append(tt)
                        lastpe = tt
                    te = self.A(so_f[0:32, 0:256], self.ps[0:32, bso, 0:256], AF.Copy, deps=tts + grp_so)
                    self.release_bank(bso, [te])
                slot_free[slot] = ureaders + ctoks[0] + ctoks[1]
                if cfg.samp is not None:
                    for gv in range(2):
                        for r in range(2):
                            self.so_dma(self.o_fconv_s[:, r, gv * DFF + jf * 128: gv * DFF + (jf + 1) * 128],
                                        so_f[r * 16:(r + 1) * 16, gv * 128:(gv + 1) * 128],
                                        deps=[P.cur("act")], stream="ffn")
                    grp_so = [self.so_tok("ffn")]
                if pending_tail is not None:
                    emit_tail(pending_tail)
                pending_tail = (C_ap(slot, 0), ctoks[0], C_ap(slot, 1), ctoks[1], jf, slot)
            self.wrelease(wgi, lastpe)
            self.wrelease(wvi, lastpe)
        emit_tail(pending_tail)
        if cfg.samp is not None:
            self.out_tokens += self.so_all()

    def stage_store_y(self, cfg):
        P = self.P
        xT = self.xT(cfg)
        ost = [self.R2[:, 0:2048], self.R2[:, 2048:4096]]
        for ti, (col0, yrow0, nr) in enumerate(cfg.out_tiles):
            s = ti % 2
            prev = (self.s_o[s], self.o_cnt[s]) if self.o_cnt[s] else None
            evs = []
            for g4 in range(4):
                b = self.alloc_bank()
                tt = None
                for qq in range(4):
                    k = g4 * 4 + qq
                    d = (self.bank_free[b] if qq == 0 else [])
                    tt = self.TR(self.ps[0:nr, b, qq * 128:(qq + 1) * 128], xT[:, k, col0:col0 + nr], self.ident[:],
                                 deps=d, signal=(qq == 3))
                dst = ost[s][0:nr, g4 * 512:(g4 + 1) * 512]
                if g4 % 2 == 0:
                    te = self.A(dst, self.ps[0:nr, b, :], AF.Copy, deps=[tt, prev])
                else:
                    te = self.Vcopy(dst, self.ps[0:nr, b, :], deps=[tt, prev])
                self.release_bank(b, [te])
                evs.append(te)
            self.o_cnt[s] += 16
            src = ost[s]
            dsty = self.y[yrow0:yrow0 + nr, :]
            src = ost[s][0:nr, :]
            t = P.dma("sp", lambda e, dsty=dsty, src=src: e.dma_start(out=dsty, in_=src), self.s_o[s],
                      self.o_cnt[s], deps=evs)
            self.out_tokens.append(t)

    def epilogue(self):
        P = self.P
        self.barrier()
        stg = self.R2[:, 0:12288]
        jobs = [(self.hist_pool, 8, 15, self.o_pool_p), (self.hist_lru, 16, 3, self.o_lconv_p),
                (self.hist_up, 96, 2, self.o_fconv_p)]
        prev = None
        for (src, nch, ncol, dst) in jobs:
            evs = []
            for c0 in range(0, nch, 4):
                b = self.alloc_bank()
                tt = None
                for qq in range(4):
                    d = (self.bank_free[b] if qq == 0 else [])
                    tt = self.TR(self.ps[0:ncol, b, qq * 128:(qq + 1) * 128], src[:, c0 + qq, :], self.ident[:],
                                 deps=d, signal=(qq == 3))
                te = self.A(stg[0:ncol, c0 * 128:(c0 + 4) * 128], self.ps[0:ncol, b, :], AF.Copy, deps=[tt, prev])
                self.release_bank(b, [te])
                evs.append(te)
            t = self.so_dma(dst[:, :], stg[0:ncol, 0:nch * 128], deps=evs, stream="epi")
            prev = self.so_tok("epi")
        evs = []
        for c0 in range(0, 16, 4):
            b = self.alloc_bank()
            tt = None
            for qq in range(4):
                d = (self.bank_free[b] if qq == 0 else [])
                tt = self.TR(self.ps[0:1, b, qq * 128:(qq + 1) * 128], self.h_carry[:, c0 + qq:c0 + qq + 1],
                             self.ident[:], deps=d, signal=(qq == 3))
            te = self.A(stg[0:1, c0 * 128:(c0 + 4) * 128], self.ps[0:1, b, :], AF.Copy, deps=[tt, prev])
            self.release_bank(b, [te])
            evs.append(te)
        self.so_dma(self.o_lh_p[:, :], stg[0:1, 0:2048], deps=evs, stream="epi")
        final = self.so_all() + [(self.s_o[s], self.o_cnt[s]) for s in range(2)]
        P.wait_only("sp", final + self.out_tokens)


_NC_CACHE = {}


def _get_nc():
    if "nc" not in _NC_CACHE:
        b = Builder()
        _NC_CACHE["nc"] = b
    return _NC_CACHE["nc"]


def _pack_vec(v):
    v = np.asarray(v, np.float32).reshape(-1)
    return np.ascontiguousarray(v.reshape(-1, 128).T)


def kernel(x_prompt, x_sample, c_prompt, c_sample, state_pool, state_lru_conv, state_lru_h, state_ffn_conv,
           w_ada, b_ada, g_pre1, g_post1, g_pre2, g_post2, w_in, w_pool_grp, pool_scale,
           w_lru_conv, b_lru_conv, w_rg, b_rg, w_ig, b_ig, lru_lambda,
           w_pool_up, w_lru_up, w_out, w_ffn_up, w_ffn_conv, b_ffn_conv, w_ffn_down):
    f32 = np.float32
    A = lambda a: np.ascontiguousarray(np.asarray(a, f32))
    x_prompt, x_sample = A(x_prompt), A(x_sample)
    c_prompt, c_sample = A(c_prompt), A(c_sample)
    cvec = np.zeros((128, NV), f32)
    cvec[:, CV_GPRE1:CV_GPRE1 + 16] = _pack_vec(g_pre1[0])
    cvec[:, CV_GPOST1:CV_GPOST1 + 16] = _pack_vec(g_post1[0])
    cvec[:, CV_GPRE2:CV_GPRE2 + 16] = _pack_vec(g_pre2[0])
    cvec[:, CV_GPOST2:CV_GPOST2 + 16] = _pack_vec(g_post2[0])
    cvec[:, CV_PSCALE:CV_PSCALE + 8] = _pack_vec(pool_scale[0])
    for k in range(4):
        cvec[:, CV_WLC + 16 * k:CV_WLC + 16 * (k + 1)] = _pack_vec(np.asarray(w_lru_conv)[0, k])
    cvec[:, CV_BLC:CV_BLC + 16] = _pack_vec(b_lru_conv[0])
    cvec[:, CV_BRG:CV_BRG + 16] = _pack_vec(b_rg[0])
    cvec[:, CV_BIG:CV_BIG + 16] = _pack_vec(b_ig[0])
    cvec[:, CV_LAM:CV_LAM + 16] = _pack_vec(lru_lambda[0])
    for k in range(3):
        cvec[:, CV_WFC + 96 * k:CV_WFC + 96 * (k + 1)] = _pack_vec(np.asarray(w_ffn_conv)[0, k])
    cvec[:, CV_BFC:CV_BFC + 96] = _pack_vec(b_ffn_conv[0])
    cvec[:, CV_BADA:CV_BADA + 96] = _pack_vec(b_ada[0])
    ident = np.eye(128, dtype=f32)
    weights = {
        "w_ada": A(w_ada)[0], "w_in": A(w_in)[0], "w_pool_grp": A(w_pool_grp)[0], "w_rg": A(w_rg)[0],
        "w_ig": A(w_ig)[0], "w_pool_up": A(w_pool_up)[0], "w_lru_up": A(w_lru_up)[0], "w_out": A(w_out)[0],
        "w_ffn_up": A(w_ffn_up)[0], "w_ffn_down": A(w_ffn_down)[0],
    }
    state_pool, state_lru_conv = A(state_pool)[0], A(state_lru_conv)[0]
    state_lru_h, state_ffn_conv = A(state_lru_h)[0], A(state_ffn_conv)[0]
    in_maps = []
    for c in range(NCORES):
        b, hf = c // 2, c % 2
        xq = np.zeros((2176, D), f32)
        if hf == 1:
            xq[0:1024] = x_prompt[b, 0:1024]
        xq[1024:2048] = x_prompt[b, hf * 1024:(hf + 1) * 1024]
        xs = x_sample[16 * c:16 * (c + 1)]
        xq[2048:2176] = xs.transpose(1, 0, 2).reshape(128, D)
        cc = np.concatenate([c_prompt[b:b + 1], c_sample[16 * c:16 * (c + 1)]], axis=0)
        cT = np.ascontiguousarray(cc.reshape(17, 16, 128).transpose(2, 1, 0))
        sel = np.full((128, 1), float(hf), f32)
        invc = np.zeros((128, 4, 16), f32)
        for g in range(4):
            w = 2 ** (g + 1)
            for j in range(16):
                cnt = w if hf == 1 else min(w, j + 1)
                invc[:, g, j] = 1.0 / cnt
        m = {"xq": xq, "cT": cT, "cvec": cvec, "sel": sel, "invc": invc, "ident": ident,
             "st_pool": np.ascontiguousarray(state_pool[16 * c:16 * (c + 1)]),
             "st_lconv": np.ascontiguousarray(state_lru_conv[16 * c:16 * (c + 1)]),
             "st_lh": np.ascontiguousarray(state_lru_h[16 * c:16 * (c + 1)]),
             "st_fconv": np.ascontiguousarray(state_ffn_conv[16 * c:16 * (c + 1)])}
        m.update(weights)
        in_maps.append(m)
    nc = build_nc()
    res = run_bass_kernel_spmd(nc, in_maps, core_ids=list(range(NCORES)))
    R = res.results
    y_p = np.zeros((4, 2048, D), f32)
    y_s = np.zeros((128, 8, D), f32)
    pool_p = np.zeros((1, 4, 15, PW), f32)
    lconv_p = np.zeros((1, 4, 3, D), f32)
    lh_p = np.zeros((1, 4, D), f32)
    fconv_p = np.zeros((1, 4, 2, 2 * DFF), f32)
    pool_s = np.zeros((1, 128, 15, PW), f32)
    lconv_s = np.zeros((1, 128, 3, D), f32)
    lh_s = np.zeros((1, 128, D), f32)
    fconv_s = np.zeros((1, 128, 2, 2 * DFF), f32)
    for c in range(NCORES):
        b, hf = c // 2, c % 2
        r = R[c]
        y_p[b, hf * 1024:(hf + 1) * 1024] = r["y"][0:1024]
        y_s[16 * c:16 * (c + 1)] = r["y"][1024:1152].reshape(8, 16, D).transpose(1, 0, 2)
        if hf == 1:
            pool_p[0, b] = r["o_pool_p"]
            lconv_p[0, b] = r["o_lconv_p"]
            lh_p[0, b] = r["o_lh_p"][0]
            fconv_p[0, b] = r["o_fconv_p"]
        pool_s[0, 16 * c:16 * (c + 1)] = r["o_pool_s"]
        lconv_s[0, 16 * c:16 * (c + 1)] = r["o_lconv_s"]
        lh_s[0, 16 * c:16 * (c + 1)] = r["o_lh_s"]
        fconv_s[0, 16 * c:16 * (c + 1)] = r["o_fconv_s"]
    return (y_p, y_s, pool_p, lconv_p, lh_p, fconv_p, pool_s, lconv_s, lh_s, fconv_s)


def build_nc():
    if "built" not in _NC_CACHE:
        b = Builder()
        _NC_CACHE["built"] = b.build()
    return _NC_CACHE["built"]
```

```python
import contextlib
import numpy as np
import concourse.bass as bass
import concourse.mybir as mybir
from concourse.bass_utils import run_bass_kernel_spmd

F32 = mybir.dt.float32
BF16 = mybir.dt.bfloat16
AF = mybir.ActivationFunctionType
ALU = mybir.AluOpType

D = 2048
DK = 16
PW = 1024
DFF = 6144
EPS = 1e-6
NCORES = 8
HALO = 32
NPRE = 992
ENGS = ("pe", "act", "dve", "pool", "sp")

CV_GPRE1, CV_GPOST1, CV_GPRE2, CV_GPOST2 = 0, 16, 32, 48
CV_PSCALE = 64
CV_WLC = 72
CV_BLC = 136
CV_BRG = 152
CV_BIG = 168
CV_LAM = 184
CV_WFC = 200
CV_BFC = 488
CV_BADA = 584
NV = 680


class Prog:
    def __init__(self, nc, stack):
        self.nc = nc
        self.stack = stack
        self.q = {e: [] for e in ENGS}
        self.sem = {e: stack.enter_context(nc.semaphore("prog_" + e)) for e in ENGS}
        self.cnt = {e: 0 for e in ENGS}
        self.waited = {e: {} for e in ENGS}

    def new_sem(self, name):
        return self.stack.enter_context(self.nc.semaphore(name))

    def _waits(self, eng, deps):
        out = []
        for d in deps:
            if d is None:
                continue
            if isinstance(d, list):
                out.extend(self._waits(eng, d))
                continue
            s, v = d
            key = id(s)
            prev = self.waited[eng].get(key, 0)
            if v > prev:
                self.waited[eng][key] = v
                out.append((s, v))
        return out

    def op(self, eng, fn, deps=(), signal=True):
        w = self._waits(eng, deps)
        tok = None
        if signal:
            self.cnt[eng] += 1
            tok = (self.sem[eng], self.cnt[eng])
        self.q[eng].append((fn, w, tok))
        return tok

    def dma(self, eng, fn, sem, val, deps=()):
        w = self._waits(eng, deps)
        self.q[eng].append((fn, w, ("dma", sem)))
        return (sem, val)

    def cur(self, eng):
        if self.cnt[eng] == 0:
            return None
        return (self.sem[eng], self.cnt[eng])

    def wait_only(self, eng, deps):
        w = self._waits(eng, deps)
        if w:
            self.q[eng].append((None, w, None))

    def replay(self, block):
        def run(name):
            def body(e):
                for fn, waits, tok in self.q[name]:
                    for (s, v) in waits:
                        e.wait_ge(s, v)
                    if fn is None:
                        continue
                    ins = fn(e)
                    if tok is not None:
                        if tok[0] == "dma":
                            ins.then_inc(tok[1], 16)
                        else:
                            ins.then_inc(tok[0], 1)
            return body
        block.tensor(run("pe"))
        block.scalar(run("act"))
        block.vector(run("dve"))
        block.gpsimd(run("pool"))
        block.sync(run("sp"))


class PassCfg:
    def __init__(self, name, ntok, groups, samp, prm, halo, xtiles, lru_only, out_tiles):
        self.name = name
        self.ntok = ntok
        self.groups = groups
        self.samp = samp
        self.prm = prm
        self.halo = halo
        self.xtiles = xtiles
        self.lru_only = lru_only
        self.out_tiles = out_tiles


PASSES = [
    PassCfg("p0", 992, [(0, 496), (496, 992)], None, (0, 992), 0,
            [(0, 128, 0), (128, 128, 128), (256, 128, 256), (384, 128, 384), (512, 128, 512), (640, 128, 640),
             (768, 128, 768), (896, 96, 896)], True, []),
    PassCfg("p1", 608, [(0, 160), (160, 608)], (0, 128), (128, 608), HALO,
            [(2048, 128, 0), (992, 32, 128), (1024, 128, 160), (1152, 128, 288), (1280, 128, 416), (1408, 64, 544)],
            False, [(0, 1024, 128), (160, 0, 128), (288, 128, 128), (416, 256, 128), (544, 384, 64)]),
    PassCfg("p2", 576, [(0, 512), (512, 576)], None, (0, 576), 0,
            [(1472, 128, 0), (1600, 128, 128), (1728, 128, 256), (1856, 128, 384), (1984, 64, 512)],
            False, [(0, 448, 128), (128, 576, 128), (256, 704, 128), (384, 832, 128), (512, 960, 64)]),
]
NTMAX = 672


class Builder:
    def __init__(self, debug=False):
        self.debug = debug
        nc = bass.Bass("TRN2", target_bir_lowering=False)
        self.nc = nc
        di = lambda n, s: nc.dram_tensor(n, s, F32, kind="ExternalInput").ap()
        do = lambda n, s: nc.dram_tensor(n, s, F32, kind="ExternalOutput").ap()
        self.xq = di("xq", [2176, D])
        self.cT_d = di("cT", [128, DK, 17])
        self.cvec_d = di("cvec", [128, NV])
        self.sel_d = di("sel", [128, 1])
        self.invc_d = di("invc", [128, 4, 16])
        self.ident_d = di("ident", [128, 128])
        self.st_pool = di("st_pool", [16, 15, PW])
        self.st_lconv = di("st_lconv", [16, 3, D])
        self.st_lh = di("st_lh", [16, D])
        self.st_fconv = di("st_fconv", [16, 2, 2 * DFF])
        self.w_ada = di("w_ada", [D, 6 * D])
        self.w_in = di("w_in", [D, 7168])
        self.w_grp = di("w_pool_grp", [4, 256, 256])
        self.w_rg = di("w_rg", [8, 256, 256])
        self.w_ig = di("w_ig", [8, 256, 256])
        self.w_pup = di("w_pool_up", [PW, D])
        self.w_lup = di("w_lru_up", [D, D])
        self.w_out = di("w_out", [D, D])
        self.w_fup = di("w_ffn_up", [D, 2 * DFF])
        self.w_fdn = di("w_ffn_down", [DFF, D])
        self.y = do("y", [1152, D])
        self.o_pool_p = do("o_pool_p", [15, PW])
        self.o_lconv_p = do("o_lconv_p", [3, D])
        self.o_lh_p = do("o_lh_p", [1, D])
        self.o_fconv_p = do("o_fconv_p", [2, 2 * DFF])
        self.o_pool_s = do("o_pool_s", [16, 15, PW])
        self.o_lconv_s = do("o_lconv_s", [16, 3, D])
        self.o_lh_s = do("o_lh_s", [16, D])
        self.o_fconv_s = do("o_fconv_s", [16, 2, 2 * DFF])

    def sb(self, name, shape, dt):
        return self.st.enter_context(self.nc.sbuf_tensor("sb_" + name, shape, dt))

    def A(self, out, in_, func, scale=None, bias=None, deps=()):
        kw = {}
        if scale is not None:
            kw["scale"] = scale
        if bias is not None:
            kw["bias"] = bias
        return self.P.op("act", lambda e: e.activation(out=out, in_=in_, func=func, **kw), deps=deps)

    def Vtt(self, out, in0, in1, op, deps=(), eng="dve"):
        return self.P.op(eng, lambda e: e.tensor_tensor(out=out, in0=in0, in1=in1, op=op), deps=deps)

    def Vstt(self, out, in0, scalar, in1, op0, op1, deps=()):
        return self.P.op("dve", lambda e: e.scalar_tensor_tensor(out=out, in0=in0, scalar=scalar, in1=in1,
                                                                  op0=op0, op1=op1), deps=deps)

    def Vts(self, out, in0, s1, s2, op0, op1=None, deps=()):
        if op1 is None:
            return self.P.op("dve", lambda e: e.tensor_scalar(out=out, in0=in0, scalar1=s1, scalar2=None, op0=op0),
                             deps=deps)
        return self.P.op("dve", lambda e: e.tensor_scalar(out=out, in0=in0, scalar1=s1, scalar2=s2, op0=op0, op1=op1),
                         deps=deps)

    def Vcopy(self, out, in_, deps=()):
        return self.P.op("dve", lambda e: e.tensor_copy(out=out, in_=in_), deps=deps)

    def MM(self, out, lhsT, rhs, start, stop, deps=(), signal=False):
        return self.P.op("pe", lambda e: e.matmul(out, lhsT=lhsT, rhs=rhs, start=start, stop=stop),
                         deps=deps, signal=signal)

    def TR(self, out, in_, ident, deps=(), signal=False):
        return self.P.op("pe", lambda e: e.transpose(out, in_, ident), deps=deps, signal=signal)

    def barrier(self, extra=()):
        P = self.P
        toks = [P.cur(e) for e in ("pe", "act", "dve", "pool")] + list(extra) + self.so_all()
        for e in ("pe", "act", "dve"):
            P.wait_only(e, toks)
        self.last_barrier = toks
        return toks

    def alloc_bank(self):
        for _ in range(8):
            b = self.bank_next
            self.bank_next = (self.bank_next + 1) % 8
            if b not in self.bank_reserved:
                if b in self.bank_busy:
                    raise RuntimeError("PSUM bank %d re-allocated before release" % b)
                self.bank_busy.add(b)
                return b
        raise RuntimeError("no bank")

    def bank_ap(self, b, n, p=128):
        return self.ps[0:p, b, 0:n]

    def release_bank(self, b, toks):
        self.bank_free[b] = list(toks)
        self.bank_busy.discard(b)

    def wload(self, src, kch=16, ncol=256):
        i = self.w_next
        self.w_next = (i + 1) % len(self.wslots)
        slot = self.wslots[i]
        self.w_cnt[i] += 16
        dst = slot[:, 0:kch, 0:ncol]
        srcv = src.rearrange("(k p) n -> p k n", p=128)
        tok = self.P.dma("pool", lambda e: e.dma_start(out=dst, in_=srcv), self.w_sem[i], self.w_cnt[i],
                         deps=[self.w_rel[i]])
        return dst, tok, i

    def wrelease(self, i, tok):
        self.w_rel[i] = tok

    def job(self, groups, parts):
        banks = [self.alloc_bank() for _ in groups]
        n = len(parts)
        tok = None
        for idx, (lhsT, rhs_fn, deps) in enumerate(parts):
            for gi, (c0, c1) in enumerate(groups):
                b = banks[gi]
                d = list(deps)
                if idx == 0:
                    d += self.bank_free[b]
                last = (idx == n - 1 and gi == len(groups) - 1)
                t = self.MM(self.bank_ap(b, c1 - c0), lhsT, rhs_fn(c0, c1), idx == 0, idx == n - 1, deps=d,
                            signal=last)
                if last:
                    tok = t
        return banks, tok

    def build(self):
        nc = self.nc
        with contextlib.ExitStack() as st:
            self.st = st
            self.P = P = Prog(nc, st)
            self.ident = self.sb("ident", [128, 128], F32)
            self.ones = self.sb("ones", [128, 128], BF16)
            self.cvec = self.sb("cvec", [128, NV], F32)
            self.dv = self.sb("dv", [128, 4, 16], F32)
            self.mod = self.sb("mod", [128, 6, 16, 17], F32)
            self.sel = self.sb("sel", [128, 1], F32)
            self.invc = self.sb("invc", [128, 4, 16], F32)
            self.cT = self.sb("cT", [128, DK, 17], F32)
            self.sl = self.sb("sl", [128, DK, 17], BF16)
            self.wgrp = self.sb("wgrp", [128, 4, 2, 256], BF16)
            self.hist_pool = self.sb("hist_pool", [128, 8, 15], F32)
            self.hist_lru = self.sb("hist_lru", [128, 16, 3], F32)
            self.h_carry = self.sb("h_carry", [128, 16], F32)
            self.hist_up = self.sb("hist_up", [128, 96, 2], F32)
            self.rstd = self.sb("rstd", [128, NTMAX], F32)
            self.sq_scratch = self.sb("sqs", [128, NTMAX], F32)
            self.ada_tm_buf = self.sb("ada_tm", [128, 512], F32)
            self.R1 = self.sb("R1", [128, 16 * NTMAX], F32)
            self.R2 = self.sb("R2", [128, 16128], F32)
            self.R4 = self.sb("R4", [128, 16 * NTMAX], F32)
            NW = 4
            self.wslots = [self.sb("w%d" % i, [128, 16, 256], BF16) for i in range(NW)]
            self.w_sem = [P.new_sem("wsem%d" % i) for i in range(NW)]
            self.w_cnt = [0] * NW
            self.w_rel = [None] * NW
            self.w_next = 0
            self.wsm = [self.sb("wsm%d" % i, [128, 2, 2, 256], BF16) for i in range(2)]
            self.wsm_sem = [P.new_sem("wsmsem%d" % i) for i in range(2)]
            self.wsm_cnt = [0, 0]
            self.wsm_rel = [None, None]
            self.wsm_next = 0
            self.ps = st.enter_context(nc.psum_tensor("ps_all", [128, 8, 512], F32))
            self.bank_next = 0
            self.bank_reserved = set()
            self.bank_busy = set()
            self.bank_free = [[] for _ in range(8)]
            self.s_misc = P.new_sem("misc")
            self.misc_cnt = 0
            self.s_x = [P.new_sem("xs0"), P.new_sem("xs1"), P.new_sem("xs2")]
            self.x_cnt = [0, 0, 0]
            self.s_o = [P.new_sem("os0"), P.new_sem("os1")]
            self.o_cnt = [0, 0]
            self.s_so = P.new_sem("so")
            self.s_stf = [P.new_sem("stf0"), P.new_sem("stf1")]
            self.stf_cnt = [0, 0]
            self.so_cnt = 0
            self.so_streams = {}
            self.out_tokens = []
            self.last_barrier = []

            self.prologue()
            for cfg in PASSES:
                self.run_pass(cfg)
            self.epilogue()
            with nc.Block() as block:
                P.replay(block)
        return nc

    def misc_dma(self, eng, out, in_, deps=()):
        self.misc_cnt += 16
        return self.P.dma(eng, lambda e: e.dma_start(out=out, in_=in_), self.s_misc, self.misc_cnt, deps=deps)

    def so_tok(self, stream):
        st_ = self.so_streams.get(stream)
        return (st_[0], st_[1]) if st_ else None

    def so_all(self):
        return [(v[0], v[1]) for v in self.so_streams.values()]

    def so_dma(self, out, in_, deps=(), stream="main"):
        if stream not in self.so_streams:
            self.so_streams[stream] = [self.P.new_sem("so_" + stream), 0]
        st_ = self.so_streams[stream]
        st_[1] += 16
        self.so_cnt += 16
        t = self.P.dma("sp", lambda e: e.dma_start(out=out, in_=in_), st_[0], st_[1], deps=deps)
        return t

    def prologue(self):
        P = self.P
        cv = self.cvec
        lds = []
        lds.append(self.misc_dma("sp", self.ident[:], self.ident_d))
        lds.append(self.misc_dma("sp", self.cvec[:], self.cvec_d))
        lds.append(self.misc_dma("sp", self.sel[:], self.sel_d))
        lds.append(self.misc_dma("sp", self.invc[:], self.invc_d))
        lds.append(self.misc_dma("sp", self.cT[:], self.cT_d))
        ld = lds[-1]
        ld = (self.s_misc, self.misc_cnt)
        s_wg = P.new_sem("wgsem")
        wgv = self.w_grp.rearrange("g (k p) n -> p g k n", p=128)
        self.t_wgrp = P.dma("pool", lambda e: e.dma_start(out=self.wgrp[:], in_=wgv), s_wg, 16)
        t0 = P.op("dve", lambda e: e.memset(self.ones[:], 1.0))
        P.op("dve", lambda e: e.memset(self.hist_pool[:], 0.0))
        P.op("dve", lambda e: e.memset(self.hist_lru[:], 0.0))
        P.op("dve", lambda e: e.memset(self.h_carry[:], 0.0))
        self.t_init = P.op("dve", lambda e: e.memset(self.hist_up[:], 0.0))
        self.Vts(self.dv[:, 0, :], cv[:, CV_BRG:CV_BRG + 16], 0.5, None, ALU.mult, deps=[ld])
        self.Vts(self.dv[:, 1, :], cv[:, CV_BIG:CV_BIG + 16], 0.5, None, ALU.mult)
        ta = self.A(self.dv[:, 2, :], cv[:, CV_LAM:CV_LAM + 16], AF.Exp, scale=-1.0, deps=[ld])
        ta = self.A(self.dv[:, 2, :], self.dv[:, 2, :], AF.Ln, bias=1.0, deps=[ta])
        tv = self.Vts(self.dv[:, 2, :], self.dv[:, 2, :], -8.0, None, ALU.mult, deps=[ta])
        self.t_dv = self.Vts(self.dv[:, 3, :], self.dv[:, 2, :], 0.5, None, ALU.mult, deps=[tv])
        th = self.R4[:, 0:DK * 17].rearrange("p (k j) -> p k j", k=DK)
        ta = self.A(th, self.cT[:], AF.Tanh, scale=0.5, deps=[ld])
        t_sl = self.Vstt(self.sl[:], th, 1.0, self.cT[:], ALU.add, ALU.mult, deps=[ta])
        self.t_sl = t_sl
        self.t_ld = ld
        self.ada_tm_free = None
        self.ada_last = None
        self.ada_pending = list(range(8, 24))
        self.ada_mid_done = False
        for cb in range(8):
            self.ada_item(cb)
        self.ada_finalize([0, 1])
        self.barrier([ld, self.t_wgrp])


    def ada_item(self, cb):
        ada_tm = self.ada_tm_buf
        modf = self.mod[:].rearrange("p m k j -> p (m k) j")
        b = self.alloc_bank()
        tok = None
        for half in range(2):
            src = self.w_ada[:, cb * 512 + half * 256: cb * 512 + (half + 1) * 256]
            w, wt, wi = self.wload(src)
            for k in range(DK):
                d = [wt, self.t_sl] + (self.bank_free[b] if (k == 0 and half == 0) else [])
                tok = self.MM(self.ps[0:17, b, half * 256:(half + 1) * 256], self.sl[:, k, :], w[:, k, :],
                              k == 0, k == DK - 1, deps=d, signal=(k == DK - 1))
            self.wrelease(wi, tok)
        te = self.A(ada_tm[0:17, :], self.ps[0:17, b, :], AF.Copy, scale=0.5, deps=[tok, self.ada_tm_free])
        self.release_bank(b, [te])
        b2 = self.alloc_bank()
        tt = None
        for qq in range(4):
            d = [te, self.t_ld] + (self.bank_free[b2] if qq == 0 else [])
            tt = self.TR(self.ps[:, b2, qq * 17:(qq + 1) * 17], ada_tm[0:17, qq * 128:(qq + 1) * 128],
                         self.ident[0:17, 0:17], deps=d, signal=(qq == 3))
        self.ada_tm_free = tt
        te2 = self.Vcopy(modf[:, cb * 4:(cb + 1) * 4, :],
                         self.ps[:, b2, 0:68].rearrange("p (q j) -> p q j", q=4), deps=[tt])
        self.release_bank(b2, [te2])
        self.ada_last = te2

    def ada_finalize(self, ms):
        cv = self.cvec
        t = self.ada_last
        for m in ms:
            bada = cv[:, CV_BADA + 16 * m:CV_BADA + 16 * (m + 1)].unsqueeze(2).broadcast_to([128, 16, 17])
            t = self.Vtt(self.mod[:, m], self.mod[:, m], bada, ALU.add, deps=[t, self.t_ld])
            goff = {1: CV_GPRE1, 2: CV_GPOST1, 4: CV_GPRE2, 5: CV_GPOST2}.get(m)
            if goff is not None:
                gbc = cv[:, goff:goff + 16].unsqueeze(2).broadcast_to([128, 16, 17])
                if m in (1, 4):
                    t = self.Vstt(self.mod[:, m], self.mod[:, m], 1.0, gbc, ALU.add, ALU.mult, deps=[t])
                else:
                    t = self.Vtt(self.mod[:, m], self.mod[:, m], gbc, ALU.mult, deps=[t])
        self.t_mod = t

    def ada_drain(self):
        while self.ada_pending:
            self.ada_item(self.ada_pending.pop(0))
        self.ada_finalize([2, 3, 4, 5])

    def xT(self, cfg):
        if cfg.lru_only:
            return None
        return self.R1[:, 0:16 * cfg.ntok].rearrange("p (k n) -> p k n", k=16)

    def bfview(self, region, off_bytes, nch, ntok):
        o = off_bytes // 4
        n32 = nch * ntok // 2
        return region[:, o:o + n32].bitcast(BF16).rearrange("p (k n) -> p k n", k=nch)

    def f32view(self, region, off_bytes, nch, ntok):
        o = off_bytes // 4
        return region[:, o:o + nch * ntok].rearrange("p (k n) -> p k n", k=nch)

    def stage_load_x(self, cfg):
        P = self.P
        xT = self.xT(cfg)
        stg = [self.R2[:, 8064:8064 + 2048], self.R2[:, 8064 + 2048:8064 + 4096]]
        stg_free = [None, None]
        last = []
        for ti, (row0, nrows, col0) in enumerate(cfg.xtiles):
            s = ti % 2
            self.x_cnt[s] += 16
            dst = stg[s][0:nrows, :]
            src = self.xq[row0:row0 + nrows, :]
            tl = P.dma("sp", lambda e, dst=dst, src=src: e.dma_start(out=dst, in_=src), self.s_x[s], self.x_cnt[s],
                       deps=[stg_free[s]] + self.last_barrier)
            evs = []
            trs = None
            for g4 in range(4):
                b = self.alloc_bank()
                for qq in range(4):
                    k = g4 * 4 + qq
                    d = [tl] + (self.bank_free[b] if qq == 0 else [])
                    trs = self.TR(self.ps[:, b, qq * 128: qq * 128 + nrows], stg[s][0:nrows, k * 128:(k + 1) * 128],
                                  self.ident[0:nrows, 0:nrows], deps=d, signal=(qq == 3))
                src_ps = self.ps[:, b, :].rearrange("p (q n) -> p q n", q=4)[:, :, 0:nrows]
                dst_x = xT[:, g4 * 4:(g4 + 1) * 4, col0:col0 + nrows]
                if g4 % 2 == 0:
                    te = self.A(dst_x, src_ps, AF.Copy, deps=[trs])
                else:
                    te = self.Vcopy(dst_x, src_ps, deps=[trs])
                self.release_bank(b, [te])
                evs.append(te)
            stg_free[s] = trs
            last = evs
        return [P.cur("act"), P.cur("dve")]

    def stage_front(self, cfg, h):
        P = self.P
        xT = self.xT(cfg)
        NSLOT = 3
        stg = [self.R2[:, 8064 + i * 2048:8064 + (i + 1) * 2048] for i in range(NSLOT)]
        sqb = self.R4[:, 0:2048]
        ssb = [self.R4[:, 2048 + i:2049 + i] for i in range(NSLOT)]
        stg_free = [None] * NSLOT
        AX = mybir.AxisListType.X
        sqf = [None]
        tinfo = {}
        def phaseA(ti):
            row0, nrows, col0 = cfg.xtiles[ti]
            sq_free = sqf[0]
            s = ti % NSLOT
            self.x_cnt[s] += 16
            dst = stg[s][0:nrows, :]
            src = self.xq[row0:row0 + nrows, :]
            tl = P.dma("sp", lambda e, dst=dst, src=src: e.dma_start(out=dst, in_=src), self.s_x[s], self.x_cnt[s],
                       deps=[stg_free[s]] + self.last_barrier)
            tq = self.A(sqb[0:nrows, :], stg[s][0:nrows, :], AF.Square, deps=[tl, sq_free])
            ss = ssb[s][0:nrows, :]
            tr_ = P.op("dve", lambda e, ss=ss, nrows=nrows: e.reduce_sum(out=ss, in_=sqb[0:nrows, :], axis=AX),
                       deps=[tq, stg_free[s]])
            sq_free = tr_
            ta = self.A(ss, ss, AF.Sqrt, scale=1.0 / D, bias=EPS, deps=[tr_])
            trc = P.op("dve", lambda e, ss=ss: e.reciprocal(out=ss, in_=ss), deps=[ta])
            raw_done = []
            if not cfg.lru_only:
                for g4 in range(4):
                    b = self.alloc_bank()
                    trs = None
                    for qq in range(4):
                        k = g4 * 4 + qq
                        d = [tl, self.t_ld] + (self.bank_free[b] if qq == 0 else [])
                        trs = self.TR(self.ps[:, b, qq * 128: qq * 128 + nrows],
                                      stg[s][0:nrows, k * 128:(k + 1) * 128],
                                      self.ident[0:nrows, 0:nrows], deps=d, signal=(qq == 3))
                    src_ps = self.ps[:, b, :].rearrange("p (q n) -> p q n", q=4)[:, :, 0:nrows]
                    dst_x = xT[:, g4 * 4:(g4 + 1) * 4, col0:col0 + nrows]
                    if g4 % 2 == 0:
                        te = self.A(dst_x, src_ps, AF.Copy, deps=[trs])
                    else:
                        te = self.Vcopy(dst_x, src_ps, deps=[trs])
                    self.release_bank(b, [te])
                    raw_done = [trs]
            tsc = self.Vts(stg[s][0:nrows, :], stg[s][0:nrows, :], ss, None, ALU.mult, deps=[trc, tl] + raw_done)
            sqf[0] = sq_free
            tinfo[ti] = (s, nrows, col0, tsc)

        def phaseB(ti):
            s, nrows, col0, tsc = tinfo.pop(ti)
            is_samp = cfg.samp is not None and col0 == 0
            last_tr = None
            for g4 in range(4):
                b = self.alloc_bank()
                trs = None
                for qq in range(4):
                    k = g4 * 4 + qq
                    d = [tsc, self.t_ld] + (self.bank_free[b] if qq == 0 else [])
                    trs = self.TR(self.ps[:, b, qq * 128: qq * 128 + nrows], stg[s][0:nrows, k * 128:(k + 1) * 128],
                                  self.ident[0:nrows, 0:nrows], deps=d, signal=(qq == 3))
                last_tr = trs
                rel = []
                for qq in range(4):
                    k = g4 * 4 + qq
                    src_ps = self.ps[:, b, qq * 128: qq * 128 + nrows]
                    dst_h = h[:, k, col0:col0 + nrows]
                    if is_samp:
                        s3 = src_ps.rearrange("p (t s) -> p t s", t=8)
                        d3 = dst_h.rearrange("p (t s) -> p t s", t=8)
                        scb = self.mod[:, 1, k, 1:17].unsqueeze(1).broadcast_to([128, 8, 16])
                        shb = self.mod[:, 0, k, 1:17].unsqueeze(1).broadcast_to([128, 8, 16])
                        tmp3 = self.R4[:, 2056 + qq * 128:2056 + (qq + 1) * 128].rearrange("p (t s) -> p t s", t=8)
                        t1 = self.Vtt(tmp3, s3, scb, ALU.mult, deps=[trs, self.t_mod])
                        t2 = self.Vtt(d3, tmp3, shb, ALU.add, deps=[t1])
                        rel += [t1, t2]
                    elif g4 % 2 == 0:
                        rel.append(self.A(dst_h, src_ps, AF.Identity, scale=self.mod[:, 1, k, 0:1],
                                          bias=self.mod[:, 0, k, 0:1], deps=[trs, self.t_mod]))
                    else:
                        rel.append(self.Vts(dst_h, src_ps, self.mod[:, 1, k, 0:1], self.mod[:, 0, k, 0:1],
                                            ALU.mult, ALU.add, deps=[trs, self.t_mod]))
                self.release_bank(b, rel)
            stg_free[s] = last_tr
        nt_ = len(cfg.xtiles)
        phaseA(0)
        for ti in range(nt_):
            if ti + 1 < nt_:
                phaseA(ti + 1)
            phaseB(ti)
        toks = [P.cur("act"), P.cur("dve")]
        if cfg.halo:
            p0 = cfg.prm[0]
            hv = h[:, :, p0:p0 + cfg.halo]
            t = self.Vts(hv, hv, self.sel[:, 0:1], None, ALU.mult, deps=toks)
            toks = toks + [t]
        return toks

    def stage_stats(self, cfg, src, src_ready, pre_scale=1.0):
        P = self.P
        ntok = cfg.ntok
        sq = [self.R4[:, 0:ntok // 2].bitcast(BF16), self.R4[:, 512:512 + ntok // 2].bitcast(BF16)]
        sq_free = [None, None]
        banks = [self.alloc_bank() for _ in cfg.groups]
        for b in banks:
            self.bank_reserved.add(b)
        tok = None
        for k in range(DK):
            s = k % 2
            if k % 2 == 0:
                tq = self.A(sq[s][:, 0:ntok], src[:, k, :], AF.Square, deps=[src_ready, sq_free[s]])
            else:
                tq = self.Vtt(sq[s][:, 0:ntok], src[:, k, :], src[:, k, :], ALU.mult, deps=[src_ready, sq_free[s]])
            for gi, (c0, c1) in enumerate(cfg.groups):
                d = [tq] + (self.bank_free[banks[gi]] if k == 0 else [])
                tok = self.MM(self.bank_ap(banks[gi], c1 - c0), self.ones[:], sq[s][:, c0:c1], k == 0, k == DK - 1,
                              deps=d, signal=True)
            sq_free[s] = tok
        toks = []
        for gi, (c0, c1) in enumerate(cfg.groups):
            ta = self.A(self.rstd[:, c0:c1], self.bank_ap(banks[gi], c1 - c0), AF.Sqrt, scale=1.0 / D, bias=EPS,
                        deps=[tok])
            tv = P.op("dve", lambda e, c0=c0, c1=c1: e.reciprocal(out=self.rstd[:, c0:c1], in_=self.rstd[:, c0:c1]),
                      deps=[ta])
            self.release_bank(banks[gi], [ta])
            self.bank_reserved.discard(banks[gi])
            toks.append(tv)
        return toks

    def stage_normmod(self, cfg, src, dst, mi_shift, mi_scale, deps):
        P = self.P
        ntok = cfg.ntok
        p0, p1 = cfg.prm
        tmp = [self.R4[:, 1024:1024 + ntok], self.R4[:, 1024 + NTMAX:1024 + NTMAX + ntok]]
        tmp_free = [None, None]
        for k in range(DK):
            s = k % 2
            t1 = self.Vtt(tmp[s], src[:, k, :], self.rstd[:, 0:ntok], ALU.mult, deps=[deps, tmp_free[s]],
                          eng=("pool" if k % 2 == 1 else "dve"))
            ta = self.A(dst[:, k, p0:p1], tmp[s][:, p0:p1], AF.Identity, scale=self.mod[:, mi_scale, k, 0:1],
                        bias=self.mod[:, mi_shift, k, 0:1], deps=[t1, self.t_mod])
            rel = [ta]
            if cfg.samp is not None:
                v3 = tmp[s][:, 0:128].rearrange("p (t s) -> p t s", t=8)
                scb = self.mod[:, mi_scale, k, 1:17].unsqueeze(1).broadcast_to([128, 8, 16])
                shb = self.mod[:, mi_shift, k, 1:17].unsqueeze(1).broadcast_to([128, 8, 16])
                t2 = self.Vtt(v3, v3, scb, ALU.mult, deps=[t1, self.t_mod])
                t3 = self.Vtt(dst[:, k, 0:128].rearrange("p (t s) -> p t s", t=8), v3, shb, ALU.add, deps=[t2])
                rel.append(t3)
            tmp_free[s] = rel
        toks = [P.cur("act"), P.cur("dve"), P.cur("pool")]
        if cfg.halo:
            hv = dst[:, :, p0:p0 + cfg.halo]
            t = self.Vts(hv, hv, self.sel[:, 0:1], None, ALU.mult, deps=toks)
            toks = [t]
        return toks

    def stage_resid(self, cfg, acc, mi_gate, deps, fuse_stats=False):
        P = self.P
        xT = self.xT(cfg)
        ntok = cfg.ntok
        p0, p1 = cfg.prm
        if fuse_stats:
            sq = [self.R4[:, 0:ntok // 2].bitcast(BF16), self.R4[:, 512:512 + ntok // 2].bitcast(BF16)]
            sq_free = [None, None]
            banks = [self.alloc_bank() for _ in cfg.groups]
            for b in banks:
                self.bank_reserved.add(b)
            tok = None
        for k in range(DK):
            t1 = self.Vtt(acc[:, k, :], acc[:, k, :], self.rstd[:, 0:ntok], ALU.mult, deps=[deps],
                          eng=("pool" if k % 2 == 1 else "dve"))
            done = [self.Vstt(xT[:, k, p0:p1], acc[:, k, p0:p1], self.mod[:, mi_gate, k, 0:1], xT[:, k, p0:p1],
                              ALU.mult, ALU.add, deps=[t1, self.t_mod])]
            if cfg.samp is not None:
                v3 = acc[:, k, 0:128].rearrange("p (t s) -> p t s", t=8)
                gtb = self.mod[:, mi_gate, k, 1:17].unsqueeze(1).broadcast_to([128, 8, 16])
                t2 = self.Vtt(v3, v3, gtb, ALU.mult, deps=[t1, self.t_mod])
                x3 = xT[:, k, 0:128].rearrange("p (t s) -> p t s", t=8)
                done.append(self.Vtt(x3, x3, v3, ALU.add, deps=[t2]))
            if fuse_stats:
                s_ = k % 2
                tq = self.A(sq[s_][:, 0:ntok], xT[:, k, :], AF.Square, deps=done + [sq_free[s_]])
                for gi, (c0, c1) in enumerate(cfg.groups):
                    d = [tq] + (self.bank_free[banks[gi]] if k == 0 else [])
                    tok = self.MM(self.bank_ap(banks[gi], c1 - c0), self.ones[:], sq[s_][:, c0:c1], k == 0,
                                  k == DK - 1, deps=d, signal=True)
                sq_free[s_] = tok
        if not fuse_stats:
            return [P.cur("dve"), P.cur("pool")]
        toks = []
        last_dve = [P.cur("dve"), P.cur("pool")]
        for gi, (c0, c1) in enumerate(cfg.groups):
            ta = self.A(self.rstd[:, c0:c1], self.bank_ap(banks[gi], c1 - c0), AF.Sqrt, scale=1.0 / D, bias=EPS,
                        deps=[tok, last_dve])
            tv = P.op("dve", lambda e, c0=c0, c1=c1: e.reciprocal(out=self.rstd[:, c0:c1], in_=self.rstd[:, c0:c1]),
                      deps=[ta])
            self.release_bank(banks[gi], [ta])
            self.bank_reserved.discard(banks[gi])
            toks.append(tv)
        return toks

    def ext_layout(self, cfg, H):
        if cfg.samp is not None:
            hs = H * 16
            return hs + cfg.ntok, hs, hs + 128, hs
        return H + cfg.ntok, H, H, None

    def run_pass(self, cfg):
        P = self.P
        nt = cfg.ntok
        p0, p1 = cfg.prm
        Lp = p1 - p0
        groups = cfg.groups
        cv = self.cvec
        xT = self.xT(cfg)
        h = self.bfview(self.R2, 0, 16, nt)
        if cfg.lru_only:
            th = self.stage_front(cfg, h)
            self.barrier()
            self.stage_lru(cfg, h, th, None)
            self.barrier()
            return
        y_pool = self.bfview(self.R2, 21504, 8, nt)
        y_lru = self.bfview(self.R2, 32256, 16, nt)
        o_sb = self.f32view(self.R2, 0, 16, nt)
        merged = self.bfview(self.R4, 0, 16, nt)
        h2 = self.bfview(self.R4, 21504, 16, nt)
        d_sb = self.f32view(self.R4, 0, 16, nt)
        f = self.bfview(self.R2, 0, 48, nt)

        th = self.stage_front(cfg, h)
        self.barrier()
        h_ready = th

        if not cfg.lru_only:
            self.stage_pool(cfg, h, h_ready, y_pool)
            self.barrier()
        self.stage_lru(cfg, h, h_ready, y_lru)
        self.barrier()
        if cfg.lru_only:
            return
        self.stage_merge(cfg, h, y_pool, y_lru, merged)
        self.barrier()
        self.stage_proj_norm(cfg, merged, self.w_out, 16, o_sb, 0.5)
        if self.ada_pending is not None and not self.ada_mid_done:
            while self.ada_pending and self.ada_pending[0] < 20:
                self.ada_item(self.ada_pending.pop(0))
            self.ada_finalize([2, 3, 4])
            self.ada_mid_done = True
        ts = self.stage_resid(cfg, o_sb, 2, [P.cur("dve"), P.cur("act")], fuse_stats=True)
        th2 = self.stage_normmod(cfg, xT, h2, 3, 4, ts)
        self.barrier()
        self.stage_ffn_up(cfg, h2, f)
        if self.ada_pending is not None:
            while self.ada_pending:
                self.ada_item(self.ada_pending.pop(0))
            self.ada_finalize([5])
            self.ada_pending = None
        self.barrier(self.so_all())
        self.stage_proj_norm(cfg, f, self.w_fdn, 48, d_sb, 1.0)
        self.stage_resid(cfg, d_sb, 5, [P.cur("dve"), P.cur("act")])
        self.barrier()
        self.stage_store_y(cfg)
        self.barrier(self.out_tokens)

    def stage_pool(self, cfg, h, h_ready, y_pool):
        P = self.P
        nt = cfg.ntok
        p0, p1 = cfg.prm
        Lp = p1 - p0
        cv = self.cvec
        W, cur, prm, scur = self.ext_layout(cfg, 15)
        def u_ap(c, i):
            o = (c * 3 + i) * 928
            return self.R4[:, o:o + W]
        dbuf = [self.R4[:, 6 * 928 + c * 336: 6 * 928 + c * 336 + nt // 2].bitcast(BF16) for c in range(2)]
        sstage = self.R4[:, 7256:7256 + 2048]
        t_hl = None
        if cfg.samp is not None:
            for r in range(15):
                tno, rr = (0, r) if r < 8 else (1, r - 8)
                self.misc_dma("sp", sstage[rr * 16:(rr + 1) * 16, tno * 1024:(tno + 1) * 1024], self.st_pool[:, r, :],
                              deps=self.last_barrier)
            t_hl = (self.s_misc, self.misc_cnt)
        fix = self.R4[:, 6 * 928 + 2 * 336 + 512: 6 * 928 + 2 * 336 + 512 + 16]
        so_stage = self.R4[:, 7000:7000 + 256]
        prev_done = None
        for g in range(4):
            w = 2 ** (g + 1)
            if self.ada_pending and cfg.samp is not None and self.ada_pending[0] < 16:
                self.ada_item(self.ada_pending.pop(0))
            wsl, wt, wi = self.wload(self.w_in[:, g * 256:(g + 1) * 256])
            dtoks = []
            hist_b = []
            if cfg.samp is not None:
                for c in range(2):
                    b = self.alloc_bank()
                    tt = None
                    for tno, nrow in enumerate((128, 112)):
                        d = [t_hl] + (self.bank_free[b] if tno == 0 else [])
                        tt = self.TR(self.ps[:, b, tno * 128: tno * 128 + nrow],
                                     sstage[0:nrow, tno * 1024 + (2 * g + c) * 128: tno * 1024 + (2 * g + c + 1) * 128],
                                     self.ident[0:nrow, 0:nrow], deps=d, signal=(tno == 1))
                    hist_b.append((b, tt))
            zjobs = []
            for c in range(2):
                banks, tok = self.job(cfg.groups, [(wsl[:, k, c * 128:(c + 1) * 128],
                                                    (lambda c0, c1, k=k: h[:, k, c0:c1]), [wt, h_ready])
                                                   for k in range(DK)])
                zjobs.append((banks, tok))
            self.wrelease(wi, zjobs[1][1])
            evs_c = []
            for c in range(2):
                ch = 2 * g + c
                U = u_ap(c, 0)
                banks, tok = zjobs[c]
                evs = []
                for gi, (c0, c1) in enumerate(cfg.groups):
                    evs.append(self.A(U[:, cur + c0:cur + c1], self.bank_ap(banks[gi], c1 - c0), AF.Copy,
                                      deps=[tok, prev_done]))
                    self.release_bank(banks[gi], [evs[-1]])
                if cfg.samp is not None:
                    b, tt = hist_b[c]
                    tevh = self.Vcopy(U[:, 0:240], self.ps[:, b, 0:240], deps=[tt, prev_done])
                    self.release_bank(b, [tevh])
                    evs.append(tevh)
                else:
                    evs.append(self.Vcopy(U[:, 0:15], self.hist_pool[:, ch, :], deps=[prev_done, self.t_init]))
                evs_c.append(evs)
                st_ = 16 if cfg.samp is not None else 1
                regions = []
                if cfg.samp is not None:
                    regions.append((0, 240 + 128, 16, 240))
                    regions.append((prm - 15, W, 1, prm))
                else:
                    regions.append((0, W, 1, cur))
                bufs = [U, u_ap(c, 1), u_ap(c, 2)]
                tlast = evs
                for (r0, r1, strd, fo) in regions:
                    srcb = U
                    di = 1
                    sh = 1
                    tl = tlast
                    lo = r0
                    while sh < w:
                        dstb = bufs[di]
                        lo2 = lo + sh * strd
                        tl = [self.Vtt(dstb[:, lo2:r1], srcb[:, lo2:r1], srcb[:, lo2 - sh * strd:r1 - sh * strd],
                                       ALU.add, deps=tl)]
                        srcb = dstb
                        di = 2 if di == 1 else 1
                        lo = lo2
                        sh *= 2
                    n = r1 - fo
                    dcol = 0 if (cfg.samp is not None and strd == 16) else p0
                    td = self.Vstt(dbuf[c][:, dcol:dcol + n], srcb[:, fo:r1], 1.0 / w, U[:, fo:r1], ALU.mult,
                                   ALU.subtract, deps=tl)
                    dtoks.append(td)
                    tlast = evs + [td]
                    if cfg.halo and strd == 1:
                        m0 = fo + cfg.halo
                        tf = self.Vtt(fix, srcb[:, m0:m0 + 16], self.invc[:, g, :], ALU.mult, deps=tl)
                        td2 = self.Vtt(dbuf[c][:, p0 + cfg.halo:p0 + cfg.halo + 16], fix, U[:, m0:m0 + 16],
                                       ALU.subtract, deps=[tf, td])
                        dtoks.append(td2)
                tsv = self.A(self.hist_pool[:, ch, :], U[:, W - 15:W], AF.Copy, deps=evs)
                dtoks.append(tsv)
            if cfg.samp is not None:
                for c in range(2):
                    U = u_ap(c, 0)
                    b = self.alloc_bank()
                    tt = self.TR(self.ps[:, b, 0:128], U[:, 240:368], self.ident[:],
                                 deps=evs_c[c] + self.bank_free[b], signal=True)
                    te = self.A(so_stage[:, c * 128:(c + 1) * 128], self.ps[:, b, 0:128], AF.Copy,
                                deps=[tt, prev_done])
                    self.release_bank(b, [te])
                    dtoks += [te, tt]
                so_toks = []
                for t in range(8):
                    so_toks.append(self.so_dma(self.o_pool_s[:, 7 + t, g * 256:(g + 1) * 256],
                                               so_stage[t * 16:(t + 1) * 16, :], deps=dtoks, stream="pool"))
                t_so = self.so_tok("pool")
            ytoks = []
            for j in range(2):
                banks, tok = self.job(cfg.groups, [(self.wgrp[:, g, kk, j * 128:(j + 1) * 128],
                                                    (lambda c0, c1, kk=kk: dbuf[kk][:, c0:c1]),
                                                    [self.t_wgrp] + dtoks) for kk in range(2)])
                for gi, (c0, c1) in enumerate(cfg.groups):
                    te = self.A(y_pool[:, 2 * g + j, c0:c1], self.bank_ap(banks[gi], c1 - c0), AF.Copy,
                                scale=cv[:, CV_PSCALE + 2 * g + j:CV_PSCALE + 2 * g + j + 1], deps=[tok])
                    self.release_bank(banks[gi], [te])
                    ytoks.append(te)
            prev_done = [P.cur("pe"), P.cur("dve"), P.cur("act")]
            if cfg.samp is not None:
                prev_done = prev_done + [t_so]
        if cfg.samp is not None:
            self.so_dma(self.o_pool_s[:, 0:7, :], self.st_pool[:, 8:15, :], stream="carry")
            self.out_tokens += self.so_all()

    def wsm_load(self, blk):
        i = self.wsm_next
        self.wsm_next = (i + 1) % 2
        self.wsm_cnt[i] += 32
        P = self.P
        dst0 = self.wsm[i][:, 0]
        dst1 = self.wsm[i][:, 1]
        s0 = self.w_rg[blk].rearrange("(k p) n -> p k n", p=128)
        s1 = self.w_ig[blk].rearrange("(k p) n -> p k n", p=128)
        P.dma("pool", lambda e: e.dma_start(out=dst0, in_=s0), self.wsm_sem[i], 0, deps=[self.wsm_rel[i]])
        tok = P.dma("pool", lambda e: e.dma_start(out=dst1, in_=s1), self.wsm_sem[i], self.wsm_cnt[i])
        return self.wsm[i], tok, i

    def stage_lru(self, cfg, h, h_ready, y_lru):
        P = self.P
        nt = cfg.ntok
        p0, p1 = cfg.prm
        Lp = p1 - p0
        cv = self.cvec
        W, cur, prm, scur = self.ext_layout(cfg, 3)
        samp = cfg.samp is not None
        big = nt > NTMAX
        NTP = 992 if big else NTMAX
        UW = 1000 if big else 768
        CS, GSZ = UW + NTP, 3 * NTP
        def U_ap(s_, c):
            o = (s_ * 2 + c) * CS
            return self.R4[:, o:o + W]
        def xc_ap(s_, c):
            o = (s_ * 2 + c) * CS + UW
            return self.R4[:, o:o + nt]
        def ap3(base, stride):
            return bass.AP(base.tensor, base.offset, [list(base.ap[0]), [stride, 2], [1, nt]])
        def xc3(s_):
            return ap3(xc_ap(s_, 0), CS)
        NGS = 2 if cfg.lru_only else 1
        gs_cur = [0]
        def g_ap(c, i):
            if big:
                reg, base = (self.R1, 0) if gs_cur[0] == 0 else (self.R2, 8064)
                o = base + c * GSZ + i * NTP
                return reg[:, o:o + nt]
            if gs_cur[0] == 1:
                o = c * GSZ + i * NTP
                return self.R1[:, o:o + nt]
            o = 5760 + c * GSZ + i * NTP
            return self.R4[:, o:o + nt]
        def g3(i):
            return ap3(g_ap(0, i), GSZ)
        XH = NTP // 2
        if big:
            xcb_t = [self.R1[:, 5952:5952 + 2 * XH], self.R1[:, 5952 + 2 * XH:5952 + 4 * XH]]
        else:
            xcb_t = [self.rstd, self.sq_scratch]
        def xcb_ap(s_, c):
            return xcb_t[s_][:, c * XH:c * XH + nt // 2].bitcast(BF16)
        def xcb3(s_):
            return xcb_t[s_][:, 0:2 * XH].bitcast(BF16).rearrange("p (c n) -> p c n", c=2)[:, :, 0:nt]
        stage_start = list(self.last_barrier)
        if samp:
            st_l = self.R2[:, 13440:13440 + 2048]
            for r in range(3):
                self.misc_dma("sp", st_l[r * 16:(r + 1) * 16, :], self.st_lconv[:, r, :], deps=stage_start)
            self.misc_dma("sp", st_l[48:64, :], self.st_lh, deps=stage_start)
            t_stl = (self.s_misc, self.misc_cnt)
            h0s = self.R2[:, 15488:15488 + 256].rearrange("p (k s) -> p k s", k=16)
            so_conv = self.R2[:, 15744:15744 + 256]
        st = {"conv_free": [None, None], "xcb_free": [None, None], "gate_free": [None, None], "so1": None, "so2": None,
              "s1": {}}

        def S1A(blk):
            s_ = blk % 2
            if self.ada_pending:
                if cfg.lru_only and blk % 2 == 0 and self.ada_pending[0] < 12:
                    self.ada_item(self.ada_pending.pop(0))
                elif samp and blk % 2 == 0 and self.ada_pending[0] < 20:
                    self.ada_item(self.ada_pending.pop(0))
            wsl, wt, wi = self.wload(self.w_in[:, 1024 + blk * 256:1024 + (blk + 1) * 256])
            wg, wgt, wgi = self.wsm_load(blk)
            cfree = st["conv_free"][s_]
            tap0 = []
            evs_c = []
            hist_toks = []
            hist_tr = []
            if samp:
                for c in range(2):
                    ch = 2 * blk + c
                    b = self.alloc_bank()
                    tt = self.TR(self.ps[:, b, 0:64], st_l[0:64, ch * 128:(ch + 1) * 128], self.ident[0:64, 0:64],
                                 deps=[t_stl] + self.bank_free[b], signal=True)
                    hist_tr.append((b, tt))
            jobs = []
            for c in range(2):
                banks, tok = self.job(cfg.groups, [(wsl[:, k, c * 128:(c + 1) * 128],
                                                    (lambda c0, c1, k=k: h[:, k, c0:c1]), [wt, h_ready])
                                                   for k in range(DK)])
                jobs.append((banks, tok))
            self.wrelease(wi, jobs[1][1])
            for c in range(2):
                ch = 2 * blk + c
                U = U_ap(s_, c)
                xc = xc_ap(s_, c)
                banks, tok = jobs[c]
                evs = []
                if samp:
                    b, tt = hist_tr[c]
                    te1 = self.Vcopy(U[:, 0:48], self.ps[:, b, 0:48], deps=[tt, cfree])
                    te2 = self.Vcopy(h0s[:, ch, :], self.ps[:, b, 48:64], deps=[tt])
                    self.release_bank(b, [te1, te2])
                    evs += [te1, te2]
                else:
                    evs.append(self.Vcopy(U[:, 0:3], self.hist_lru[:, ch, :], deps=[cfree, self.t_init]))
                for gi, (c0, c1) in enumerate(cfg.groups):
                    evs.append(self.A(U[:, cur + c0:cur + c1], self.bank_ap(banks[gi], c1 - c0), AF.Copy,
                                      deps=[tok, cfree]))
                    self.release_bank(banks[gi], [evs[-1]])
                wl3 = cv[:, CV_WLC + 48 + ch:CV_WLC + 48 + ch + 1]
                bl = cv[:, CV_BLC + ch:CV_BLC + ch + 1]
                regs = []
                if samp:
                    regs.append((48, 128, 16, 0))
                    regs.append((prm, Lp, 1, p0))
                else:
                    regs.append((cur, Lp, 1, p0))
                t0s = []
                for (co, n, strd, xo) in regs:
                    t0s.append(self.A(xc[:, xo:xo + n], U[:, co:co + n], AF.Identity, scale=wl3, bias=bl,
                                      deps=evs + [cfree]))
                tap0.append((regs, t0s))
                evs_c.append(evs)
                hist_toks.append(self.A(self.hist_lru[:, ch, :], U[:, W - 3:W], AF.Copy, deps=evs))
            if samp:
                for c in range(2):
                    U = U_ap(s_, c)
                    b = self.alloc_bank()
                    tt = self.TR(self.ps[0:48, b, 0:128], U[:, 128:176], self.ident[:],
                                 deps=evs_c[c] + self.bank_free[b], signal=True)
                    te = self.A(so_conv[0:48, c * 128:(c + 1) * 128], self.ps[0:48, b, 0:128], AF.Copy,
                                deps=[tt, st["so1"]])
                    self.release_bank(b, [te])
                    hist_toks += [te, tt]
                for r in range(3):
                    self.so_dma(self.o_lconv_s[:, r, blk * 256:(blk + 1) * 256], so_conv[r * 16:(r + 1) * 16, :],
                                deps=hist_toks, stream="lru1")
                st["so1"] = self.so_tok("lru1")
            st["s1"][blk] = dict(wg=wg, wgt=wgt, wgi=wgi, tap0=tap0, evs=evs_c, ureaders=list(hist_toks))

        def S1B(blk):
            s_ = blk % 2
            info = st["s1"][blk]
            ctoks_all = []
            for c in range(2):
                ch = 2 * blk + c
                U = U_ap(s_, c)
                xc = xc_ap(s_, c)
                regs, t0s = info["tap0"][c]
                for ri, (co, n, strd, xo) in enumerate(regs):
                    t = t0s[ri]
                    for k in range(3):
                        sh = (3 - k) * strd
                        t = self.Vstt(xc[:, xo:xo + n], U[:, co - sh:co - sh + n],
                                      cv[:, CV_WLC + 16 * k + ch:CV_WLC + 16 * k + ch + 1], xc[:, xo:xo + n],
                                      ALU.mult, ALU.add, deps=[t] + info["evs"][c])
                    ctoks_all.append(t)
            tb = self.Vcopy(xcb3(s_), xc3(s_), deps=ctoks_all + [st["xcb_free"][s_]])
            info["tb"] = tb
            info["ureaders"] += ctoks_all

        def P1(blk):
            s_ = blk % 2
            info = st["s1"][blk]
            wg, wgt, wgi, tb = info["wg"], info["wgt"], info["wgi"], info["tb"]
            gs_cur[0] = blk % NGS
            gfree = st["gate_free"][blk % NGS]
            tanh_r, tanh_i = [], []
            lastpe = None
            for j in range(2):
                ch = 2 * blk + j
                a = g_ap(j, 0)
                g = g_ap(j, 2)
                for (gi_, dst, brow, lst) in ((0, a, 0, tanh_r), (1, g, 1, tanh_i)):
                    banks, tok = self.job(cfg.groups, [(wg[:, gi_, kk, j * 128:(j + 1) * 128],
                                                        (lambda c0, c1, kk=kk: xcb_ap(s_, kk)[:, c0:c1]), [wgt, tb])
                                                       for kk in range(2)])
                    lastpe = tok
                    for gi, (c0, c1) in enumerate(cfg.groups):
                        t = self.A(dst[:, c0:c1], self.bank_ap(banks[gi], c1 - c0), AF.Tanh, scale=0.5,
                                   bias=self.dv[:, brow, ch:ch + 1], deps=[tok, gfree, self.t_dv])
                        self.release_bank(banks[gi], [t])
                        lst.append(t)
            self.wsm_rel[wgi] = lastpe
            st["xcb_free"][s_] = lastpe
            ta = []
            for j in range(2):
                ch = 2 * blk + j
                a = g_ap(j, 0)
                ta.append(self.A(a, a, AF.Exp, scale=self.dv[:, 3, ch:ch + 1], bias=self.dv[:, 3, ch:ch + 1],
                                 deps=tanh_r))
            tgx = self.Vstt(g3(2), g3(2), 1.0, xc3(s_), ALU.add, ALU.mult, deps=tanh_i + [tb])
            st["conv_free"][s_] = info["ureaders"] + [tgx, tb]
            tm = self.A(g3(1), g3(0), AF.Square, deps=ta + [gfree])
            info.update(ta=ta, tgx=tgx, tm=tm)

        def P2a(blk):
            gs_cur[0] = blk % NGS
            info = st["s1"][blk]
            info["tsq"] = self.A(g3(1), g3(1), AF.Sqrt, scale=-0.25, bias=0.25, deps=[info["tm"]])

        def P2(blk):
            gs_cur[0] = blk % NGS
            info = st["s1"][blk]
            ta, tgx, tsq = info["ta"], info["tgx"], info["tsq"]
            tuu = self.Vtt(g3(2), g3(2), g3(1), ALU.mult, deps=[tgx, tsq])
            fin = []
            for j in range(2):
                ch = 2 * blk + j
                a, hs, g = g_ap(j, 0), g_ap(j, 1), g_ap(j, 2)
                hc = self.h_carry[:, ch:ch + 1]
                if cfg.halo:
                    t1 = P.op("dve", lambda e, a=a, g=g, hs=hs, hc=hc: e.tensor_tensor_scan(
                        out=hs[:, p0:p0 + HALO], data0=a[:, p0:p0 + HALO], data1=g[:, p0:p0 + HALO], initial=hc,
                        op0=ALU.mult, op1=ALU.add), deps=[tuu] + ta + [self.t_init])
                    hm = self.R2[:, 16100 + j:16101 + j]
                    t2 = self.Vts(hm, hs[:, p0 + HALO - 1:p0 + HALO], self.sel[:, 0:1], None, ALU.mult, deps=[t1])
                    t3 = P.op("dve", lambda e, a=a, g=g, hs=hs, hm=hm: e.tensor_tensor_scan(
                        out=hs[:, p0 + HALO:p1], data0=a[:, p0 + HALO:p1], data1=g[:, p0 + HALO:p1], initial=hm,
                        op0=ALU.mult, op1=ALU.add), deps=[t2])
                else:
                    t3 = P.op("dve", lambda e, a=a, g=g, hs=hs, hc=hc: e.tensor_tensor_scan(
                        out=hs[:, p0:p1], data0=a[:, p0:p1], data1=g[:, p0:p1], initial=hc,
                        op0=ALU.mult, op1=ALU.add), deps=[tuu] + ta + [self.t_init])
                t4 = self.Vcopy(hc, hs[:, p1 - 1:p1], deps=[t3])
                fin += [t3, t4]
            if samp:
                a3, hs3, gg3 = g3(0), g3(1), g3(2)
                prev = h0s[:, 2 * blk:2 * blk + 2, :]
                t = [tuu] + ta
                for tstep in range(8):
                    sl_ = slice(tstep * 16, (tstep + 1) * 16)
                    t = [self.Vtt(hs3[:, :, sl_], a3[:, :, sl_], prev, ALU.mult, deps=t)]
                    t = [self.Vtt(hs3[:, :, sl_], hs3[:, :, sl_], gg3[:, :, sl_], ALU.add, deps=t)]
                    prev = hs3[:, :, sl_]
                fin += t
            info["fin"] = fin

        def P3(blk):
            gs_cur[0] = blk % NGS
            info = st["s1"].pop(blk)
            fin = info["fin"]
            if not cfg.lru_only:
                ty = self.A(y_lru[:, 2 * blk:2 * blk + 2, :], g3(1), AF.Copy, deps=fin)
                fin.append(ty)
            if samp:
                so_t = []
                for j in range(2):
                    hs = g_ap(j, 1)
                    b = self.alloc_bank()
                    tt = self.TR(self.ps[0:16, b, 0:128], hs[:, 112:128], self.ident[:],
                                 deps=fin + self.bank_free[b], signal=True)
                    te = self.A(so_conv[64:80, j * 128:(j + 1) * 128], self.ps[0:16, b, 0:128], AF.Copy,
                                deps=[tt, st["so2"]])
                    self.release_bank(b, [te])
                    so_t += [te, tt]
                self.so_dma(self.o_lh_s[:, blk * 256:(blk + 1) * 256], so_conv[64:80, :], deps=so_t, stream="lru2")
                st["so2"] = self.so_tok("lru2")
                fin += so_t
            st["gate_free"][blk % NGS] = fin

        S1A(0)
        S1B(0)
        S1A(1)
        S1B(1)
        for blk in range(8):
            P1(blk)
            P2a(blk)
            if blk + 2 < 8:
                S1A(blk + 2)
            P2(blk)
            if blk + 2 < 8:
                S1B(blk + 2)
            P3(blk)
        if samp:
            self.out_tokens += self.so_all()

    def stage_merge(self, cfg, h, y_pool, y_lru, merged):
        P = self.P
        nt = cfg.ntok
        base = 5376
        gb = [self.R4[:, base + i * NTMAX: base + i * NTMAX + nt] for i in range(4)]
        prev_done = None
        for q in range(8):
            specs = [(self.w_in[:, 3072 + q * 256:3072 + (q + 1) * 256], 16, h, 0),
                     (self.w_in[:, 5120 + q * 256:5120 + (q + 1) * 256], 16, h, 2)]
            for (src, kch, opnd, bi) in specs:
                wsl, wt, wi = self.wload(src, kch)
                for j in range(2):
                    banks, tok = self.job(cfg.groups, [(wsl[:, k, j * 128:(j + 1) * 128],
                                                        (lambda c0, c1, k=k, opnd=opnd: opnd[:, k, c0:c1]), [wt])
                                                       for k in range(kch)])
                    if j == 1:
                        self.wrelease(wi, tok)
                    for gi, (c0, c1) in enumerate(cfg.groups):
                        t = self.A(gb[bi + j][:, c0:c1], self.bank_ap(banks[gi], c1 - c0), AF.Tanh, scale=0.5,
                                   deps=[tok, prev_done])
                        self.release_bank(banks[gi], [t])
            tg = P.cur("act")
            specs = [(self.w_pup[:, q * 256:(q + 1) * 256], 8, y_pool, 0),
                     (self.w_lup[:, q * 256:(q + 1) * 256], 16, y_lru, 2)]
            for (src, kch, opnd, bi) in specs:
                wsl, wt, wi = self.wload(src, kch)
                for j in range(2):
                    banks, tok = self.job(cfg.groups, [(wsl[:, k, j * 128:(j + 1) * 128],
                                                        (lambda c0, c1, k=k, opnd=opnd: opnd[:, k, c0:c1]), [wt])
                                                       for k in range(kch)])
                    if j == 1:
                        self.wrelease(wi, tok)
                    for gi, (c0, c1) in enumerate(cfg.groups):
                        t = self.Vstt(gb[bi + j][:, c0:c1], gb[bi + j][:, c0:c1], 1.0,
                                      self.bank_ap(banks[gi], c1 - c0), ALU.add, ALU.mult, deps=[tok, tg])
                        self.release_bank(banks[gi], [t])
            tv = P.cur("dve")
            for j in range(2):
                self.Vtt(merged[:, 2 * q + j, :], gb[j][:, 0:nt], gb[2 + j][:, 0:nt], ALU.add, deps=[tv])
            prev_done = [P.cur("dve")]

    def stage_proj_norm(self, cfg, opnd, wsrc, kchunks, acc, evac_scale):
        P = self.P
        nt = cfg.ntok
        nparts = kchunks // 16
        sqb = [self.sq_scratch[:, i * 336:i * 336 + nt // 2].bitcast(BF16) for i in range(2)]
        sq_free = [None, None]
        sbanks = [self.alloc_bank() for _ in cfg.groups]
        for b in sbanks:
            self.bank_reserved.add(b)
        pend = None
        stok = None
        nsq = 0

        def flush(pend, first, last):
            (tq, s) = pend
            tk = None
            for gi, (c0, c1) in enumerate(cfg.groups):
                d = [tq] + (self.bank_free[sbanks[gi]] if first else [])
                tk = self.MM(self.bank_ap(sbanks[gi], c1 - c0), self.ones[:], sqb[s][:, c0:c1], first, last,
                             deps=d, signal=True)
            sq_free[s] = tk
            return tk

        for cb in range(8):
            open_jobs = []
            for j in range(2):
                open_jobs.append([self.alloc_bank() for _ in cfg.groups])
            toks = [None, None]
            for kp in range(nparts):
                src = wsrc[kp * 2048:(kp + 1) * 2048, cb * 256:(cb + 1) * 256]
                wsl, wt, wi = self.wload(src)
                for j in range(2):
                    banks = open_jobs[j]
                    for k in range(DK):
                        first = (kp == 0 and k == 0)
                        last = (kp == nparts - 1 and k == DK - 1)
                        for gi, (c0, c1) in enumerate(cfg.groups):
                            d = [wt] + (self.bank_free[banks[gi]] if first else [])
                            sig = (k == DK - 1 and gi == len(cfg.groups) - 1)
                            t = self.MM(self.bank_ap(banks[gi], c1 - c0), wsl[:, k, j * 128:(j + 1) * 128],
                                        opnd[:, kp * 16 + k, c0:c1], first, last, deps=d, signal=sig)
                            if sig:
                                toks[j] = t
                self.wrelease(wi, toks[1])
            for j in range(2):
                o = 2 * cb + j
                banks = open_jobs[j]
                evs = []
                for gi, (c0, c1) in enumerate(cfg.groups):
                    te = self.A(acc[:, o, c0:c1], self.bank_ap(banks[gi], c1 - c0), AF.Copy, scale=evac_scale,
                                deps=[toks[j]])
                    self.release_bank(banks[gi], [te])
                    evs.append(te)
                s = nsq % 2
                tq = self.Vtt(sqb[s][:, 0:nt], acc[:, o, :], acc[:, o, :], ALU.mult, deps=evs + [sq_free[s]])
                if pend is not None:
                    stok = flush(pend, nsq == 1, False)
                pend = (tq, s)
                nsq += 1
        stok = flush(pend, False, True)
        for gi, (c0, c1) in enumerate(cfg.groups):
            ta = self.A(self.rstd[:, c0:c1], self.bank_ap(sbanks[gi], c1 - c0), AF.Sqrt, scale=1.0 / D, bias=EPS,
                        deps=[stok])
            P.op("dve", lambda e, c0=c0, c1=c1: e.reciprocal(out=self.rstd[:, c0:c1], in_=self.rstd[:, c0:c1]),
                 deps=[ta])
            self.release_bank(sbanks[gi], [ta])
            self.bank_reserved.discard(sbanks[gi])

    def stage_ffn_up(self, cfg, h2, f):
        P = self.P
        nt = cfg.ntok
        p0, p1 = cfg.prm
        Lp = p1 - p0
        cv = self.cvec
        W, cur, prm, scur = self.ext_layout(cfg, 2)
        def U_ap(slot, gv):
            o = slot * 2752 + gv * 704
            return self.R4[:, o:o + W]
        def C_ap(slot, gv):
            if slot == 1 and gv == 1:
                return self.sq_scratch[:, 0:nt]
            o = slot * 2752 + 1408 + gv * 672
            return self.R4[:, o:o + nt]
        st_fs = [self.rstd[:, 0:256], self.rstd[:, 256:512]]
        so_f = self.cT[:].rearrange("p k j -> p (k j)")[:, 0:256]
        stf_rd = [[], []]
        stf_tok = [None, None]

        def prefetch_hist(jf_):
            bi = jf_ % 2
            for gv_ in range(2):
                for r in range(2):
                    self.stf_cnt[bi] += 16
                    dst = st_fs[bi][r * 16:(r + 1) * 16, gv_ * 128:(gv_ + 1) * 128]
                    src = self.st_fconv[:, r, gv_ * DFF + jf_ * 128: gv_ * DFF + (jf_ + 1) * 128]
                    P.dma("sp", lambda e, dst=dst, src=src: e.dma_start(out=dst, in_=src), self.s_stf[bi],
                          self.stf_cnt[bi], deps=stf_rd[bi] + stage_start)
            stf_tok[bi] = (self.s_stf[bi], self.stf_cnt[bi])
            stf_rd[bi] = []
        stage_start = list(self.last_barrier)
        slot_free = [None, None]
        grp_so = []
        grp_rd = []
        pending_tail = None
        wrel = []

        def emit_tail(tl):
            (Cg, tg, Cv, tvv, jf_, slot_) = tl
            tgl = self.A(Cg[:, 0:nt], Cg[:, 0:nt], AF.Gelu_apprx_tanh, deps=tg)
            tf = self.Vtt(f[:, jf_, :], Cg[:, 0:nt], Cv[:, 0:nt], ALU.mult, deps=[tgl] + tvv)
            slot_free[slot_] = slot_free[slot_] + [tf]

        it = 0
        for q in range(24):
            if self.ada_pending and q % 6 == 0:
                self.ada_item(self.ada_pending.pop(0))
            wg_, wgt, wgi = self.wload(self.w_fup[:, q * 256:(q + 1) * 256])
            wv_, wvt, wvi = self.wload(self.w_fup[:, DFF + q * 256:DFF + (q + 1) * 256])
            lastpe = None
            for j in range(2):
                jf = 2 * q + j
                slot = it % 2
                it += 1
                if cfg.samp is not None:
                    if jf == 0:
                        prefetch_hist(0)
                    if jf + 1 < 48:
                        prefetch_hist(jf + 1)
                sfree = slot_free[slot]
                jobs = []
                for gv, (wsl, wt) in enumerate(((wg_, wgt), (wv_, wvt))):
                    banks, tok = self.job(cfg.groups, [(wsl[:, k, j * 128:(j + 1) * 128],
                                                        (lambda c0, c1, k=k: h2[:, k, c0:c1]), [wt])
                                                       for k in range(DK)])
                    jobs.append((banks, tok))
                    lastpe = tok
                hist_ps = []
                if cfg.samp is not None:
                    st_f = st_fs[jf % 2]
                    b = self.alloc_bank()
                    for gv in range(2):
                        tt = self.TR(self.ps[:, b, gv * 32:(gv + 1) * 32], st_f[0:32, gv * 128:(gv + 1) * 128],
                                     self.ident[0:32, 0:32],
                                     deps=[stf_tok[jf % 2]] + (self.bank_free[b] if gv == 0 else []), signal=True)
                        stf_rd[jf % 2].append(tt)
                        hist_ps.append((b, tt))
                        lastpe = tt
                evs_all = []
                for gv in range(2):
                    U = U_ap(slot, gv)
                    banks, tok = jobs[gv]
                    evs = []
                    for gi, (c0, c1) in enumerate(cfg.groups):
                        te = self.A(U[:, cur + c0:cur + c1], self.bank_ap(banks[gi], c1 - c0), AF.Copy,
                                    deps=[tok, sfree])
                        self.release_bank(banks[gi], [te])
                        evs.append(te)
                    evs_all.append(evs)
                for gv in range(2):
                    U = U_ap(slot, gv)
                    chn = jf + gv * 48
                    if cfg.samp is not None:
                        b, tt = hist_ps[gv]
                        te = self.Vcopy(U[:, 0:32], self.ps[:, b, gv * 32:(gv + 1) * 32],
                                        deps=[hist_ps[0][1], hist_ps[1][1], sfree])
                        if gv == 0:
                            hist_rel = [te]
                        else:
                            self.release_bank(b, hist_rel + [te])
                    else:
                        te = self.Vcopy(U[:, 0:2], self.hist_up[:, chn, :], deps=[sfree, self.t_init])
                    evs_all[gv].append(te)
                regs = []
                if cfg.samp is not None:
                    regs.append((32, 128, 16, 0))
                    regs.append((prm, Lp, 1, p0))
                else:
                    regs.append((cur, Lp, 1, p0))
                c0t = [[], []]
                for gv in range(2):
                    U = U_ap(slot, gv)
                    C = C_ap(slot, gv)
                    chn = jf + gv * 48
                    for (co, n, strd, xo) in regs:
                        c0t[gv].append(self.A(C[:, xo:xo + n], U[:, co:co + n], AF.Identity,
                                              scale=cv[:, CV_WFC + 192 + chn:CV_WFC + 192 + chn + 1],
                                              bias=cv[:, CV_BFC + chn:CV_BFC + chn + 1], deps=evs_all[gv] + [sfree]))
                ctoks = [[], []]
                for gv in range(2):
                    U = U_ap(slot, gv)
                    C = C_ap(slot, gv)
                    chn = jf + gv * 48
                    for ri, (co, n, strd, xo) in enumerate(regs):
                        t = c0t[gv][ri]
                        for k in range(2):
                            sh = (2 - k) * strd
                            t = self.Vstt(C[:, xo:xo + n], U[:, co - sh:co - sh + n],
                                          cv[:, CV_WFC + 96 * k + chn:CV_WFC + 96 * k + chn + 1], C[:, xo:xo + n],
                                          ALU.mult, ALU.add, deps=[t] + evs_all[gv])
                        ctoks[gv].append(t)
                ureaders = []
                for gv in range(2):
                    U = U_ap(slot, gv)
                    chn = jf + gv * 48
                    tsv = self.A(self.hist_up[:, chn, :], U[:, W - 2:W], AF.Copy, deps=evs_all[gv])
                    ureaders.append(tsv)
                if cfg.samp is not None:
                    bso = self.alloc_bank()
                    tts = []
                    for gv in range(2):
                        U = U_ap(slot, gv)
                        tt = self.TR(self.ps[0:32, bso, gv * 128:(gv + 1) * 128], U[:, 128:160], self.ident[:],
                                     deps=evs_all[gv] + (self.bank_free[bso] if gv == 0 else []), signal=True)
                        tts.append(tt)
                        ureaders.append(tt)
                        lastpe = tt
                    te = self.A(so_f[0:32, 0:256], self.ps[0:32, bso, 0:256], AF.Copy, deps=tts + grp_so)
                    self.release_bank(bso, [te])
                slot_free[slot] = ureaders + ctoks[0] + ctoks[1]
                if cfg.samp is not None:
                    for gv in range(2):
                        for r in range(2):
                            self.so_dma(self.o_fconv_s[:, r, gv * DFF + jf * 128: gv * DFF + (jf + 1) * 128],
                                        so_f[r * 16:(r + 1) * 16, gv * 128:(gv + 1) * 128],
                                        deps=[P.cur("act")], stream="ffn")
                    grp_so = [self.so_tok("ffn")]
                if pending_tail is not None:
                    emit_tail(pending_tail)
                pending_tail = (C_ap(slot, 0), ctoks[0], C_ap(slot, 1), ctoks[1], jf, slot)
            self.wrelease(wgi, lastpe)
            self.wrelease(wvi, lastpe)
        emit_tail(pending_tail)
        if cfg.samp is not None:
            self.out_tokens += self.so_all()

    def stage_store_y(self, cfg):
        P = self.P
        xT = self.xT(cfg)
        ost = [self.R2[:, 0:2048], self.R2[:, 2048:4096]]
        for ti, (col0, yrow0, nr) in enumerate(cfg.out_tiles):
            s = ti % 2
            prev = (self.s_o[s], self.o_cnt[s]) if self.o_cnt[s] else None
            evs = []
            for g4 in range(4):
                b = self.alloc_bank()
                tt = None
                for qq in range(4):
                    k = g4 * 4 + qq
                    d = (self.bank_free[b] if qq == 0 else [])
                    tt = self.TR(self.ps[0:nr, b, qq * 128:(qq + 1) * 128], xT[:, k, col0:col0 + nr], self.ident[:],
                                 deps=d, signal=(qq == 3))
                dst = ost[s][0:nr, g4 * 512:(g4 + 1) * 512]
                if g4 % 2 == 0:
                    te = self.A(dst, self.ps[0:nr, b, :], AF.Copy, deps=[tt, prev])
                else:
                    te = self.Vcopy(dst, self.ps[0:nr, b, :], deps=[tt, prev])
                self.release_bank(b, [te])
                evs.append(te)
            self.o_cnt[s] += 16
            src = ost[s]
            dsty = self.y[yrow0:yrow0 + nr, :]
            src = ost[s][0:nr, :]
            t = P.dma("sp", lambda e, dsty=dsty, src=src: e.dma_start(out=dsty, in_=src), self.s_o[s],
                      self.o_cnt[s], deps=evs)
            self.out_tokens.append(t)

    def epilogue(self):
        P = self.P
        self.barrier()
        stg = self.R2[:, 0:12288]
        jobs = [(self.hist_pool, 8, 15, self.o_pool_p), (self.hist_lru, 16, 3, self.o_lconv_p),
                (self.hist_up, 96, 2, self.o_fconv_p)]
        prev = None
        for (src, nch, ncol, dst) in jobs:
            evs = []
            for c0 in range(0, nch, 4):
                b = self.alloc_bank()
                tt = None
                for qq in range(4):
                    d = (self.bank_free[b] if qq == 0 else [])
                    tt = self.TR(self.ps[0:ncol, b, qq * 128:(qq + 1) * 128], src[:, c0 + qq, :], self.ident[:],
                                 deps=d, signal=(qq == 3))
                te = self.A(stg[0:ncol, c0 * 128:(c0 + 4) * 128], self.ps[0:ncol, b, :], AF.Copy, deps=[tt, prev])
                self.release_bank(b, [te])
                evs.append(te)
            t = self.so_dma(dst[:, :], stg[0:ncol, 0:nch * 128], deps=evs, stream="epi")
            prev = self.so_tok("epi")
        evs = []
        for c0 in range(0, 16, 4):
            b = self.alloc_bank()
            tt = None
            for qq in range(4):
                d = (self.bank_free[b] if qq == 0 else [])
                tt = self.TR(self.ps[0:1, b, qq * 128:(qq + 1) * 128], self.h_carry[:, c0 + qq:c0 + qq + 1],
                             self.ident[:], deps=d, signal=(qq == 3))
            te = self.A(stg[0:1, c0 * 128:(c0 + 4) * 128], self.ps[0:1, b, :], AF.Copy, deps=[tt, prev])
            self.release_bank(b, [te])
            evs.append(te)
        self.so_dma(self.o_lh_p[:, :], stg[0:1, 0:2048], deps=evs, stream="epi")
        final = self.so_all() + [(self.s_o[s], self.o_cnt[s]) for s in range(2)]
        P.wait_only("sp", final + self.out_tokens)


_NC_CACHE = {}


def _get_nc():
    if "nc" not in _NC_CACHE:
        b = Builder()
        _NC_CACHE["nc"] = b
    return _NC_CACHE["nc"]


def _pack_vec(v):
    v = np.asarray(v, np.float32).reshape(-1)
    return np.ascontiguousarray(v.reshape(-1, 128).T)


def kernel(x_prompt, x_sample, c_prompt, c_sample, state_pool, state_lru_conv, state_lru_h, state_ffn_conv,
           w_ada, b_ada, g_pre1, g_post1, g_pre2, g_post2, w_in, w_pool_grp, pool_scale,
           w_lru_conv, b_lru_conv, w_rg, b_rg, w_ig, b_ig, lru_lambda,
           w_pool_up, w_lru_up, w_out, w_ffn_up, w_ffn_conv, b_ffn_conv, w_ffn_down):
    f32 = np.float32
    A = lambda a: np.ascontiguousarray(np.asarray(a, f32))
    x_prompt, x_sample = A(x_prompt), A(x_sample)
    c_prompt, c_sample = A(c_prompt), A(c_sample)
    cvec = np.zeros((128, NV), f32)
    cvec[:, CV_GPRE1:CV_GPRE1 + 16] = _pack_vec(g_pre1[0])
    cvec[:, CV_GPOST1:CV_GPOST1 + 16] = _pack_vec(g_post1[0])
    cvec[:, CV_GPRE2:CV_GPRE2 + 16] = _pack_vec(g_pre2[0])
    cvec[:, CV_GPOST2:CV_GPOST2 + 16] = _pack_vec(g_post2[0])
    cvec[:, CV_PSCALE:CV_PSCALE + 8] = _pack_vec(pool_scale[0])
    for k in range(4):
        cvec[:, CV_WLC + 16 * k:CV_WLC + 16 * (k + 1)] = _pack_vec(np.asarray(w_lru_conv)[0, k])
    cvec[:, CV_BLC:CV_BLC + 16] = _pack_vec(b_lru_conv[0])
    cvec[:, CV_BRG:CV_BRG + 16] = _pack_vec(b_rg[0])
    cvec[:, CV_BIG:CV_BIG + 16] = _pack_vec(b_ig[0])
    cvec[:, CV_LAM:CV_LAM + 16] = _pack_vec(lru_lambda[0])
    for k in range(3):
        cvec[:, CV_WFC + 96 * k:CV_WFC + 96 * (k + 1)] = _pack_vec(np.asarray(w_ffn_conv)[0, k])
    cvec[:, CV_BFC:CV_BFC + 96] = _pack_vec(b_ffn_conv[0])
    cvec[:, CV_BADA:CV_BADA + 96] = _pack_vec(b_ada[0])
    ident = np.eye(128, dtype=f32)
    weights = {
        "w_ada": A(w_ada)[0], "w_in": A(w_in)[0], "w_pool_grp": A(w_pool_grp)[0], "w_rg": A(w_rg)[0],
        "w_ig": A(w_ig)[0], "w_pool_up": A(w_pool_up)[0], "w_lru_up": A(w_lru_up)[0], "w_out": A(w_out)[0],
        "w_ffn_up": A(w_ffn_up)[0], "w_ffn_down": A(w_ffn_down)[0],
    }
    state_pool, state_lru_conv = A(state_pool)[0], A(state_lru_conv)[0]
    state_lru_h, state_ffn_conv = A(state_lru_h)[0], A(state_ffn_conv)[0]
    in_maps = []
    for c in range(NCORES):
        b, hf = c // 2, c % 2
        xq = np.zeros((2176, D), f32)
        if hf == 1:
            xq[0:1024] = x_prompt[b, 0:1024]
        xq[1024:2048] = x_prompt[b, hf * 1024:(hf + 1) * 1024]
        xs = x_sample[16 * c:16 * (c + 1)]
        xq[2048:2176] = xs.transpose(1, 0, 2).reshape(128, D)
        cc = np.concatenate([c_prompt[b:b + 1], c_sample[16 * c:16 * (c + 1)]], axis=0)
        cT = np.ascontiguousarray(cc.reshape(17, 16, 128).transpose(2, 1, 0))
        sel = np.full((128, 1), float(hf), f32)
        invc = np.zeros((128, 4, 16), f32)
        for g in range(4):
            w = 2 ** (g + 1)
            for j in range(16):
                cnt = w if hf == 1 else min(w, j + 1)
                invc[:, g, j] = 1.0 / cnt
        m = {"xq": xq, "cT": cT, "cvec": cvec, "sel": sel, "invc": invc, "ident": ident,
             "st_pool": np.ascontiguousarray(state_pool[16 * c:16 * (c + 1)]),
             "st_lconv": np.ascontiguousarray(state_lru_conv[16 * c:16 * (c + 1)]),
             "st_lh": np.ascontiguousarray(state_lru_h[16 * c:16 * (c + 1)]),
             "st_fconv": np.ascontiguousarray(state_ffn_conv[16 * c:16 * (c + 1)])}
        m.update(weights)
        in_maps.append(m)
    nc = build_nc()
    res = run_bass_kernel_spmd(nc, in_maps, core_ids=list(range(NCORES)))
    R = res.results
    y_p = np.zeros((4, 2048, D), f32)
    y_s = np.zeros((128, 8, D), f32)
    pool_p = np.zeros((1, 4, 15, PW), f32)
    lconv_p = np.zeros((1, 4, 3, D), f32)
    lh_p = np.zeros((1, 4, D), f32)
    fconv_p = np.zeros((1, 4, 2, 2 * DFF), f32)
    pool_s = np.zeros((1, 128, 15, PW), f32)
    lconv_s = np.zeros((1, 128, 3, D), f32)
    lh_s = np.zeros((1, 128, D), f32)
    fconv_s = np.zeros((1, 128, 2, 2 * DFF), f32)
    for c in range(NCORES):
        b, hf = c // 2, c % 2
        r = R[c]
        y_p[b, hf * 1024:(hf + 1) * 1024] = r["y"][0:1024]
        y_s[16 * c:16 * (c + 1)] = r["y"][1024:1152].reshape(8, 16, D).transpose(1, 0, 2)
        if hf == 1:
            pool_p[0, b] = r["o_pool_p"]
            lconv_p[0, b] = r["o_lconv_p"]
            lh_p[0, b] = r["o_lh_p"][0]
            fconv_p[0, b] = r["o_fconv_p"]
        pool_s[0, 16 * c:16 * (c + 1)] = r["o_pool_s"]
        lconv_s[0, 16 * c:16 * (c + 1)] = r["o_lconv_s"]
        lh_s[0, 16 * c:16 * (c + 1)] = r["o_lh_s"]
        fconv_s[0, 16 * c:16 * (c + 1)] = r["o_fconv_s"]
    return (y_p, y_s, pool_p, lconv_p, lh_p, fconv_p, pool_s, lconv_s, lh_s, fconv_s)


def build_nc():
    if "built" not in _NC_CACHE:
        b = Builder()
        _NC_CACHE["built"] = b.build()
    return _NC_CACHE["built"]
```
